# Optimizing a Trainium2 kernel written in Bass

```python
import math
import jax, jax.numpy as jnp
from jax import lax
import numpy as np

D_MODEL = 1024
BATCH = 1
SEQ = 16384
DEPTH = 4

N_A = DEPTH // 2
N_B = DEPTH - N_A
Q_BLOCK = 128
RMS_EPS = 1e-6

FOX_HEADS = 16
FOX_HEAD_DIM = D_MODEL // FOX_HEADS
FOX_WIDTH = FOX_HEADS * FOX_HEAD_DIM

MLA_HEADS = 16
MLA_NOPE = 64
MLA_ROPE = 32
MLA_V = 64
KV_RANK = 4 * MLA_V
Q_RANK = 12 * MLA_V
ROPE_THETA = 10000.0

D_FF = ((8 * D_MODEL // 3 + 255) // 256) * 256

kernel_name = "yoco_fox_mla_hybrid"


def rmsnorm(x, g):
    xf = x.astype(jnp.float32)
    inv = lax.rsqrt(jnp.mean(xf * xf, axis=-1, keepdims=True) + RMS_EPS)
    return (xf * inv).astype(x.dtype) * g


def rope(x, positions):
    d = x.shape[-1]
    inv_freq = ROPE_THETA ** (-jnp.arange(0, d // 2, dtype=jnp.float32) * 2.0 / d)
    ang = positions.astype(jnp.float32)[..., None] * inv_freq
    cos = jnp.cos(ang)[:, :, None, :]
    sin = jnp.sin(ang)[:, :, None, :]
    xf = x.astype(jnp.float32)
    x1, x2 = xf[..., : d // 2], xf[..., d // 2:]
    out = jnp.concatenate([x1 * cos - x2 * sin, x1 * sin + x2 * cos], axis=-1)
    return out.astype(x.dtype)


def causal_block_attention(q, k, v, scale, log_f_cum=None):
    B, S, H, dq = q.shape
    nb = S // Q_BLOCK
    qb = q.reshape(B, nb, Q_BLOCK, H, dq).transpose(1, 0, 2, 3, 4)
    k_pos = jnp.arange(S)
    if log_f_cum is not None:
        cq_blocks = log_f_cum.reshape(B, nb, Q_BLOCK, H).transpose(1, 0, 2, 3)
        ck = log_f_cum.transpose(0, 2, 1)
    else:
        cq_blocks, ck = None, None

    def one_block(args):
        i, q_blk, cq = args
        s = jnp.einsum('bqhd,bkhd->bhqk', q_blk, k,
                       preferred_element_type=jnp.float32) * scale
        if cq is not None:
            s = s + (cq.transpose(0, 2, 1)[..., None] - ck[:, :, None, :])
        q_pos = i * Q_BLOCK + jnp.arange(Q_BLOCK)
        mask = k_pos[None, :] <= q_pos[:, None]
        s = jnp.where(mask[None, None], s, -jnp.inf)
        p = jax.nn.softmax(s, axis=-1)
        return jnp.einsum('bhqk,bkhd->bqhd', p.astype(v.dtype), v)

    out = lax.map(one_block, (jnp.arange(nb), qb, cq_blocks))
    return out.transpose(1, 0, 2, 3, 4).reshape(B, S, H, v.shape[-1])


def fox_attention(xn, w_in, b_f, w_o):
    B, S, _ = xn.shape
    proj = xn @ w_in
    q = proj[..., :FOX_WIDTH].reshape(B, S, FOX_HEADS, FOX_HEAD_DIM)
    k = proj[..., FOX_WIDTH:2 * FOX_WIDTH].reshape(B, S, FOX_HEADS, FOX_HEAD_DIM)
    v = proj[..., 2 * FOX_WIDTH:3 * FOX_WIDTH].reshape(B, S, FOX_HEADS, FOX_HEAD_DIM)
    f_logit = proj[..., 3 * FOX_WIDTH:] + b_f
    log_f = jax.nn.log_sigmoid(f_logit.astype(jnp.float32))
    cum = jnp.cumsum(log_f, axis=1)
    o = causal_block_attention(q, k, v, 1.0 / math.sqrt(FOX_HEAD_DIM), cum)
    return o.reshape(B, S, FOX_WIDTH) @ w_o


def mla_shared_kv(h, positions, kv_norm, w_kv_a, ckv_norm, w_uk, w_uv):
    B, S, _ = h.shape
    hn = rmsnorm(h, kv_norm)
    a = hn @ w_kv_a
    c_kv = rmsnorm(a[..., :KV_RANK], ckv_norm)
    k_rope = rope(a[..., KV_RANK:][:, :, None, :], positions)
    k_nope = jnp.einsum('bsr,rhd->bshd', c_kv, w_uk)
    v = jnp.einsum('bsr,rhd->bshd', c_kv, w_uv)
    k = jnp.concatenate(
        [k_nope, jnp.broadcast_to(k_rope, (B, S, MLA_HEADS, MLA_ROPE))], axis=-1)
    return k, v


def mla_attention(xn, positions, k, v, w_dq, cq_norm, w_uq, w_o):
    B, S, _ = xn.shape
    c_q = rmsnorm(xn @ w_dq, cq_norm)
    q = (c_q @ w_uq).reshape(B, S, MLA_HEADS, MLA_NOPE + MLA_ROPE)
    q = jnp.concatenate([q[..., :MLA_NOPE], rope(q[..., MLA_NOPE:], positions)], axis=-1)
    o = causal_block_attention(q, k, v, 1.0 / math.sqrt(MLA_NOPE + MLA_ROPE))
    return o.reshape(B, S, MLA_HEADS * MLA_V) @ w_o


def swiglu(xn, w_gate, w_up, w_down):
    return (jax.nn.silu(xn @ w_gate) * (xn @ w_up)) @ w_down


def setup_inputs(seed: int = 0) -> dict:
    key = jax.random.key(seed)
    ks = jax.random.split(key, 24)

    def nrm(k, shape, fan_in):
        return jax.random.normal(k, shape, jnp.float32) * (fan_in ** -0.5)

    def gain(k, shape):
        return 1.0 + 0.02 * jax.random.normal(k, shape, jnp.float32)

    x = jax.random.normal(ks[0], (BATCH, SEQ, D_MODEL), jnp.float32)
    positions = jnp.broadcast_to(jnp.arange(SEQ, dtype=jnp.int32), (BATCH, SEQ))
    return {
        "x": x,
        "positions": positions,
        "attn_norm": gain(ks[1], (DEPTH, D_MODEL)),
        "ffn_norm": gain(ks[2], (DEPTH, D_MODEL)),
        "w_gate": nrm(ks[3], (DEPTH, D_MODEL, D_FF), D_MODEL),
        "w_up": nrm(ks[4], (DEPTH, D_MODEL, D_FF), D_MODEL),
        "w_down": nrm(ks[5], (DEPTH, D_FF, D_MODEL), D_FF),
        "fox_w_in": nrm(ks[6], (N_A, D_MODEL, 3 * FOX_WIDTH + FOX_HEADS), D_MODEL),
        "fox_b_f": 3.0 + 0.5 * jax.random.normal(ks[7], (N_A, FOX_HEADS), jnp.float32),
        "fox_w_o": nrm(ks[8], (N_A, FOX_WIDTH, D_MODEL), FOX_WIDTH),
        "kv_norm": gain(ks[9], (D_MODEL,)),
        "w_kv_a": nrm(ks[10], (D_MODEL, KV_RANK + MLA_ROPE), D_MODEL),
        "ckv_norm": gain(ks[11], (KV_RANK,)),
        "w_uk": nrm(ks[12], (KV_RANK, MLA_HEADS, MLA_NOPE), KV_RANK),
        "w_uv": nrm(ks[13], (KV_RANK, MLA_HEADS, MLA_V), KV_RANK),
        "mla_w_dq": nrm(ks[14], (N_B, D_MODEL, Q_RANK), D_MODEL),
        "cq_norm": gain(ks[15], (N_B, Q_RANK)),
        "mla_w_uq": nrm(ks[16], (N_B, Q_RANK, MLA_HEADS * (MLA_NOPE + MLA_ROPE)), Q_RANK),
        "mla_w_o": nrm(ks[17], (N_B, MLA_HEADS * MLA_V, D_MODEL), MLA_HEADS * MLA_V),
        "final_norm": gain(ks[18], (D_MODEL,)),
    }


def reference(x, positions, attn_norm, ffn_norm, w_gate, w_up, w_down,
              fox_w_in, fox_b_f, fox_w_o, kv_norm, w_kv_a, ckv_norm, w_uk, w_uv,
              mla_w_dq, cq_norm, mla_w_uq, mla_w_o, final_norm):
    h = x
    k_shared, v_shared = None, None
    for l in range(DEPTH):
        if l == N_A:
            k_shared, v_shared = mla_shared_kv(h, positions, kv_norm, w_kv_a,
                                               ckv_norm, w_uk, w_uv)
        xn = rmsnorm(h, attn_norm[l])
        if l < N_A:
            h = h + fox_attention(xn, fox_w_in[l], fox_b_f[l], fox_w_o[l])
        else:
            j = l - N_A
            h = h + mla_attention(xn, positions, k_shared, v_shared,
                                  mla_w_dq[j], cq_norm[j], mla_w_uq[j], mla_w_o[j])
        h = h + swiglu(rmsnorm(h, ffn_norm[l]), w_gate[l], w_up[l], w_down[l])
    return rmsnorm(h, final_norm)
```

```python
import math
from contextlib import ExitStack
import numpy as np
import concourse.bass as bass
import concourse.mybir as mybir
from concourse.bass_utils import run_bass_kernel_spmd

F32 = mybir.dt.float32
BF16 = mybir.dt.bfloat16
I32 = mybir.dt.int32
AF = mybir.ActivationFunctionType
ALU = mybir.AluOpType

NCORES = 8
D = 1024
S = 16384
NT = 2048
DFF = 2816
NFC = 22
EPS = 1e-6
ROLL = 30000
NEG = -30000.0
DEBUG_STOP = 1000
DEBUG_DUMP = False

PC_ATTN = 0
PC_FFN = 32
PC_KV = 64
PC_FIN = 72
PC_CQ = 80
PC_CKV = 92
PC_INVF = 94
NPAR = 96

ENGS = ("pe", "act", "dve", "pool", "sp")


class Tok:
    __slots__ = ("kind", "eng", "sig", "sem", "val", "seq")

    def __init__(self, kind, eng=None, sem=None, val=None, seq=0):
        self.kind = kind
        self.eng = eng
        self.seq = seq
        self.sig = False
        self.sem = sem
        self.val = val


class Prog:
    def __init__(self, nc, es):
        self.nc = nc
        self.es = es
        self.q = {e: [] for e in ENGS}
        self.res = {}
        self.dsem = {}
        self.dcnt = {}
        self.esems = {e: [] for e in ENGS}

    def sem(self, name):
        return self.es.enter_context(self.nc.semaphore(name))

    def _r(self, n):
        r = self.res.get(n)
        if r is None:
            r = self.res[n] = {"w": [], "r": []}
        return r

    def _deps(self, e, reads, writes, adds):
        raw = []
        for n in reads:
            raw += self._r(n)["w"]
        oth = []
        for n in writes:
            r = self._r(n)
            oth += r["w"] + r["r"]
        for n in adds:
            oth += self._r(n)["r"]
        out = []
        seen = set()
        best = {}
        for (lst, is_raw) in ((raw, True), (oth, False)):
            for d in lst:
                if id(d) in seen:
                    continue
                seen.add(id(d))
                if d.kind == "eng":
                    if d.eng == e and (e == "pe" or not is_raw):
                        continue
                    b = best.get(d.eng)
                    if b is None or b.seq < d.seq:
                        best[d.eng] = d
                else:
                    out.append(d)
        for d in best.values():
            d.sig = True
            out.append(d)
        return out

    def _upd(self, tok, reads, writes, adds):
        for n in reads:
            self._r(n)["r"].append(tok)
        for n in writes:
            r = self._r(n)
            r["w"] = [tok]
            r["r"] = []
        for n in adds:
            self._r(n)["w"].append(tok)

    def op(self, e, fn, reads=(), writes=(), adds=()):
        deps = self._deps(e, reads, writes, adds)
        tok = Tok("eng", eng=e, seq=len(self.q[e]))
        self.q[e].append((fn, deps, tok, 1))
        self._upd(tok, reads, writes, adds)
        return tok

    def dma(self, e, out, in_, key, reads=(), writes=(), adds=()):
        deps = self._deps(e, reads, writes, adds)
        if key not in self.dsem:
            self.dsem[key] = self.sem("d_" + key)
            self.dcnt[key] = 0
        self.dcnt[key] += 16
        tok = Tok("dma", sem=self.dsem[key], val=self.dcnt[key])
        fn = lambda eng, o=out, i=in_: eng.dma_start(out=o, in_=i)
        self.q[e].append((fn, deps, tok, 16))
        self._upd(tok, reads, writes, adds)
        return tok

    def cc(self, ins, outs, key, reads=(), writes=()):
        writes = list(writes) + ["__CC__"]
        deps = self._deps("pool", reads, writes, ())
        if key not in self.dsem:
            self.dsem[key] = self.sem("c_" + key)
            self.dcnt[key] = 0
        self.dcnt[key] += 1
        tok = Tok("dma", sem=self.dsem[key], val=self.dcnt[key])
        fn = lambda eng, i=ins, o=outs: eng.collective_compute(
            "AllGather", ALU.bypass, replica_groups=[list(range(NCORES))], ins=[i], outs=[o])
        self.q["pool"].append((fn, deps, tok, 1))
        self._upd(tok, reads, writes, ())
        return tok

    def alias(self, src, dst):
        toks = []
        for n in src:
            r = self._r(n)
            toks += r["w"] + r["r"]
        for n in dst:
            self._r(n)["r"] += toks

    def wait_all(self, e, names):
        deps = self._deps(e, names, (), ())
        self.q[e].append((None, deps, None, 0))

    def finalize(self):
        for e in ENGS:
            n = 0
            for (fn, deps, tok, inc) in self.q[e]:
                if tok is not None and tok.kind == "eng" and tok.sig:
                    k = n // ROLL
                    while len(self.esems[e]) <= k:
                        self.esems[e].append(self.sem("e_%s%d" % (e, len(self.esems[e]))))
                    tok.sem = self.esems[e][k]
                    tok.val = n % ROLL + 1
                    n += 1

    def emit(self, e, eng):
        waited = {}
        for (fn, deps, tok, inc) in self.q[e]:
            need = {}
            for d in deps:
                k = id(d.sem)
                if waited.get(k, 0) >= d.val:
                    continue
                if k not in need or need[k][1] < d.val:
                    need[k] = (d.sem, d.val)
            for k, (s, v) in need.items():
                eng.wait_ge(s, v)
                waited[k] = v
            if fn is None:
                continue
            ins = fn(eng)
            if tok.kind == "dma":
                ins.then_inc(tok.sem, inc)
            elif tok.sig:
                ins.then_inc(tok.sem, 1)


def build_program():
    nc = bass.Bass("TRN2", target_bir_lowering=False)
    es = ExitStack()
    p = Prog(nc, es)

    def din(name, shape, dt=F32):
        return nc.dram_tensor(name, list(shape), dt, kind="ExternalInput").ap()

    def dscr(name, shape, dt):
        return nc.dram_tensor(name, list(shape), dt).ap()

    def sb(name, shape, dt):
        return es.enter_context(nc.sbuf_tensor(name, list(shape), dt))

    xT_d = din("xT", [128, 8 * NT])
    pos_d = din("pos", [1, NT], I32)
    par_d = din("par", [128, NPAR])
    bfb_d = din("bfb", [128, 2 * 256])
    msk_d = din("msk", [128, 32 * 128])
    cst_d = din("cst", [128, 3 * 128 + 16])
    w_in_d = [din("w_in%d" % l, [D, 3088]) for l in range(2)]
    fwo_d = [din("fwo%d" % l, [D, D]) for l in range(2)]
    wkva_d = din("wkva", [D, 288])
    wuk_d = din("wuk", [256, D])
    wuv_d = din("wuv", [256, D])
    wdq_d = [din("wdq%d" % j, [D, 768]) for j in range(2)]
    wuq_d = [din("wuq%d" % j, [768, 1536]) for j in range(2)]
    mwo_d = [din("mwo%d" % j, [D, D]) for j in range(2)]
    wg_d = [din("wg%d" % l, [D, DFF]) for l in range(4)]
    wu_d = [din("wu%d" % l, [D, DFF]) for l in range(4)]
    wd_d = [din("wd%d" % l, [DFF, D]) for l in range(4)]
    outT_d = nc.dram_tensor("outT", [128, 8 * NT], F32, kind="ExternalOutput").ap()

    QM = dscr("QM", [D, NT], BF16)
    QA = dscr("QA", [16 * 4, NT], BF16)
    QR = dscr("QR", [2 * 256, NT], BF16)
    KTb = dscr("KTb", [D, NT], BF16)
    KTg = dscr("KTg", [NCORES * D, NT], BF16)
    KXb = dscr("KXb", [32, NT], BF16)
    KXg = dscr("KXg", [NCORES * 32, NT], BF16)
    Vb = dscr("Vb", [16 * 128, 16 * 64], BF16)
    Vg = dscr("Vg", [NCORES * 16 * 128, 16 * 64], BF16)
    LFb = dscr("LFb", [NT, 16], F32)
    LFg = dscr("LFg", [S, 16], F32)
    CKd = dscr("CKd", [16 * 4, S], BF16)
    COSd = dscr("COSd", [128, NT], F32)
    SINd = dscr("SINd", [128, NT], F32)

    hT = sb("hT", [128, 8 * NT], F32)
    A = sb("A", [128, 8 * NT], BF16)
    B = sb("B", [128, 8 * NT], BF16)
    Wt = [sb("W%d" % i, [128, 4096], BF16) for i in range(3)]
    Vt = [sb("V%d" % i, [128, 16 * 128], BF16) for i in range(2)]
    MSK = sb("MSK", [128, 32 * 128], BF16)
    CST = sb("CST", [128, 3 * 128 + 16], BF16)
    PAR = sb("PAR", [128, NPAR], F32)
    BFB = sb("BFB", [128, 512], F32)
    ONES = sb("ONES", [128, 128], BF16)
    LFown = sb("LFown", [128, 256], F32)
    T32 = [sb("T32_%d" % i, [128, 512], F32) for i in range(5)]
    TB16 = [sb("TB16_%d" % i, [128, 512], BF16) for i in range(2)]
    STG = [B[:, 12288 + i * NT: 12288 + (i + 1) * NT] for i in range(2)]
    CS = [sb("CS%d" % i, [128, 512], F32) for i in range(2)]
    PS = [es.enter_context(nc.psum_tensor("ps%d" % i, [128, 512], F32)) for i in range(8)]

    IDN = CST[:, 0:128]
    UTR = CST[:, 128:256]
    MLT = CST[:, 256:384]
    MOWN = CST[:, 384:400]

    Kt = [A[:, i * NT:(i + 1) * NT] for i in range(2)]
    Qt = [A[:, (2 + i) * NT:(3 + i) * NT] for i in range(2)]
    Pt = [A[:, 4 * NT + i * 512: 4 * NT + (i + 1) * 512] for i in range(3)]
    ATT_RES = ["K0m", "K0x", "K1m", "K1x", "Q0", "Q1", "P0", "P1", "P2"]
    A_RES = ["A.%d" % t for t in range(4)]
    B_RES = ["B.%d" % t for t in range(4)]

    B32 = B[:, :].bitcast(F32)

    psi = [0]

    def next_ps():
        i = psi[0] % 8
        psi[0] += 1
        return "ps%d" % i, PS[i]

    wi = [0]

    def next_w():
        i = wi[0] % 3
        wi[0] += 1
        return "W%d" % i, Wt[i]

    def load_w(src_ap, rows_k, ncols, c0=0):
        name, t = next_w()
        view = t[:, 0:rows_k * ncols].rearrange("p (k c) -> p k c", k=rows_k)
        src = src_ap[:, c0:c0 + ncols].rearrange("(k p) c -> p k c", p=128)
        p.dma("pool", view, src, name, writes=[name])
        return name, t

    p.dma("sp", PAR[:, :], par_d, "i_par", writes=["PAR"])
    p.dma("sp", BFB[:, :], bfb_d, "i_bfb", writes=["BFB"])
    p.dma("pool", MSK[:, :], msk_d, "i_msk", writes=["MSK"])
    p.dma("pool", CST[:, :], cst_d, "i_cst", writes=["CST"])
    for kc in range(8):
        p.dma("sp", hT[:, kc * NT:(kc + 1) * NT], xT_d[:, kc * NT:(kc + 1) * NT], "i_h",
              adds=["hT.%d" % t for t in range(4)])
    p.op("pool", lambda e: e.memset(ONES[:, :], 1.0), writes=["ONES"])
    for i in range(2):
        p.op("pool", lambda e, i=i: e.memset(Vt[i][:, :], 1.0), writes=["V%d" % i])
    p.op("pool", lambda e: e.memset(A[0:16, 0:NT], 1.0), writes=["A.0"])
    CKv = CKd.rearrange("(h r) s -> h r s", r=4)
    QAv = QA.rearrange("(h r) s -> h r s", r=4)
    for q8 in range(8):
        p.dma("sp", CKv[:, 0, q8 * NT:(q8 + 1) * NT], A[0:16, 0:NT], "initw", reads=["A.0"], adds=["CKd"])
    for r in range(1, 4):
        p.dma("sp", QAv[:, r, :], A[0:16, 0:NT], "initw", reads=["A.0"], adds=["QA"])

    POSI = B[:, 0:2 * NT].bitcast(I32)
    ANG = B32[:, NT:2 * NT]
    ARG = B32[:, 2 * NT:3 * NT]
    TAB = B32[:, 3 * NT:4 * NT]
    p.dma("sp", POSI, pos_d.partition_broadcast(128), "i_pos", writes=["B.0"])
    p.op("dve", lambda e: e.tensor_copy(out=ANG, in_=POSI), reads=["B.0"], writes=["B.1"])
    p.op("dve", lambda e: e.tensor_scalar(out=ANG, in0=ANG, scalar1=PAR[:, PC_INVF:PC_INVF + 1], scalar2=None,
                                          op0=ALU.mult), reads=["PAR", "B.1"], writes=["B.1"])
    RR = B32[:, 0:NT]
    MAGIC = 12582912.0
    C1 = 6.28125
    C2 = 2.0 * math.pi - 6.28125
    PI_LO = 3.1415925
    for (dst, nm) in ((SINd, "SINd"), (COSd, "COSd")):
        if nm == "COSd":
            p.op("dve", lambda e: e.tensor_scalar(out=ANG, in0=ANG, scalar1=0.5 * math.pi, scalar2=None, op0=ALU.add),
                 reads=["B.1"], writes=["B.1"])
        p.op("dve", lambda e: e.tensor_scalar(out=ARG, in0=ANG, scalar1=1.0 / (2.0 * math.pi), scalar2=MAGIC,
                                              op0=ALU.mult, op1=ALU.add), reads=["B.1"], writes=["B.2"])
        p.op("dve", lambda e: e.tensor_scalar(out=ARG, in0=ARG, scalar1=-MAGIC, scalar2=None, op0=ALU.add),
             reads=["B.2"], writes=["B.2"])
        p.op("dve", lambda e: e.scalar_tensor_tensor(out=RR, in0=ARG, scalar=-C1, in1=ANG, op0=ALU.mult, op1=ALU.add),
             reads=["B.2", "B.1"], writes=["B.0"])
        p.op("dve", lambda e: e.scalar_tensor_tensor(out=RR, in0=ARG, scalar=-C2, in1=RR, op0=ALU.mult, op1=ALU.add),
             reads=["B.2", "B.0"], writes=["B.0"])
        p.op("dve", lambda e: e.tensor_scalar(out=RR, in0=RR, scalar1=-PI_LO, scalar2=PI_LO, op0=ALU.max, op1=ALU.min),
             reads=["B.0"], writes=["B.0"])
        p.op("act", lambda e: e.activation(out=TAB, in_=RR, func=AF.Sin), reads=["B.0"], writes=["B.3"])
        p.dma("sp", dst, TAB, "i_tab", reads=["B.3"], writes=[nm])

    def mm(out_ap, psname, pairs, reads, first=True, last=True):
        n = len(pairs)
        for i, (l, r) in enumerate(pairs):
            st = first and i == 0
            sp_ = last and i == n - 1
            if st:
                p.op("pe", lambda e, l=l, r=r, st=st, sp_=sp_: e.matmul(out_ap, lhsT=l, rhs=r, start=st, stop=sp_),
                     reads=reads, writes=[psname])
            else:
                p.op("pe", lambda e, l=l, r=r, st=st, sp_=sp_: e.matmul(out_ap, lhsT=l, rhs=r, start=st, stop=sp_),
                     reads=reads, adds=[psname])

    def rms_rstd(chunks, chunk_res, N, qs, tg_tag):
        psn, ps = next_ps()
        n = len(chunks)
        for i, c in enumerate(chunks):
            sq = TB16[i % 2]
            sqn = "TB16_%d" % (i % 2)
            p.op("act", lambda e, c=c, sq=sq: e.activation(out=sq[:, :], in_=c, func=AF.Square),
                 reads=chunk_res, writes=[sqn])
            st = (i == 0)
            sp_ = (i == n - 1)
            if st:
                p.op("pe", lambda e, sq=sq, st=st, sp_=sp_: e.matmul(ps[:, :], lhsT=ONES[:, :], rhs=sq[:, :], start=st, stop=sp_),
                     reads=[sqn, "ONES"], writes=[psn])
            else:
                p.op("pe", lambda e, sq=sq, st=st, sp_=sp_: e.matmul(ps[:, :], lhsT=ONES[:, :], rhs=sq[:, :], start=st, stop=sp_),
                     reads=[sqn, "ONES"], adds=[psn])
        rs = T32[4]
        p.op("act", lambda e: e.activation(out=rs[:, :], in_=ps[:, :], func=AF.Sqrt, scale=1.0 / (N * qs * qs),
                                           bias=EPS / (qs * qs)), reads=[psn], writes=["T32_4"])
        p.op("dve", lambda e: e.reciprocal(out=rs[:, :], in_=rs[:, :]), reads=["T32_4"], writes=["T32_4"])
        return rs, "T32_4"

    def norm_to_A(gcol):
        for tg in range(4):
            chunks = [hT[:, kc * NT + tg * 512: kc * NT + (tg + 1) * 512] for kc in range(8)]
            rs, rsn = rms_rstd(chunks, ["hT.%d" % tg], float(D), 1.0, tg)
            for kc in range(8):
                p.op("dve", lambda e, kc=kc, tg=tg, c=chunks[kc]: e.scalar_tensor_tensor(
                    out=A[:, kc * NT + tg * 512: kc * NT + (tg + 1) * 512], in0=c,
                    scalar=PAR[:, gcol + kc:gcol + kc + 1], in1=rs[:, :], op0=ALU.mult, op1=ALU.mult),
                    reads=["hT.%d" % tg, rsn, "PAR"], **({"writes": ["A.%d" % tg]} if kc == 0 else {"adds": ["A.%d" % tg]}))

    def xn(kc, t0, n):
        return A[:, kc * NT + t0: kc * NT + t0 + n]

    def proj_fm(w_src, ncol_total, c0, nchunks, dst_fn, scale=None, kin=8, rhs_fn=None, rhs_res=None):
        done = 0
        while done < nchunks:
            nb = min(4, nchunks - done)
            wn, wt = load_w(w_src, kin, nb * 128, c0 + done * 128)
            for j in range(nb):
                for tg in range(4):
                    psn, ps = next_ps()
                    pairs = []
                    for kc in range(kin):
                        l = wt[:, kc * nb * 128 + j * 128: kc * nb * 128 + (j + 1) * 128]
                        r = rhs_fn(kc, tg) if rhs_fn else xn(kc, tg * 512, 512)
                        pairs.append((l, r))
                    mm(ps[:, :], psn, pairs, [wn] + (rhs_res(tg) if rhs_res else ["A.%d" % tg]))
                    dst_fn(done + j, tg, ps, psn)
            done += nb

    def evac_stage_store(dst_dram_rows, scale=None):
        def fn(ci, tg, ps, psn, dst=dst_dram_rows):
            s = STG[ci % 2]
            sn = "STG%d" % (ci % 2)
            kw = {"writes": [sn]} if tg == 0 else {"adds": [sn]}
            if scale is None:
                p.op("dve", lambda e: e.tensor_copy(out=s[:, tg * 512:(tg + 1) * 512], in_=ps[:, :]), reads=[psn], **kw)
            else:
                p.op("dve", lambda e: e.tensor_scalar(out=s[:, tg * 512:(tg + 1) * 512], in0=ps[:, :], scalar1=scale,
                                                      scalar2=None, op0=ALU.mult), reads=[psn], **kw)
            if tg == 3:
                d_ap, d_res = dst(ci)
                p.dma("sp", d_ap, s[:, :], "st_%s_%d" % (d_res, ci % 2), reads=[sn], adds=[d_res])
        return fn

    def v_proj(lhs_fn, lhs_res, w_src, c0, kin):
        Vb4 = Vb.rearrange("(h p) (t d) -> p h t d", p=128, d=64)
        for half in range(2):
            wn, wt = load_w(w_src, kin, 512, c0 + half * 512)
            for tb in range(16):
                psn, ps = next_ps()
                pairs = [(lhs_fn(kc, tb), wt[:, kc * 512:(kc + 1) * 512]) for kc in range(kin)]
                mm(ps[:, :], psn, pairs, [wn] + lhs_res(tb))
                s = TB16[tb % 2]
                sn = "TB16_%d" % (tb % 2)
                p.op("dve", lambda e, s=s, ps=ps: e.tensor_copy(out=s[:, :], in_=ps[:, :]), reads=[psn], writes=[sn])
                p.dma("sp", Vb4[:, half * 8:(half + 1) * 8, tb, :], s[:, :].rearrange("p (h d) -> p h d", d=64),
                      "st_Vb_%d" % (tb % 2), reads=[sn], adds=["Vb"])

    def split3(src, srcres, hi, mid, lo, tmp, tmpres, outres, first):
        kw = (lambda: {"writes": [outres]}) if first else (lambda: {"adds": [outres]})
        p.op("dve", lambda e: e.tensor_copy(out=hi, in_=src), reads=srcres, **kw())
        p.op("dve", lambda e: e.tensor_tensor(out=tmp, in0=src, in1=hi, op=ALU.subtract), reads=srcres + [outres], writes=[tmpres])
        p.op("dve", lambda e: e.tensor_copy(out=mid, in_=tmp), reads=[tmpres], adds=[outres])
        p.op("dve", lambda e: e.tensor_tensor(out=tmp, in0=tmp, in1=mid, op=ALU.subtract), reads=[tmpres, outres], writes=[tmpres])
        p.op("dve", lambda e: e.tensor_copy(out=lo, in_=tmp), reads=[tmpres], adds=[outres])

    def rope_apply(x1ps, x1n, x2ps, x2n, np_, tg, o_tile, o_res):
        cosn, sinn = "CS0", "CS1"
        t = [T32[i][0:np_, :] for i in range(4)]
        tn = ["T32_%d" % i for i in range(4)]
        p.op("dve", lambda e: e.tensor_tensor(out=t[0], in0=x1ps[0:np_, :], in1=CS[0][0:np_, :], op=ALU.mult), reads=[x1n, cosn], writes=[tn[0]])
        p.op("dve", lambda e: e.tensor_tensor(out=t[1], in0=x2ps[0:np_, :], in1=CS[1][0:np_, :], op=ALU.mult), reads=[x2n, sinn], writes=[tn[1]])
        p.op("dve", lambda e: e.tensor_tensor(out=t[2], in0=x1ps[0:np_, :], in1=CS[1][0:np_, :], op=ALU.mult), reads=[x1n, sinn], writes=[tn[2]])
        p.op("dve", lambda e: e.tensor_tensor(out=t[3], in0=x2ps[0:np_, :], in1=CS[0][0:np_, :], op=ALU.mult), reads=[x2n, cosn], writes=[tn[3]])
        p.op("dve", lambda e: e.tensor_tensor(out=o_tile[0:np_, 0:512], in0=t[0], in1=t[1], op=ALU.subtract), reads=[tn[0], tn[1]], writes=[o_res])
        p.op("dve", lambda e: e.tensor_tensor(out=o_tile[0:np_, 512:1024], in0=t[2], in1=t[3], op=ALU.add), reads=[tn[2], tn[3]], adds=[o_res])

    def load_cs(tg):
        p.dma("sp", CS[0][:, :], COSd[:, tg * 512:(tg + 1) * 512], "CS0", reads=["COSd"], writes=["CS0"])
        p.dma("sp", CS[1][:, :], SINd[:, tg * 512:(tg + 1) * 512], "CS1", reads=["SINd"], writes=["CS1"])

    def attention(l, fox):
        kx = 4 if fox else 32
        nrow = 64 + kx
        p.alias(A_RES, ATT_RES)
        ACC = [(3 + g, "ps%d" % (3 + g)) for g in range(4)]
        tiles = []
        kvi = 0
        for h in range(16):
            for r in range(8):
                sl = kvi % 2
                kvi += 1
                for g in range(4):
                    for lb in range(4 * g + 4):
                        m = lb // 2
                        kp = lb % 2
                        if m < 2 * g:
                            c0, c1, msk = 0, 512, []
                        elif m == 2 * g:
                            c0, c1, msk = 0, 512, [(0, 0), (1, 128)]
                        else:
                            c0, c1, msk = 256, 512, [(0, 256), (1, 384)]
                        msk = [(qp, cc) for (qp, cc) in msk if not (r == 0 and kp == 0 and qp == 1)]
                        tiles.append(dict(h=h, r=r, sl=sl, g=g, lb=lb, kp=kp, c0=c0, c1=c1, msk=msk,
                                          first=(r == 0 and lb == 0), last=(r == 7 and lb == 4 * g + 3),
                                          newq=(r == 0 and g == 0 and lb == 0), newkv=(g == 0 and lb == 0),
                                          endhead=(r == 7 and g == 3 and lb == 15)))
        for i, t in enumerate(tiles):
            t["si"] = i % 3

        def emit_loads(t):
            h, r, sl = t["h"], t["r"], t["sl"]
            if t["newq"]:
                qn = "Q%d" % (h % 2)
                qt = Qt[h % 2]
                p.dma("sp", qt[0:64, :], QM[h * 64:(h + 1) * 64, :], qn, reads=["QM"], writes=[qn])
                if fox:
                    p.dma("sp", qt[64:68, :], QA[h * 4:(h + 1) * 4, :], qn, reads=["QA"], adds=[qn])
                else:
                    p.dma("sp", qt[64:80, :], QR[h * 16:(h + 1) * 16, :], qn, reads=["QR"], adds=[qn])
                    p.dma("sp", qt[80:96, :], QR[256 + h * 16:256 + (h + 1) * 16, :], qn, reads=["QR"], adds=[qn])
            if t["newkv"]:
                kt = Kt[sl]
                vt = Vt[sl]
                kmn, kxn, vn = "K%dm" % sl, "K%dx" % sl, "V%d" % sl
                p.dma("sp", kt[0:64, :], KTg[r * D + h * 64: r * D + (h + 1) * 64, :], kmn, reads=["KTg"], writes=[kmn])
                if fox:
                    p.dma("sp", kt[64:68, :], CKd[h * 4:(h + 1) * 4, r * NT:(r + 1) * NT], kxn, reads=["CKd"], writes=[kxn])
                else:
                    p.dma("sp", kt[64:96, :], KXg[r * 32:(r + 1) * 32, :], kxn, reads=["KXg"], writes=[kxn])
                p.dma("sp", vt[:, :].rearrange("p (l c) -> p l c", l=16)[:, :, 0:64],
                      Vg[r * 2048 + h * 128: r * 2048 + (h + 1) * 128, :].rearrange("p (l c) -> p l c", l=16),
                      vn, reads=["Vg"], writes=[vn])

        def emit_qk(t):
            emit_loads(t)
            h, r, sl, g, lb, kp, c0, c1, msk, si = (t[k] for k in ("h", "r", "sl", "g", "lb", "kp", "c0", "c1", "msk", "si"))
            kt = Kt[sl]
            qt = Qt[h % 2]
            kmn, kxn, qn = "K%dm" % sl, "K%dx" % sl, "Q%d" % (h % 2)
            sps = PS[si]
            spn = "ps%d" % si
            lq = kt[0:nrow, lb * 128:(lb + 1) * 128]
            rq = qt[0:nrow, g * 512 + c0: g * 512 + c1]
            nm = len(msk)
            p.op("pe", lambda e, sps=sps, lq=lq, rq=rq, c0=c0, c1=c1, nm=nm: e.matmul(
                sps[:, c0:c1], lhsT=lq, rhs=rq, start=True, stop=(nm == 0), skip_group_check=True),
                reads=[kmn, kxn, qn], writes=[spn])
            for mi, (qp, cc) in enumerate(msk):
                mo = ((qp * 8 + r) * 2 + kp) * 128
                p.op("pe", lambda e, sps=sps, cc=cc, mo=mo, mi=mi, nm=nm: e.matmul(
                    sps[:, cc:cc + 128], lhsT=IDN, rhs=MSK[:, mo:mo + 128], start=False, stop=(mi == nm - 1),
                    skip_group_check=True), reads=["CST", "MSK"], adds=[spn])

        def emit_exp(t):
            si, c0, c1 = t["si"], t["c0"], t["c1"]
            sps, pt = PS[si], Pt[si]
            p.op("act", lambda e, pt=pt, sps=sps, c0=c0, c1=c1: e.activation(
                out=pt[:, c0:c1], in_=sps[:, c0:c1], func=AF.Exp), reads=["ps%d" % si], writes=["P%d" % si])

        def emit_pv(t):
            h, sl, g, lb, c0, c1, si = (t[k] for k in ("h", "sl", "g", "lb", "c0", "c1", "si"))
            vt, pt = Vt[sl], Pt[si]
            accps, accn = PS[ACC[g][0]], ACC[g][1]
            first, last = t["first"], t["last"]
            kw = {"writes": [accn]} if first else {"adds": [accn]}
            p.op("pe", lambda e, accps=accps, vt=vt, lb=lb, pt=pt, c0=c0, c1=c1, first=first, last=last: e.matmul(
                accps[:, c0:c1], lhsT=vt[:, lb * 128:(lb + 1) * 128], rhs=pt[:, c0:c1], start=first, stop=last,
                skip_group_check=True), reads=["V%d" % sl, "P%d" % si], **kw)

        def emit_norm(h):
            for g in range(4):
                accps = PS[ACC[g][0]]
                accn = ACC[g][1]
                rc = T32[g % 2]
                rcn = "T32_%d" % (g % 2)
                p.op("dve", lambda e, rc=rc, accps=accps: e.reciprocal(out=rc[0:64, :], in_=accps[64:128, :]),
                     reads=[accn], writes=[rcn])
                po = (h % 2) * 64
                o_ap = B[po:po + 64, (h // 2) * NT + g * 512:(h // 2) * NT + (g + 1) * 512]
                p.op("dve", lambda e, rc=rc, accps=accps, o_ap=o_ap: e.tensor_tensor(out=o_ap, in0=accps[0:64, :], in1=rc[0:64, :],
                                                                                     op=ALU.mult),
                     reads=[accn, rcn], adds=["B.%d" % g])

        n = len(tiles)
        emit_qk(tiles[0])
        for i, t in enumerate(tiles):
            if i + 1 < n:
                emit_qk(tiles[i + 1])
            emit_exp(t)
            emit_pv(t)
            if t["endhead"]:
                emit_norm(t["h"])
        p.alias(ATT_RES, A_RES)

    def wo_proj(w_src):
        for half in range(2):
            wn, wt = load_w(w_src, 8, 512, half * 512)
            for j in range(4):
                dmc = half * 4 + j
                for tg in range(4):
                    psn, ps = next_ps()
                    pairs = [(wt[:, kc * 512 + j * 128: kc * 512 + (j + 1) * 128],
                              B[:, kc * NT + tg * 512: kc * NT + (tg + 1) * 512]) for kc in range(8)]
                    mm(ps[:, :], psn, pairs, [wn, "B.%d" % tg])
                    hs = hT[:, dmc * NT + tg * 512: dmc * NT + (tg + 1) * 512]
                    p.op("dve", lambda e, hs=hs, ps=ps: e.tensor_tensor(out=hs, in0=hs, in1=ps[:, :], op=ALU.add),
                         reads=[psn], adds=["hT.%d" % tg])

    def ffn(l):
        norm_to_A(PC_FFN + l * 8)
        ACTT = B
        p.alias(B_RES, ["ACTT"])
        for tg in range(4):
            for k in range(6):
                nb = 4 if k < 5 else 2
                gn, gt = load_w(wg_d[l], 8, nb * 128, k * 512)
                un, ut = load_w(wu_d[l], 8, nb * 128, k * 512)
                for j in range(nb):
                    fc = k * 4 + j
                    pgn, pg = next_ps()
                    mm(pg[:, :], pgn, [(gt[:, kc * nb * 128 + j * 128: kc * nb * 128 + (j + 1) * 128], xn(kc, tg * 512, 512))
                                       for kc in range(8)], [gn, "A.%d" % tg])
                    pun, pu = next_ps()
                    mm(pu[:, :], pun, [(ut[:, kc * nb * 128 + j * 128: kc * nb * 128 + (j + 1) * 128], xn(kc, tg * 512, 512))
                                       for kc in range(8)], [un, "A.%d" % tg])
                    sg = T32[fc % 2]
                    sgn = "T32_%d" % (fc % 2)
                    p.op("act", lambda e, sg=sg, pg=pg: e.activation(out=sg[:, :], in_=pg[:, :], func=AF.Silu),
                         reads=[pgn], writes=[sgn])
                    kw = {"writes": ["ACTT"]} if fc == 0 else {"adds": ["ACTT"]}
                    p.op("dve", lambda e, sg=sg, pu=pu, fc=fc: e.tensor_tensor(out=ACTT[:, fc * 512:(fc + 1) * 512], in0=sg[:, :],
                                                                              in1=pu[:, :], op=ALU.mult),
                         reads=[sgn, pun], **kw)
            for dmc in range(8):
                name, t = next_w()
                view = t[:, 0:NFC * 128].rearrange("p (k c) -> p k c", k=NFC)
                src = wd_d[l][:, dmc * 128:(dmc + 1) * 128].rearrange("(k p) c -> p k c", p=128)
                p.dma("pool", view, src, name, writes=[name])
                psn, ps = next_ps()
                mm(ps[:, :], psn, [(t[:, fc * 128:(fc + 1) * 128], ACTT[:, fc * 512:(fc + 1) * 512]) for fc in range(NFC)],
                   [name, "ACTT"])
                hs = hT[:, dmc * NT + tg * 512: dmc * NT + (tg + 1) * 512]
                p.op("dve", lambda e, hs=hs, ps=ps: e.tensor_tensor(out=hs, in0=hs, in1=ps[:, :], op=ALU.add),
                     reads=[psn], adds=["hT.%d" % tg])
        p.alias(["ACTT"], B_RES)

    def fox_proj(l):
        norm_to_A(PC_ATTN + l * 8)
        p.alias(B_RES, ["STG0", "STG1"])
        w = w_in_d[l]
        proj_fm(w, 3088, 0, 8, evac_stage_store(lambda ci: (QM[ci * 128:(ci + 1) * 128, :], "QM"), scale=0.125))
        proj_fm(w, 3088, 1024, 8, evac_stage_store(lambda ci: (KTb[ci * 128:(ci + 1) * 128, :], "KTb")))
        v_proj(lambda kc, tb: xn(kc, tb * 128, 128), lambda tb: ["A.%d" % (tb // 4)], w, 2048, 8)
        wn, wt = load_w(w, 8, 16, 3072)
        psn, ps = next_ps()
        for tb in range(16):
            for kc in range(8):
                kw = {"writes": [psn]} if (tb == 0 and kc == 0) else {"adds": [psn]}
                p.op("pe", lambda e, tb=tb, kc=kc: e.matmul(ps[:, tb * 16:(tb + 1) * 16], lhsT=xn(kc, tb * 128, 128),
                                                            rhs=wt[:, kc * 16:(kc + 1) * 16], start=(kc == 0), stop=(kc == 7),
                                                            skip_group_check=True),
                     reads=[wn, "A.%d" % (tb // 4)], **kw)
        z = T32[3]
        p.op("dve", lambda e: e.tensor_tensor(out=z[:, 0:256], in0=ps[:, 0:256], in1=BFB[:, l * 256:(l + 1) * 256], op=ALU.add),
             reads=[psn, "BFB"], writes=["T32_3"])
        p.op("act", lambda e: e.activation(out=z[:, 0:256], in_=z[:, 0:256], func=AF.Exp, scale=-1.0),
             reads=["T32_3"], writes=["T32_3"])
        p.op("act", lambda e: e.activation(out=LFown[:, :], in_=z[:, 0:256], func=AF.Ln, bias=1.0, scale=1.0),
             reads=["T32_3"], writes=["LFown"])
        p.dma("sp", LFb.rearrange("(t p) h -> p t h", p=128), LFown[:, :].rearrange("p (t h) -> p t h", h=16),
              "st_LFb", reads=["LFown"], writes=["LFb"])
        p.cc(KTb, KTg, "KTg", reads=["KTb"], writes=["KTg"])
        p.cc(Vb, Vg, "Vg", reads=["Vb"], writes=["Vg"])
        p.cc(LFb, LFg, "LFg", reads=["LFb"], writes=["LFg"])
        p.alias(["STG0", "STG1"], B_RES)

    def fox_cumsum():
        CUMN = ["CUM", "CUMh", "CUMt", "CUMo", "CUMtot", "CUMth", "CUMoff", "CUMoffo", "CUMdh"]
        p.alias(B_RES, CUMN)
        LFall = B32[:, 0:2048]
        HML = [B[:, 4096 + i * 2048: 4096 + (i + 1) * 2048] for i in range(3)]
        TMP = B32[:, 5120:6144]
        DHall = B[0:16, 12288:12288 + 1536]
        DH = [B[0:16, 12288 + i * 512: 12288 + (i + 1) * 512] for i in range(3)]
        OHML = [B[:, 14336 + i * 256: 14336 + (i + 1) * 256] for i in range(3)]
        TOTS = B32[:, 7552:7568]
        THML = [B[:, 15136 + i * 16: 15136 + (i + 1) * 16] for i in range(3)]
        OFFS = B32[0:16, 7600:7728]
        OFFO = B32[0:16, 7728:7744]
        DT = T32[0][0:16, :]
        DTMP = T32[1][0:16, :]
        p.dma("sp", LFall.rearrange("p (b h) -> p b h", h=16), LFg.rearrange("(b p) h -> p b h", p=128), "CUMld",
              reads=["LFg"], writes=["CUM"])
        for hf in range(2):
            sl = slice(hf * 1024, (hf + 1) * 1024)
            split3(LFall[:, sl], ["CUM"], HML[0][:, sl], HML[1][:, sl], HML[2][:, sl], TMP, "CUMt", "CUMh", hf == 0)
        split3(LFown[:, :], ["LFown"], OHML[0], OHML[1], OHML[2], TMP[:, 0:256], "CUMt", "CUMo", True)
        psn, ps = next_ps()
        first = True
        for h in range(16):
            for i in range(3):
                kw = {"writes": [psn]} if first else {"adds": [psn]}
                first = False
                l_ap = HML[i].rearrange("p (b h) -> p b h", h=16)[:, :, h]
                p.op("pe", lambda e, l_ap=l_ap, h=h, i=i, ps=ps: e.matmul(ps[:, h:h + 1], lhsT=l_ap, rhs=ONES[:, 0:1], start=(i == 0),
                                                                  stop=(i == 2), skip_group_check=True),
                     reads=["CUMh", "ONES"], **kw)
        p.op("dve", lambda e, ps=ps: e.tensor_copy(out=TOTS, in_=ps[:, 0:16]), reads=[psn], writes=["CUMtot"])
        split3(TOTS, ["CUMtot"], THML[0], THML[1], THML[2], TMP[:, 0:16], "CUMt", "CUMth", True)
        psn2, ps2 = next_ps()
        mm(ps2[0:16, 0:128], psn2, [(THML[i], MLT) for i in range(3)], ["CUMth", "CST"])
        p.op("dve", lambda e: e.tensor_copy(out=OFFS, in_=ps2[0:16, 0:128]), reads=[psn2], writes=["CUMoff"])
        psn3, ps3 = next_ps()
        mm(ps3[0:16, 0:16], psn3, [(THML[i], MOWN) for i in range(3)], ["CUMth", "CST"])
        p.op("dve", lambda e: e.tensor_copy(out=OFFO, in_=ps3[0:16, 0:16]), reads=[psn3], writes=["CUMoffo"])
        for ch in range(32):
            psn, ps = next_ps()
            for j in range(4):
                b = ch * 4 + j
                for i in range(3):
                    kw = {"writes": [psn]} if (j == 0 and i == 0) else {"adds": [psn]}
                    p.op("pe", lambda e, ps=ps, j=j, b=b, i=i: e.matmul(ps[0:16, j * 128:(j + 1) * 128],
                                                                        lhsT=HML[i][:, b * 16:(b + 1) * 16], rhs=UTR,
                                                                        start=(i == 0), stop=(i == 2), skip_group_check=True),
                         reads=["CUMh", "CST"], **kw)
            for j in range(4):
                b = ch * 4 + j
                kw = {"writes": ["T32_0"]} if j == 0 else {"adds": ["T32_0"]}
                p.op("dve", lambda e, ps=ps, j=j, b=b: e.tensor_scalar(out=DT[:, j * 128:(j + 1) * 128],
                                                                      in0=ps[0:16, j * 128:(j + 1) * 128],
                                                                      scalar1=OFFS[:, b:b + 1], scalar2=None, op0=ALU.add),
                     reads=[psn, "CUMoff"], **kw)
            split3(DT, ["T32_0"], DH[0], DH[1], DH[2], DTMP, "T32_1", "CUMdh", True)
            p.dma("sp", CKv[:, 1:4, ch * 512:(ch + 1) * 512], DHall.rearrange("p (r t) -> p r t", r=3), "st_CKd",
                  reads=["CUMdh"], adds=["CKd"])
        for ch in range(4):
            psn, ps = next_ps()
            for j in range(4):
                b = ch * 4 + j
                for i in range(3):
                    kw = {"writes": [psn]} if (j == 0 and i == 0) else {"adds": [psn]}
                    p.op("pe", lambda e, ps=ps, j=j, b=b, i=i: e.matmul(ps[0:16, j * 128:(j + 1) * 128],
                                                                        lhsT=OHML[i][:, b * 16:(b + 1) * 16], rhs=UTR,
                                                                        start=(i == 0), stop=(i == 2), skip_group_check=True),
                         reads=["CUMo", "CST"], **kw)
            for j in range(4):
                b = ch * 4 + j
                kw = {"writes": ["T32_0"]} if j == 0 else {"adds": ["T32_0"]}
                p.op("dve", lambda e, ps=ps, j=j, b=b: e.tensor_scalar(out=DT[:, j * 128:(j + 1) * 128],
                                                                      in0=ps[0:16, j * 128:(j + 1) * 128],
                                                                      scalar1=OFFO[:, b:b + 1], scalar2=-1.0, op0=ALU.add,
                                                                      op1=ALU.mult),
                     reads=[psn, "CUMoffo"], **kw)
            p.op("dve", lambda e: e.tensor_copy(out=DH[0], in_=DT), reads=["T32_0"], writes=["CUMdh"])
            p.dma("sp", QAv[:, 0, ch * 512:(ch + 1) * 512], DH[0], "st_QA", reads=["CUMdh"], adds=["QA"])
        p.alias(CUMN, B_RES)

    def mla_kv():
        norm_to_A(PC_KV)
        p.alias(B_RES, ["CKV", "AT", "STG0", "STG1"] + ["CKV.%d" % t for t in range(4)])
        CKV = B[:, 0:2 * NT]
        AT = [B32[:, 4096 + i * 512: 4096 + (i + 1) * 512] for i in range(2)]
        wn, wt = load_w(wkva_d, 8, 288, 0)
        for tg in range(4):
            load_cs(tg)
            for cc in range(2):
                psn, ps = next_ps()
                mm(ps[:, :], psn, [(wt[:, kc * 288 + cc * 128: kc * 288 + (cc + 1) * 128], xn(kc, tg * 512, 512)) for kc in range(8)],
                   [wn, "A.%d" % tg])
                kw = {"writes": ["AT"]} if cc == 0 else {"adds": ["AT"]}
                p.op("dve", lambda e, cc=cc, ps=ps: e.tensor_copy(out=AT[cc], in_=ps[:, :]), reads=[psn], **kw)
            rs, rsn = rms_rstd(AT, ["AT"], 256.0, 1.0, tg)
            for cc in range(2):
                kw = {"writes": ["CKV.%d" % tg]} if cc == 0 else {"adds": ["CKV.%d" % tg]}
                p.op("dve", lambda e, cc=cc, tg=tg: e.scalar_tensor_tensor(
                    out=CKV[:, cc * NT + tg * 512: cc * NT + (tg + 1) * 512], in0=AT[cc],
                    scalar=PAR[:, PC_CKV + cc:PC_CKV + cc + 1], in1=rs[:, :], op0=ALU.mult, op1=ALU.mult),
                    reads=["AT", rsn, "PAR"], **kw)
            p1n, p1 = next_ps()
            mm(p1[0:16, :], p1n, [(wt[:, kc * 288 + 256: kc * 288 + 272], xn(kc, tg * 512, 512)) for kc in range(8)], [wn, "A.%d" % tg])
            p2n, p2 = next_ps()
            mm(p2[0:16, :], p2n, [(wt[:, kc * 288 + 272: kc * 288 + 288], xn(kc, tg * 512, 512)) for kc in range(8)], [wn, "A.%d" % tg])
            ro = STG[tg % 2]
            ron = "STG%d" % (tg % 2)
            rope_apply(p1, p1n, p2, p2n, 16, tg, ro, ron)
            p.dma("sp", KXb[0:16, tg * 512:(tg + 1) * 512], ro[0:16, 0:512], "st_KXb_%d" % (tg % 2), reads=[ron], adds=["KXb"])
            p.dma("sp", KXb[16:32, tg * 512:(tg + 1) * 512], ro[0:16, 512:1024], "st_KXb_%d" % (tg % 2), reads=[ron], adds=["KXb"])
        ckv_res = lambda tg: ["CKV.%d" % tg]
        proj_fm(wuk_d, 1024, 0, 8, evac_stage_store(lambda ci: (KTb[ci * 128:(ci + 1) * 128, :], "KTb")), kin=2,
                rhs_fn=lambda kc, tg: CKV[:, kc * NT + tg * 512: kc * NT + (tg + 1) * 512], rhs_res=ckv_res)
        v_proj(lambda kc, tb: CKV[:, kc * NT + tb * 128: kc * NT + (tb + 1) * 128], lambda tb: ["CKV.%d" % (tb // 4)], wuv_d, 0, 2)
        p.cc(KTb, KTg, "KTg", reads=["KTb"], writes=["KTg"])
        p.cc(Vb, Vg, "Vg", reads=["Vb"], writes=["Vg"])
        p.cc(KXb, KXg, "KXg", reads=["KXb"], writes=["KXg"])
        p.alias(["CKV", "AT", "STG0", "STG1"] + ["CKV.%d" % t for t in range(4)], B_RES)

    def mla_q(j, l):
        norm_to_A(PC_ATTN + l * 8)
        p.alias(B_RES, ["CQ", "CQN", "STG0", "STG1"])
        CQ = [B32[:, i * 512:(i + 1) * 512] for i in range(6)]
        CQN = B[:, 8192:8192 + 6 * 512]
        qs = 1.0 / math.sqrt(96.0)
        for tg in range(4):
            load_cs(tg)
            for half in range(2):
                nb = 4 if half == 0 else 2
                wn, wt = load_w(wdq_d[j], 8, nb * 128, half * 512)
                for jj in range(nb):
                    qc = half * 4 + jj
                    psn, ps = next_ps()
                    mm(ps[:, :], psn, [(wt[:, kc * nb * 128 + jj * 128: kc * nb * 128 + (jj + 1) * 128], xn(kc, tg * 512, 512))
                                       for kc in range(8)], [wn, "A.%d" % tg])
                    kw = {"writes": ["CQ"]} if qc == 0 else {"adds": ["CQ"]}
                    p.op("dve", lambda e, qc=qc, ps=ps: e.tensor_copy(out=CQ[qc], in_=ps[:, :]), reads=[psn], **kw)
            rs, rsn = rms_rstd(CQ, ["CQ"], 768.0, qs, tg)
            for qc in range(6):
                kw = {"writes": ["CQN"]} if qc == 0 else {"adds": ["CQN"]}
                p.op("dve", lambda e, qc=qc: e.scalar_tensor_tensor(
                    out=CQN[:, qc * 512:(qc + 1) * 512], in0=CQ[qc], scalar=PAR[:, PC_CQ + j * 6 + qc:PC_CQ + j * 6 + qc + 1],
                    in1=rs[:, :], op0=ALU.mult, op1=ALU.mult), reads=["CQ", rsn, "PAR"], **kw)
            for half in range(2):
                wn, wt = load_w(wuq_d[j], 6, 512, half * 512)
                for jj in range(4):
                    hp = half * 4 + jj
                    psn, ps = next_ps()
                    mm(ps[:, :], psn, [(wt[:, kc * 512 + jj * 128: kc * 512 + (jj + 1) * 128], CQN[:, kc * 512:(kc + 1) * 512])
                                       for kc in range(6)], [wn, "CQN"])
                    s = TB16[hp % 2]
                    sn = "TB16_%d" % (hp % 2)
                    p.op("dve", lambda e, s=s, ps=ps: e.tensor_copy(out=s[:, :], in_=ps[:, :]), reads=[psn], writes=[sn])
                    p.dma("sp", QM[hp * 128:(hp + 1) * 128, tg * 512:(tg + 1) * 512], s[:, :], "st_QMb_%d" % (hp % 2), reads=[sn], adds=["QM"])
            wn, wt = load_w(wuq_d[j], 6, 512, 1024)
            for hh in range(2):
                p1n, p1 = next_ps()
                mm(p1[:, :], p1n, [(wt[:, kc * 512 + hh * 128: kc * 512 + (hh + 1) * 128], CQN[:, kc * 512:(kc + 1) * 512])
                                   for kc in range(6)], [wn, "CQN"])
                p2n, p2 = next_ps()
                mm(p2[:, :], p2n, [(wt[:, kc * 512 + 256 + hh * 128: kc * 512 + 256 + (hh + 1) * 128], CQN[:, kc * 512:(kc + 1) * 512])
                                   for kc in range(6)], [wn, "CQN"])
                ro = STG[hh]
                ron = "STG%d" % hh
                rope_apply(p1, p1n, p2, p2n, 128, tg, ro, ron)
                p.dma("sp", QR[hh * 128:(hh + 1) * 128, tg * 512:(tg + 1) * 512], ro[:, 0:512], "st_QR_%d" % hh, reads=[ron], adds=["QR"])
                p.dma("sp", QR[256 + hh * 128:256 + (hh + 1) * 128, tg * 512:(tg + 1) * 512], ro[:, 512:1024], "st_QR_%d" % hh,
                      reads=[ron], adds=["QR"])
        p.alias(["CQ", "CQN", "STG0", "STG1"], B_RES)

    stop = DEBUG_STOP
    cnt = [0]

    def go():
        cnt[0] += 1
        return cnt[0] <= stop

    for l in range(4):
        if l < 2:
            if go():
                fox_proj(l)
            if go():
                fox_cumsum()
            if go():
                attention(l, True)
            if go():
                wo_proj(fwo_d[l])
        else:
            if l == 2:
                if go():
                    mla_kv()
            if go():
                mla_q(l - 2, l)
            if go():
                attention(l, False)
            if go():
                wo_proj(mwo_d[l - 2])
        if go():
            ffn(l)

    for tg in range(4):
        chunks = [hT[:, kc * NT + tg * 512: kc * NT + (tg + 1) * 512] for kc in range(8)]
        rs, rsn = rms_rstd(chunks, ["hT.%d" % tg], float(D), 1.0, tg)
        for kc in range(8):
            o = T32[kc % 4]
            on = "T32_%d" % (kc % 4)
            p.op("dve", lambda e, kc=kc, o=o, c=chunks[kc]: e.scalar_tensor_tensor(
                out=o[:, :], in0=c, scalar=PAR[:, PC_FIN + kc:PC_FIN + kc + 1], in1=rs[:, :], op0=ALU.mult, op1=ALU.mult),
                reads=["hT.%d" % tg, rsn, "PAR"], writes=[on])
            p.dma("sp", outT_d[:, kc * NT + tg * 512: kc * NT + (tg + 1) * 512], o[:, :], "out%d" % (kc % 4), reads=[on], adds=["OUT"])
    if DEBUG_DUMP:
        for (nm, src, shp, dt) in (("d_CKd", CKd, [64, S], BF16), ("d_QA", QA, [64, NT], BF16), ("d_QM", QM, [D, NT], BF16),
                                   ("d_KTg", KTg, [NCORES * D, NT], BF16), ("d_Vb", Vb, [2048, 1024], BF16),
                                   ("d_LFg", LFg, [S, 16], F32)):
            dd = nc.dram_tensor(nm, shp, dt, kind="ExternalOutput").ap()
            p.dma("sp", dd, src, "out", reads=[nm[2:]], adds=["OUT"])
    p.wait_all("sp", ["OUT"])

    p.finalize()
    block = es.enter_context(nc.Block())

    @block.tensor
    def _(e):
        p.emit("pe", e)

    @block.scalar
    def _(e):
        p.emit("act", e)

    @block.vector
    def _(e):
        p.emit("dve", e)

    @block.gpsimd
    def _(e):
        p.emit("pool", e)

    @block.sync
    def _(e):
        p.emit("sp", e)

    es.close()
    return nc


def _tok_index(c):
    m = np.arange(8)[:, None]
    j = np.arange(256)[None, :]
    return (m * 2048 + c * 256 + j).reshape(-1)


def _blockpos(r, lb):
    return 16 * (lb // 2) + 2 * r + (lb % 2)


def kernel(x, positions, attn_norm, ffn_norm, w_gate, w_up, w_down, fox_w_in, fox_b_f, fox_w_o, kv_norm, w_kv_a,
           ckv_norm, w_uk, w_uv, mla_w_dq, cq_norm, mla_w_uq, mla_w_o, final_norm):
    f32 = np.float32
    x = np.asarray(x, f32)
    positions = np.asarray(positions)
    par = np.zeros((128, NPAR), f32)

    def colmajor(v):
        v = np.asarray(v, f32)
        return v.reshape(-1, 128).T

    for l in range(4):
        par[:, PC_ATTN + l * 8: PC_ATTN + (l + 1) * 8] = colmajor(attn_norm[l])
        par[:, PC_FFN + l * 8: PC_FFN + (l + 1) * 8] = colmajor(ffn_norm[l])
    par[:, PC_KV:PC_KV + 8] = colmajor(kv_norm)
    par[:, PC_FIN:PC_FIN + 8] = colmajor(final_norm)
    for j in range(2):
        par[:, PC_CQ + j * 6: PC_CQ + (j + 1) * 6] = colmajor(cq_norm[j])
    par[:, PC_CKV:PC_CKV + 2] = colmajor(ckv_norm)
    inv_freq = (10000.0 ** (-np.arange(0, 16, dtype=np.float32) * 2.0 / 32)).astype(f32)
    par[:, PC_INVF] = inv_freq[np.arange(128) % 16]
    bfb = np.zeros((128, 512), f32)
    for l in range(2):
        bfb[:, l * 256:(l + 1) * 256] = np.tile(np.asarray(fox_b_f[l], f32), 16)[None, :]
    ident = np.eye(128, dtype=f32)
    utr = (np.arange(128)[:, None] <= np.arange(128)[None, :]).astype(f32)
    bpos = np.array([_blockpos(b // 16, b % 16) for b in range(128)])
    mlt = (bpos[:, None] < bpos[None, :]).astype(f32)
    tri = np.where(np.arange(128)[:, None] > np.arange(128)[None, :], NEG, 0.0).astype(f32)
    wuq_p = []
    for j in range(2):
        w = np.asarray(mla_w_uq[j], f32).reshape(768, 16, 96)
        wuq_p.append(np.ascontiguousarray(np.concatenate(
            [w[:, :, 0:64].reshape(768, 1024), w[:, :, 64:80].reshape(768, 256), w[:, :, 80:96].reshape(768, 256)], axis=1)))
    shared = {
        "par": par, "bfb": bfb,
        "wkva": np.ascontiguousarray(w_kv_a, f32),
        "wuk": np.ascontiguousarray(np.asarray(w_uk, f32).reshape(256, 1024)),
        "wuv": np.ascontiguousarray(np.asarray(w_uv, f32).reshape(256, 1024)),
    }
    for l in range(2):
        shared["w_in%d" % l] = np.ascontiguousarray(fox_w_in[l], f32)
        shared["fwo%d" % l] = np.ascontiguousarray(fox_w_o[l], f32)
        shared["wdq%d" % l] = np.ascontiguousarray(mla_w_dq[l], f32)
        shared["wuq%d" % l] = wuq_p[l]
        shared["mwo%d" % l] = np.ascontiguousarray(mla_w_o[l], f32)
    for l in range(4):
        shared["wg%d" % l] = np.ascontiguousarray(w_gate[l], f32)
        shared["wu%d" % l] = np.ascontiguousarray(w_up[l], f32)
        shared["wd%d" % l] = np.ascontiguousarray(w_down[l], f32)
    in_maps = []
    idxs = []
    for c in range(NCORES):
        idx = _tok_index(c)
        idxs.append(idx)
        xc = x[0][idx]
        xT = np.ascontiguousarray(xc.T.reshape(8, 128, NT).transpose(1, 0, 2).reshape(128, 8 * NT))
        pos = np.ascontiguousarray(positions[0][idx].astype(np.int32).reshape(1, NT))
        msk = np.zeros((128, 2, 8, 2, 128), f32)
        for qp in range(2):
            for r in range(8):
                for kp in range(2):
                    pk, pq = 2 * r + kp, 2 * c + qp
                    if pk > pq:
                        msk[:, qp, r, kp, :] = NEG
                    elif pk == pq:
                        msk[:, qp, r, kp, :] = tri
        ownpos = np.array([_blockpos(c, lb) for lb in range(16)])
        mown = (bpos[:, None] < ownpos[None, :]).astype(f32)
        cst = np.ascontiguousarray(np.concatenate([ident, utr, mlt, mown], axis=1))
        m = dict(shared)
        m.update({"xT": xT, "pos": pos, "msk": np.ascontiguousarray(msk.reshape(128, 32 * 128)), "cst": cst})
        in_maps.append(m)
    nc = build_program()
    res = run_bass_kernel_spmd(nc, in_maps, core_ids=list(range(NCORES)))
    if DEBUG_DUMP:
        global DUMPS
        DUMPS = [{k: np.asarray(v) for k, v in r.items()} for r in res.results]
    out = np.zeros((1, S, D), f32)
    for c in range(NCORES):
        oT = np.asarray(res.results[c]["outT"]).reshape(128, 8, NT)
        out[0][idxs[c]] = oT.transpose(2, 1, 0).reshape(NT, D)
    return out
```

```python
import math
from contextlib import ExitStack
import numpy as np
import concourse.bass as bass
import concourse.mybir as mybir
from concourse.bass_utils import run_bass_kernel_spmd

F32 = mybir.dt.float32
BF16 = mybir.dt.bfloat16
I32 = mybir.dt.int32
AF = mybir.ActivationFunctionType
ALU = mybir.AluOpType

NCORES = 8
D = 1024
S = 16384
NT = 2048
DFF = 2816
NFC = 22
EPS = 1e-6
ROLL = 30000
NEG = -30000.0
DEBUG_STOP = 1000
DEBUG_DUMP = False

PC_ATTN = 0
PC_FFN = 32
PC_KV = 64
PC_FIN = 72
PC_CQ = 80
PC_CKV = 92
PC_INVF = 94
NPAR = 96

ENGS = ("pe", "act", "dve", "pool", "sp")


class Tok:
    __slots__ = ("kind", "eng", "sig", "sem", "val", "seq")

    def __init__(self, kind, eng=None, sem=None, val=None, seq=0):
        self.kind = kind
        self.eng = eng
        self.seq = seq
        self.sig = False
        self.sem = sem
        self.val = val


class Prog:
    def __init__(self, nc, es):
        self.nc = nc
        self.es = es
        self.q = {e: [] for e in ENGS}
        self.res = {}
        self.dsem = {}
        self.dcnt = {}
        self.esems = {e: [] for e in ENGS}

    def sem(self, name):
        return self.es.enter_context(self.nc.semaphore(name))

    def _r(self, n):
        r = self.res.get(n)
        if r is None:
            r = self.res[n] = {"w": [], "r": []}
        return r

    def _deps(self, e, reads, writes, adds):
        raw = []
        for n in reads:
            raw += self._r(n)["w"]
        oth = []
        for n in writes:
            r = self._r(n)
            oth += r["w"] + r["r"]
        for n in adds:
            oth += self._r(n)["r"]
        out = []
        seen = set()
        best = {}
        for (lst, is_raw) in ((raw, True), (oth, False)):
            for d in lst:
                if id(d) in seen:
                    continue
                seen.add(id(d))
                if d.kind == "eng":
                    if d.eng == e and (e == "pe" or not is_raw):
                        continue
                    b = best.get(d.eng)
                    if b is None or b.seq < d.seq:
                        best[d.eng] = d
                else:
                    out.append(d)
        for d in best.values():
            d.sig = True
            out.append(d)
        return out

    def _upd(self, tok, reads, writes, adds):
        for n in reads:
            self._r(n)["r"].append(tok)
        for n in writes:
            r = self._r(n)
            r["w"] = [tok]
            r["r"] = []
        for n in adds:
            self._r(n)["w"].append(tok)

    def op(self, e, fn, reads=(), writes=(), adds=()):
        deps = self._deps(e, reads, writes, adds)
        tok = Tok("eng", eng=e, seq=len(self.q[e]))
        self.q[e].append((fn, deps, tok, 1))
        self._upd(tok, reads, writes, adds)
        return tok

    def dma(self, e, out, in_, key, reads=(), writes=(), adds=()):
        deps = self._deps(e, reads, writes, adds)
        if key not in self.dsem:
            self.dsem[key] = self.sem("d_" + key)
            self.dcnt[key] = 0
        self.dcnt[key] += 16
        tok = Tok("dma", sem=self.dsem[key], val=self.dcnt[key])
        fn = lambda eng, o=out, i=in_: eng.dma_start(out=o, in_=i)
        self.q[e].append((fn, deps, tok, 16))
        self._upd(tok, reads, writes, adds)
        return tok

    def cc(self, ins, outs, key, reads=(), writes=()):
        writes = list(writes) + ["__CC__"]
        deps = self._deps("pool", reads, writes, ())
        if key not in self.dsem:
            self.dsem[key] = self.sem("c_" + key)
            self.dcnt[key] = 0
        self.dcnt[key] += 1
        tok = Tok("dma", sem=self.dsem[key], val=self.dcnt[key])
        fn = lambda eng, i=ins, o=outs: eng.collective_compute(
            "AllGather", ALU.bypass, replica_groups=[list(range(NCORES))], ins=[i], outs=[o])
        self.q["pool"].append((fn, deps, tok, 1))
        self._upd(tok, reads, writes, ())
        return tok

    def alias(self, src, dst):
        toks = []
        for n in src:
            r = self._r(n)
            toks += r["w"] + r["r"]
        for n in dst:
            self._r(n)["r"] += toks

    def wait_all(self, e, names):
        deps = self._deps(e, names, (), ())
        self.q[e].append((None, deps, None, 0))

    def finalize(self):
        for e in ENGS:
            n = 0
            for (fn, deps, tok, inc) in self.q[e]:
                if tok is not None and tok.kind == "eng" and tok.sig:
                    k = n // ROLL
                    while len(self.esems[e]) <= k:
                        self.esems[e].append(self.sem("e_%s%d" % (e, len(self.esems[e]))))
                    tok.sem = self.esems[e][k]
                    tok.val = n % ROLL + 1
                    n += 1

    def emit(self, e, eng):
        waited = {}
        for (fn, deps, tok, inc) in self.q[e]:
            need = {}
            for d in deps:
                k = id(d.sem)
                if waited.get(k, 0) >= d.val:
                    continue
                if k not in need or need[k][1] < d.val:
                    need[k] = (d.sem, d.val)
            for k, (s, v) in need.items():
                eng.wait_ge(s, v)
                waited[k] = v
            if fn is None:
                continue
            ins = fn(eng)
            if tok.kind == "dma":
                ins.then_inc(tok.sem, inc)
            elif tok.sig:
                ins.then_inc(tok.sem, 1)


def build_program():
    nc = bass.Bass("TRN2", target_bir_lowering=False)
    es = ExitStack()
    p = Prog(nc, es)

    def din(name, shape, dt=F32):
        return nc.dram_tensor(name, list(shape), dt, kind="ExternalInput").ap()

    def dscr(name, shape, dt):
        return nc.dram_tensor(name, list(shape), dt).ap()

    def sb(name, shape, dt):
        return es.enter_context(nc.sbuf_tensor(name, list(shape), dt))

    xT_d = din("xT", [128, 8 * NT])
    pos_d = din("pos", [1, NT], I32)
    par_d = din("par", [128, NPAR])
    bfb_d = din("bfb", [128, 2 * 256])
    msk_d = din("msk", [128, 32 * 128])
    cst_d = din("cst", [128, 3 * 128 + 16])
    w_in_d = [din("w_in%d" % l, [D, 3088]) for l in range(2)]
    fwo_d = [din("fwo%d" % l, [D, D]) for l in range(2)]
    wkva_d = din("wkva", [D, 288])
    wuk_d = din("wuk", [256, D])
    wuv_d = din("wuv", [256, D])
    wdq_d = [din("wdq%d" % j, [D, 768]) for j in range(2)]
    wuq_d = [din("wuq%d" % j, [768, 1536]) for j in range(2)]
    mwo_d = [din("mwo%d" % j, [D, D]) for j in range(2)]
    wg_d = [din("wg%d" % l, [D, DFF]) for l in range(4)]
    wu_d = [din("wu%d" % l, [D, DFF]) for l in range(4)]
    wd_d = [din("wd%d" % l, [DFF, D]) for l in range(4)]
    outT_d = nc.dram_tensor("outT", [128, 8 * NT], F32, kind="ExternalOutput").ap()

    QM = dscr("QM", [D, NT], BF16)
    QA = dscr("QA", [16 * 4, NT], BF16)
    QR = dscr("QR", [2 * 256, NT], BF16)
    KTb = dscr("KTb", [D, NT], BF16)
    KTg = dscr("KTg", [NCORES * D, NT], BF16)
    KXb = dscr("KXb", [32, NT], BF16)
    KXg = dscr("KXg", [NCORES * 32, NT], BF16)
    Vb = dscr("Vb", [16 * 128, 16 * 64], BF16)
    Vg = dscr("Vg", [NCORES * 16 * 128, 16 * 64], BF16)
    LFb = dscr("LFb", [NT, 16], F32)
    LFg = dscr("LFg", [S, 16], F32)
    CKd = dscr("CKd", [16 * 4, S], BF16)
    COSd = dscr("COSd", [128, NT], F32)
    SINd = dscr("SINd", [128, NT], F32)

    hT = sb("hT", [128, 8 * NT], F32)
    A = sb("A", [128, 8 * NT], BF16)
    B = sb("B", [128, 8 * NT], BF16)
    Wt = [sb("W%d" % i, [128, 4096], BF16) for i in range(3)]
    Vt = [sb("V%d" % i, [128, 16 * 128], BF16) for i in range(2)]
    MSK = sb("MSK", [128, 32 * 128], BF16)
    CST = sb("CST", [128, 3 * 128 + 16], BF16)
    PAR = sb("PAR", [128, NPAR], F32)
    BFB = sb("BFB", [128, 512], F32)
    ONES = sb("ONES", [128, 128], BF16)
    LFown = sb("LFown", [128, 256], F32)
    T32 = [sb("T32_%d" % i, [128, 512], F32) for i in range(5)]
    TB16 = [sb("TB16_%d" % i, [128, 512], BF16) for i in range(2)]
    STG = [B[:, 12288 + i * NT: 12288 + (i + 1) * NT] for i in range(2)]
    CS = [sb("CS%d" % i, [128, 512], F32) for i in range(2)]
    PS = [es.enter_context(nc.psum_tensor("ps%d" % i, [128, 512], F32)) for i in range(8)]

    IDN = CST[:, 0:128]
    UTR = CST[:, 128:256]
    MLT = CST[:, 256:384]
    MOWN = CST[:, 384:400]

    Kt = [A[:, i * NT:(i + 1) * NT] for i in range(2)]
    Qt = [A[:, (2 + i) * NT:(3 + i) * NT] for i in range(2)]
    Pt = [A[:, 4 * NT + i * 512: 4 * NT + (i + 1) * 512] for i in range(3)]
    ATT_RES = ["K0m", "K0x", "K1m", "K1x", "Q0", "Q1", "P0", "P1", "P2"]
    A_RES = ["A.%d" % t for t in range(4)]
    B_RES = ["B.%d" % t for t in range(4)]

    B32 = B[:, :].bitcast(F32)

    psi = [0]

    def next_ps():
        i = psi[0] % 8
        psi[0] += 1
        return "ps%d" % i, PS[i]

    wi = [0]

    def next_w():
        i = wi[0] % 3
        wi[0] += 1
        return "W%d" % i, Wt[i]

    def load_w(src_ap, rows_k, ncols, c0=0):
        name, t = next_w()
        view = t[:, 0:rows_k * ncols].rearrange("p (k c) -> p k c", k=rows_k)
        src = src_ap[:, c0:c0 + ncols].rearrange("(k p) c -> p k c", p=128)
        p.dma("pool", view, src, name, writes=[name])
        return name, t

    p.dma("sp", PAR[:, :], par_d, "i_par", writes=["PAR"])
    p.dma("sp", BFB[:, :], bfb_d, "i_bfb", writes=["BFB"])
    p.dma("pool", MSK[:, :], msk_d, "i_msk", writes=["MSK"])
    p.dma("pool", CST[:, :], cst_d, "i_cst", writes=["CST"])
    for kc in range(8):
        p.dma("sp", hT[:, kc * NT:(kc + 1) * NT], xT_d[:, kc * NT:(kc + 1) * NT], "i_h",
              adds=["hT.%d" % t for t in range(4)])
    p.op("pool", lambda e: e.memset(ONES[:, :], 1.0), writes=["ONES"])
    for i in range(2):
        p.op("pool", lambda e, i=i: e.memset(Vt[i][:, :], 1.0), writes=["V%d" % i])
    p.op("pool", lambda e: e.memset(A[0:16, 0:NT], 1.0), writes=["A.0"])
    CKv = CKd.rearrange("(h r) s -> h r s", r=4)
    QAv = QA.rearrange("(h r) s -> h r s", r=4)
    for q8 in range(8):
        p.dma("sp", CKv[:, 0, q8 * NT:(q8 + 1) * NT], A[0:16, 0:NT], "initw", reads=["A.0"], adds=["CKd"])
    for r in range(1, 4):
        p.dma("sp", QAv[:, r, :], A[0:16, 0:NT], "initw", reads=["A.0"], adds=["QA"])

    POSI = B[:, 0:2 * NT].bitcast(I32)
    ANG = B32[:, NT:2 * NT]
    ARG = B32[:, 2 * NT:3 * NT]
    TAB = B32[:, 3 * NT:4 * NT]
    p.dma("sp", POSI, pos_d.partition_broadcast(128), "i_pos", writes=["B.0"])
    p.op("dve", lambda e: e.tensor_copy(out=ANG, in_=POSI), reads=["B.0"], writes=["B.1"])
    p.op("dve", lambda e: e.tensor_scalar(out=ANG, in0=ANG, scalar1=PAR[:, PC_INVF:PC_INVF + 1], scalar2=None,
                                          op0=ALU.mult), reads=["PAR", "B.1"], writes=["B.1"])
    RR = B32[:, 0:NT]
    MAGIC = 12582912.0
    C1 = 6.28125
    C2 = 2.0 * math.pi - 6.28125
    PI_LO = 3.1415925
    for (dst, nm) in ((SINd, "SINd"), (COSd, "COSd")):
        if nm == "COSd":
            p.op("dve", lambda e: e.tensor_scalar(out=ANG, in0=ANG, scalar1=0.5 * math.pi, scalar2=None, op0=ALU.add),
                 reads=["B.1"], writes=["B.1"])
        p.op("dve", lambda e: e.tensor_scalar(out=ARG, in0=ANG, scalar1=1.0 / (2.0 * math.pi), scalar2=MAGIC,
                                              op0=ALU.mult, op1=ALU.add), reads=["B.1"], writes=["B.2"])
        p.op("dve", lambda e: e.tensor_scalar(out=ARG, in0=ARG, scalar1=-MAGIC, scalar2=None, op0=ALU.add),
             reads=["B.2"], writes=["B.2"])
        p.op("dve", lambda e: e.scalar_tensor_tensor(out=RR, in0=ARG, scalar=-C1, in1=ANG, op0=ALU.mult, op1=ALU.add),
             reads=["B.2", "B.1"], writes=["B.0"])
        p.op("dve", lambda e: e.scalar_tensor_tensor(out=RR, in0=ARG, scalar=-C2, in1=RR, op0=ALU.mult, op1=ALU.add),
             reads=["B.2", "B.0"], writes=["B.0"])
        p.op("dve", lambda e: e.tensor_scalar(out=RR, in0=RR, scalar1=-PI_LO, scalar2=PI_LO, op0=ALU.max, op1=ALU.min),
             reads=["B.0"], writes=["B.0"])
        p.op("act", lambda e: e.activation(out=TAB, in_=RR, func=AF.Sin), reads=["B.0"], writes=["B.3"])
        p.dma("sp", dst, TAB, "i_tab", reads=["B.3"], writes=[nm])

    def mm(out_ap, psname, pairs, reads, first=True, last=True):
        n = len(pairs)
        for i, (l, r) in enumerate(pairs):
            st = first and i == 0
            sp_ = last and i == n - 1
            if st:
                p.op("pe", lambda e, l=l, r=r, st=st, sp_=sp_: e.matmul(out_ap, lhsT=l, rhs=r, start=st, stop=sp_),
                     reads=reads, writes=[psname])
            else:
                p.op("pe", lambda e, l=l, r=r, st=st, sp_=sp_: e.matmul(out_ap, lhsT=l, rhs=r, start=st, stop=sp_),
                     reads=reads, adds=[psname])

    def rms_rstd(chunks, chunk_res, N, qs, tg_tag):
        psn, ps = next_ps()
        n = len(chunks)
        for i, c in enumerate(chunks):
            sq = TB16[i % 2]
            sqn = "TB16_%d" % (i % 2)
            p.op("act", lambda e, c=c, sq=sq: e.activation(out=sq[:, :], in_=c, func=AF.Square),
                 reads=chunk_res, writes=[sqn])
            st = (i == 0)
            sp_ = (i == n - 1)
            if st:
                p.op("pe", lambda e, sq=sq, st=st, sp_=sp_: e.matmul(ps[:, :], lhsT=ONES[:, :], rhs=sq[:, :], start=st, stop=sp_),
                     reads=[sqn, "ONES"], writes=[psn])
            else:
                p.op("pe", lambda e, sq=sq, st=st, sp_=sp_: e.matmul(ps[:, :], lhsT=ONES[:, :], rhs=sq[:, :], start=st, stop=sp_),
                     reads=[sqn, "ONES"], adds=[psn])
        rs = T32[4]
        p.op("act", lambda e: e.activation(out=rs[:, :], in_=ps[:, :], func=AF.Sqrt, scale=1.0 / (N * qs * qs),
                                           bias=EPS / (qs * qs)), reads=[psn], writes=["T32_4"])
        p.op("dve", lambda e: e.reciprocal(out=rs[:, :], in_=rs[:, :]), reads=["T32_4"], writes=["T32_4"])
        return rs, "T32_4"

    def norm_to_A(gcol):
        for tg in range(4):
            chunks = [hT[:, kc * NT + tg * 512: kc * NT + (tg + 1) * 512] for kc in range(8)]
            rs, rsn = rms_rstd(chunks, ["hT.%d" % tg], float(D), 1.0, tg)
            for kc in range(8):
                p.op("dve", lambda e, kc=kc, tg=tg, c=chunks[kc]: e.scalar_tensor_tensor(
                    out=A[:, kc * NT + tg * 512: kc * NT + (tg + 1) * 512], in0=c,
                    scalar=PAR[:, gcol + kc:gcol + kc + 1], in1=rs[:, :], op0=ALU.mult, op1=ALU.mult),
                    reads=["hT.%d" % tg, rsn, "PAR"], **({"writes": ["A.%d" % tg]} if kc == 0 else {"adds": ["A.%d" % tg]}))

    def xn(kc, t0, n):
        return A[:, kc * NT + t0: kc * NT + t0 + n]

    def proj_fm(w_src, ncol_total, c0, nchunks, dst_fn, scale=None, kin=8, rhs_fn=None, rhs_res=None):
        done = 0
        while done < nchunks:
            nb = min(4, nchunks - done)
            wn, wt = load_w(w_src, kin, nb * 128, c0 + done * 128)
            for j in range(nb):
                for tg in range(4):
                    psn, ps = next_ps()
                    pairs = []
                    for kc in range(kin):
                        l = wt[:, kc * nb * 128 + j * 128: kc * nb * 128 + (j + 1) * 128]
                        r = rhs_fn(kc, tg) if rhs_fn else xn(kc, tg * 512, 512)
                        pairs.append((l, r))
                    mm(ps[:, :], psn, pairs, [wn] + (rhs_res(tg) if rhs_res else ["A.%d" % tg]))
                    dst_fn(done + j, tg, ps, psn)
            done += nb

    def evac_stage_store(dst_dram_rows, scale=None):
        def fn(ci, tg, ps, psn, dst=dst_dram_rows):
            s = STG[ci % 2]
            sn = "STG%d" % (ci % 2)
            kw = {"writes": [sn]} if tg == 0 else {"adds": [sn]}
            if scale is None:
                p.op("dve", lambda e: e.tensor_copy(out=s[:, tg * 512:(tg + 1) * 512], in_=ps[:, :]), reads=[psn], **kw)
            else:
                p.op("dve", lambda e: e.tensor_scalar(out=s[:, tg * 512:(tg + 1) * 512], in0=ps[:, :], scalar1=scale,
                                                      scalar2=None, op0=ALU.mult), reads=[psn], **kw)
            if tg == 3:
                d_ap, d_res = dst(ci)
                p.dma("sp", d_ap, s[:, :], "st_%s_%d" % (d_res, ci % 2), reads=[sn], adds=[d_res])
        return fn

    def v_proj(lhs_fn, lhs_res, w_src, c0, kin):
        Vb4 = Vb.rearrange("(h p) (t d) -> p h t d", p=128, d=64)
        for half in range(2):
            wn, wt = load_w(w_src, kin, 512, c0 + half * 512)
            for tb in range(16):
                psn, ps = next_ps()
                pairs = [(lhs_fn(kc, tb), wt[:, kc * 512:(kc + 1) * 512]) for kc in range(kin)]
                mm(ps[:, :], psn, pairs, [wn] + lhs_res(tb))
                s = TB16[tb % 2]
                sn = "TB16_%d" % (tb % 2)
                p.op("dve", lambda e, s=s, ps=ps: e.tensor_copy(out=s[:, :], in_=ps[:, :]), reads=[psn], writes=[sn])
                p.dma("sp", Vb4[:, half * 8:(half + 1) * 8, tb, :], s[:, :].rearrange("p (h d) -> p h d", d=64),
                      "st_Vb_%d" % (tb % 2), reads=[sn], adds=["Vb"])

    def split3(src, srcres, hi, mid, lo, tmp, tmpres, outres, first):
        kw = (lambda: {"writes": [outres]}) if first else (lambda: {"adds": [outres]})
        p.op("dve", lambda e: e.tensor_copy(out=hi, in_=src), reads=srcres, **kw())
        p.op("dve", lambda e: e.tensor_tensor(out=tmp, in0=src, in1=hi, op=ALU.subtract), reads=srcres + [outres], writes=[tmpres])
        p.op("dve", lambda e: e.tensor_copy(out=mid, in_=tmp), reads=[tmpres], adds=[outres])
        p.op("dve", lambda e: e.tensor_tensor(out=tmp, in0=tmp, in1=mid, op=ALU.subtract), reads=[tmpres, outres], writes=[tmpres])
        p.op("dve", lambda e: e.tensor_copy(out=lo, in_=tmp), reads=[tmpres], adds=[outres])

    def rope_apply(x1ps, x1n, x2ps, x2n, np_, tg, o_tile, o_res):
        cosn, sinn = "CS0", "CS1"
        t = [T32[i][0:np_, :] for i in range(4)]
        tn = ["T32_%d" % i for i in range(4)]
        p.op("dve", lambda e: e.tensor_tensor(out=t[0], in0=x1ps[0:np_, :], in1=CS[0][0:np_, :], op=ALU.mult), reads=[x1n, cosn], writes=[tn[0]])
        p.op("dve", lambda e: e.tensor_tensor(out=t[1], in0=x2ps[0:np_, :], in1=CS[1][0:np_, :], op=ALU.mult), reads=[x2n, sinn], writes=[tn[1]])
        p.op("dve", lambda e: e.tensor_tensor(out=t[2], in0=x1ps[0:np_, :], in1=CS[1][0:np_, :], op=ALU.mult), reads=[x1n, sinn], writes=[tn[2]])
        p.op("dve", lambda e: e.tensor_tensor(out=t[3], in0=x2ps[0:np_, :], in1=CS[0][0:np_, :], op=ALU.mult), reads=[x2n, cosn], writes=[tn[3]])
        p.op("dve", lambda e: e.tensor_tensor(out=o_tile[0:np_, 0:512], in0=t[0], in1=t[1], op=ALU.subtract), reads=[tn[0], tn[1]], writes=[o_res])
        p.op("dve", lambda e: e.tensor_tensor(out=o_tile[0:np_, 512:1024], in0=t[2], in1=t[3], op=ALU.add), reads=[tn[2], tn[3]], adds=[o_res])

    def load_cs(tg):
        p.dma("sp", CS[0][:, :], COSd[:, tg * 512:(tg + 1) * 512], "CS0", reads=["COSd"], writes=["CS0"])
        p.dma("sp", CS[1][:, :], SINd[:, tg * 512:(tg + 1) * 512], "CS1", reads=["SINd"], writes=["CS1"])

    def attention(l, fox):
        kx = 4 if fox else 32
        nrow = 64 + kx
        p.alias(A_RES, ATT_RES)
        ACC = [(3 + g, "ps%d" % (3 + g)) for g in range(4)]
        tiles = []
        kvi = 0
        for h in range(16):
            for r in range(8):
                sl = kvi % 2
                kvi += 1
                for g in range(4):
                    for lb in range(4 * g + 4):
                        m = lb // 2
                        kp = lb % 2
                        if m < 2 * g:
                            c0, c1, msk = 0, 512, []
                        elif m == 2 * g:
                            c0, c1, msk = 0, 512, [(0, 0), (1, 128)]
                        else:
                            c0, c1, msk = 256, 512, [(0, 256), (1, 384)]
                        msk = [(qp, cc) for (qp, cc) in msk if not (r == 0 and kp == 0 and qp == 1)]
                        tiles.append(dict(h=h, r=r, sl=sl, g=g, lb=lb, kp=kp, c0=c0, c1=c1, msk=msk,
                                          first=(r == 0 and lb == 0), last=(r == 7 and lb == 4 * g + 3),
                                          newq=(r == 0 and g == 0 and lb == 0), newkv=(g == 0 and lb == 0),
                                          endhead=(r == 7 and g == 3 and lb == 15)))
        for i, t in enumerate(tiles):
            t["si"] = i % 3

        def emit_loads(t):
            h, r, sl = t["h"], t["r"], t["sl"]
            if t["newq"]:
                qn = "Q%d" % (h % 2)
                qt = Qt[h % 2]
                p.dma("sp", qt[0:64, :], QM[h * 64:(h + 1) * 64, :], qn, reads=["QM"], writes=[qn])
                if fox:
                    p.dma("sp", qt[64:68, :], QA[h * 4:(h + 1) * 4, :], qn, reads=["QA"], adds=[qn])
                else:
                    p.dma("sp", qt[64:80, :], QR[h * 16:(h + 1) * 16, :], qn, reads=["QR"], adds=[qn])
                    p.dma("sp", qt[80:96, :], QR[256 + h * 16:256 + (h + 1) * 16, :], qn, reads=["QR"], adds=[qn])
            if t["newkv"]:
                kt = Kt[sl]
                vt = Vt[sl]
                kmn, kxn, vn = "K%dm" % sl, "K%dx" % sl, "V%d" % sl
                p.dma("sp", kt[0:64, :], KTg[r * D + h * 64: r * D + (h + 1) * 64, :], kmn, reads=["KTg"], writes=[kmn])
                if fox:
                    p.dma("sp", kt[64:68, :], CKd[h * 4:(h + 1) * 4, r * NT:(r + 1) * NT], kxn, reads=["CKd"], writes=[kxn])
                else:
                    p.dma("sp", kt[64:96, :], KXg[r * 32:(r + 1) * 32, :], kxn, reads=["KXg"], writes=[kxn])
                p.dma("sp", vt[:, :].rearrange("p (l c) -> p l c", l=16)[:, :, 0:64],
                      Vg[r * 2048 + h * 128: r * 2048 + (h + 1) * 128, :].rearrange("p (l c) -> p l c", l=16),
                      vn, reads=["Vg"], writes=[vn])

        def emit_qk(t):
            emit_loads(t)
            h, r, sl, g, lb, kp, c0, c1, msk, si = (t[k] for k in ("h", "r", "sl", "g", "lb", "kp", "c0", "c1", "msk", "si"))
            kt = Kt[sl]
            qt = Qt[h % 2]
            kmn, kxn, qn = "K%dm" % sl, "K%dx" % sl, "Q%d" % (h % 2)
            sps = PS[si]
            spn = "ps%d" % si
            lq = kt[0:nrow, lb * 128:(lb + 1) * 128]
            rq = qt[0:nrow, g * 512 + c0: g * 512 + c1]
            nm = len(msk)
            p.op("pe", lambda e, sps=sps, lq=lq, rq=rq, c0=c0, c1=c1, nm=nm: e.matmul(
                sps[:, c0:c1], lhsT=lq, rhs=rq, start=True, stop=(nm == 0), skip_group_check=True),
                reads=[kmn, kxn, qn], writes=[spn])
            for mi, (qp, cc) in enumerate(msk):
                mo = ((qp * 8 + r) * 2 + kp) * 128
                p.op("pe", lambda e, sps=sps, cc=cc, mo=mo, mi=mi, nm=nm: e.matmul(
                    sps[:, cc:cc + 128], lhsT=IDN, rhs=MSK[:, mo:mo + 128], start=False, stop=(mi == nm - 1),
                    skip_group_check=True), reads=["CST", "MSK"], adds=[spn])

        def emit_exp(t):
            si, c0, c1 = t["si"], t["c0"], t["c1"]
            sps, pt = PS[si], Pt[si]
            p.op("act", lambda e, pt=pt, sps=sps, c0=c0, c1=c1: e.activation(
                out=pt[:, c0:c1], in_=sps[:, c0:c1], func=AF.Exp), reads=["ps%d" % si], writes=["P%d" % si])

        def emit_pv(t):
            h, sl, g, lb, c0, c1, si = (t[k] for k in ("h", "sl", "g", "lb", "c0", "c1", "si"))
            vt, pt = Vt[sl], Pt[si]
            accps, accn = PS[ACC[g][0]], ACC[g][1]
            first, last = t["first"], t["last"]
            kw = {"writes": [accn]} if first else {"adds": [accn]}
            p.op("pe", lambda e, accps=accps, vt=vt, lb=lb, pt=pt, c0=c0, c1=c1, first=first, last=last: e.matmul(
                accps[:, c0:c1], lhsT=vt[:, lb * 128:(lb + 1) * 128], rhs=pt[:, c0:c1], start=first, stop=last,
                skip_group_check=True), reads=["V%d" % sl, "P%d" % si], **kw)

        def emit_norm(h):
            for g in range(4):
                accps = PS[ACC[g][0]]
                accn = ACC[g][1]
                rc = T32[g % 2]
                rcn = "T32_%d" % (g % 2)
                p.op("dve", lambda e, rc=rc, accps=accps: e.reciprocal(out=rc[0:64, :], in_=accps[64:128, :]),
                     reads=[accn], writes=[rcn])
                po = (h % 2) * 64
                o_ap = B[po:po + 64, (h // 2) * NT + g * 512:(h // 2) * NT + (g + 1) * 512]
                p.op("dve", lambda e, rc=rc, accps=accps, o_ap=o_ap: e.tensor_tensor(out=o_ap, in0=accps[0:64, :], in1=rc[0:64, :],
                                                                                     op=ALU.mult),
                     reads=[accn, rcn], adds=["B.%d" % g])

        n = len(tiles)
        LA = 2
        for j in range(min(LA, n)):
            emit_qk(tiles[j])
        for i, t in enumerate(tiles):
            if i + LA < n:
                emit_qk(tiles[i + LA])
            emit_exp(t)
            emit_pv(t)
            if t["endhead"]:
                emit_norm(t["h"])
        p.alias(ATT_RES, A_RES)

    def wo_proj(w_src):
        for half in range(2):
            wn, wt = load_w(w_src, 8, 512, half * 512)
            for j in range(4):
                dmc = half * 4 + j
                for tg in range(4):
                    psn, ps = next_ps()
                    pairs = [(wt[:, kc * 512 + j * 128: kc * 512 + (j + 1) * 128],
                              B[:, kc * NT + tg * 512: kc * NT + (tg + 1) * 512]) for kc in range(8)]
                    mm(ps[:, :], psn, pairs, [wn, "B.%d" % tg])
                    hs = hT[:, dmc * NT + tg * 512: dmc * NT + (tg + 1) * 512]
                    p.op("dve", lambda e, hs=hs, ps=ps: e.tensor_tensor(out=hs, in0=hs, in1=ps[:, :], op=ALU.add),
                         reads=[psn], adds=["hT.%d" % tg])

    def ffn(l):
        norm_to_A(PC_FFN + l * 8)
        ACTT = B
        p.alias(B_RES, ["ACTT"])
        for tg in range(4):
            for k in range(6):
                nb = 4 if k < 5 else 2
                gn, gt = load_w(wg_d[l], 8, nb * 128, k * 512)
                un, ut = load_w(wu_d[l], 8, nb * 128, k * 512)
                for j in range(nb):
                    fc = k * 4 + j
                    pgn, pg = next_ps()
                    mm(pg[:, :], pgn, [(gt[:, kc * nb * 128 + j * 128: kc * nb * 128 + (j + 1) * 128], xn(kc, tg * 512, 512))
                                       for kc in range(8)], [gn, "A.%d" % tg])
                    pun, pu = next_ps()
                    mm(pu[:, :], pun, [(ut[:, kc * nb * 128 + j * 128: kc * nb * 128 + (j + 1) * 128], xn(kc, tg * 512, 512))
                                       for kc in range(8)], [un, "A.%d" % tg])
                    sg = T32[fc % 2]
                    sgn = "T32_%d" % (fc % 2)
                    p.op("act", lambda e, sg=sg, pg=pg: e.activation(out=sg[:, :], in_=pg[:, :], func=AF.Silu),
                         reads=[pgn], writes=[sgn])
                    kw = {"writes": ["ACTT"]} if fc == 0 else {"adds": ["ACTT"]}
                    p.op("dve", lambda e, sg=sg, pu=pu, fc=fc: e.tensor_tensor(out=ACTT[:, fc * 512:(fc + 1) * 512], in0=sg[:, :],
                                                                              in1=pu[:, :], op=ALU.mult),
                         reads=[sgn, pun], **kw)
            for dmc in range(8):
                name, t = next_w()
                view = t[:, 0:NFC * 128].rearrange("p (k c) -> p k c", k=NFC)
                src = wd_d[l][:, dmc * 128:(dmc + 1) * 128].rearrange("(k p) c -> p k c", p=128)
                p.dma("pool", view, src, name, writes=[name])
                psn, ps = next_ps()
                mm(ps[:, :], psn, [(t[:, fc * 128:(fc + 1) * 128], ACTT[:, fc * 512:(fc + 1) * 512]) for fc in range(NFC)],
                   [name, "ACTT"])
                hs = hT[:, dmc * NT + tg * 512: dmc * NT + (tg + 1) * 512]
                p.op("dve", lambda e, hs=hs, ps=ps: e.tensor_tensor(out=hs, in0=hs, in1=ps[:, :], op=ALU.add),
                     reads=[psn], adds=["hT.%d" % tg])
        p.alias(["ACTT"], B_RES)

    def fox_proj(l):
        norm_to_A(PC_ATTN + l * 8)
        p.alias(B_RES, ["STG0", "STG1"])
        w = w_in_d[l]
        proj_fm(w, 3088, 0, 8, evac_stage_store(lambda ci: (QM[ci * 128:(ci + 1) * 128, :], "QM"), scale=0.125))
        proj_fm(w, 3088, 1024, 8, evac_stage_store(lambda ci: (KTb[ci * 128:(ci + 1) * 128, :], "KTb")))
        v_proj(lambda kc, tb: xn(kc, tb * 128, 128), lambda tb: ["A.%d" % (tb // 4)], w, 2048, 8)
        wn, wt = load_w(w, 8, 16, 3072)
        psn, ps = next_ps()
        for tb in range(16):
            for kc in range(8):
                kw = {"writes": [psn]} if (tb == 0 and kc == 0) else {"adds": [psn]}
                p.op("pe", lambda e, tb=tb, kc=kc: e.matmul(ps[:, tb * 16:(tb + 1) * 16], lhsT=xn(kc, tb * 128, 128),
                                                            rhs=wt[:, kc * 16:(kc + 1) * 16], start=(kc == 0), stop=(kc == 7),
                                                            skip_group_check=True),
                     reads=[wn, "A.%d" % (tb // 4)], **kw)
        z = T32[3]
        p.op("dve", lambda e: e.tensor_tensor(out=z[:, 0:256], in0=ps[:, 0:256], in1=BFB[:, l * 256:(l + 1) * 256], op=ALU.add),
             reads=[psn, "BFB"], writes=["T32_3"])
        p.op("act", lambda e: e.activation(out=z[:, 0:256], in_=z[:, 0:256], func=AF.Exp, scale=-1.0),
             reads=["T32_3"], writes=["T32_3"])
        p.op("act", lambda e: e.activation(out=LFown[:, :], in_=z[:, 0:256], func=AF.Ln, bias=1.0, scale=1.0),
             reads=["T32_3"], writes=["LFown"])
        p.dma("sp", LFb.rearrange("(t p) h -> p t h", p=128), LFown[:, :].rearrange("p (t h) -> p t h", h=16),
              "st_LFb", reads=["LFown"], writes=["LFb"])
        p.cc(KTb, KTg, "KTg", reads=["KTb"], writes=["KTg"])
        p.cc(Vb, Vg, "Vg", reads=["Vb"], writes=["Vg"])
        p.cc(LFb, LFg, "LFg", reads=["LFb"], writes=["LFg"])
        p.alias(["STG0", "STG1"], B_RES)

    def fox_cumsum():
        CUMN = ["CUM", "CUMh", "CUMt", "CUMo", "CUMtot", "CUMth", "CUMoff", "CUMoffo", "CUMdh"]
        p.alias(B_RES, CUMN)
        LFall = B32[:, 0:2048]
        HML = [B[:, 4096 + i * 2048: 4096 + (i + 1) * 2048] for i in range(3)]
        TMP = B32[:, 5120:6144]
        DHall = B[0:16, 12288:12288 + 1536]
        DH = [B[0:16, 12288 + i * 512: 12288 + (i + 1) * 512] for i in range(3)]
        OHML = [B[:, 14336 + i * 256: 14336 + (i + 1) * 256] for i in range(3)]
        TOTS = B32[:, 7552:7568]
        THML = [B[:, 15136 + i * 16: 15136 + (i + 1) * 16] for i in range(3)]
        OFFS = B32[0:16, 7600:7728]
        OFFO = B32[0:16, 7728:7744]
        DT = T32[0][0:16, :]
        DTMP = T32[1][0:16, :]
        p.dma("sp", LFall.rearrange("p (b h) -> p b h", h=16), LFg.rearrange("(b p) h -> p b h", p=128), "CUMld",
              reads=["LFg"], writes=["CUM"])
        for hf in range(2):
            sl = slice(hf * 1024, (hf + 1) * 1024)
            split3(LFall[:, sl], ["CUM"], HML[0][:, sl], HML[1][:, sl], HML[2][:, sl], TMP, "CUMt", "CUMh", hf == 0)
        split3(LFown[:, :], ["LFown"], OHML[0], OHML[1], OHML[2], TMP[:, 0:256], "CUMt", "CUMo", True)
        psn, ps = next_ps()
        first = True
        for h in range(16):
            for i in range(3):
                kw = {"writes": [psn]} if first else {"adds": [psn]}
                first = False
                l_ap = HML[i].rearrange("p (b h) -> p b h", h=16)[:, :, h]
                p.op("pe", lambda e, l_ap=l_ap, h=h, i=i, ps=ps: e.matmul(ps[:, h:h + 1], lhsT=l_ap, rhs=ONES[:, 0:1], start=(i == 0),
                                                                  stop=(i == 2), skip_group_check=True),
                     reads=["CUMh", "ONES"], **kw)
        p.op("dve", lambda e, ps=ps: e.tensor_copy(out=TOTS, in_=ps[:, 0:16]), reads=[psn], writes=["CUMtot"])
        split3(TOTS, ["CUMtot"], THML[0], THML[1], THML[2], TMP[:, 0:16], "CUMt", "CUMth", True)
        psn2, ps2 = next_ps()
        mm(ps2[0:16, 0:128], psn2, [(THML[i], MLT) for i in range(3)], ["CUMth", "CST"])
        p.op("dve", lambda e: e.tensor_copy(out=OFFS, in_=ps2[0:16, 0:128]), reads=[psn2], writes=["CUMoff"])
        psn3, ps3 = next_ps()
        mm(ps3[0:16, 0:16], psn3, [(THML[i], MOWN) for i in range(3)], ["CUMth", "CST"])
        p.op("dve", lambda e: e.tensor_copy(out=OFFO, in_=ps3[0:16, 0:16]), reads=[psn3], writes=["CUMoffo"])
        for ch in range(32):
            psn, ps = next_ps()
            for j in range(4):
                b = ch * 4 + j
                for i in range(3):
                    kw = {"writes": [psn]} if (j == 0 and i == 0) else {"adds": [psn]}
                    p.op("pe", lambda e, ps=ps, j=j, b=b, i=i: e.matmul(ps[0:16, j * 128:(j + 1) * 128],
                                                                        lhsT=HML[i][:, b * 16:(b + 1) * 16], rhs=UTR,
                                                                        start=(i == 0), stop=(i == 2), skip_group_check=True),
                         reads=["CUMh", "CST"], **kw)
            for j in range(4):
                b = ch * 4 + j
                kw = {"writes": ["T32_0"]} if j == 0 else {"adds": ["T32_0"]}
                p.op("dve", lambda e, ps=ps, j=j, b=b: e.tensor_scalar(out=DT[:, j * 128:(j + 1) * 128],
                                                                      in0=ps[0:16, j * 128:(j + 1) * 128],
                                                                      scalar1=OFFS[:, b:b + 1], scalar2=None, op0=ALU.add),
                     reads=[psn, "CUMoff"], **kw)
            split3(DT, ["T32_0"], DH[0], DH[1], DH[2], DTMP, "T32_1", "CUMdh", True)
            p.dma("sp", CKv[:, 1:4, ch * 512:(ch + 1) * 512], DHall.rearrange("p (r t) -> p r t", r=3), "st_CKd",
                  reads=["CUMdh"], adds=["CKd"])
        for ch in range(4):
            psn, ps = next_ps()
            for j in range(4):
                b = ch * 4 + j
                for i in range(3):
                    kw = {"writes": [psn]} if (j == 0 and i == 0) else {"adds": [psn]}
                    p.op("pe", lambda e, ps=ps, j=j, b=b, i=i: e.matmul(ps[0:16, j * 128:(j + 1) * 128],
                                                                        lhsT=OHML[i][:, b * 16:(b + 1) * 16], rhs=UTR,
                                                                        start=(i == 0), stop=(i == 2), skip_group_check=True),
                         reads=["CUMo", "CST"], **kw)
            for j in range(4):
                b = ch * 4 + j
                kw = {"writes": ["T32_0"]} if j == 0 else {"adds": ["T32_0"]}
                p.op("dve", lambda e, ps=ps, j=j, b=b: e.tensor_scalar(out=DT[:, j * 128:(j + 1) * 128],
                                                                      in0=ps[0:16, j * 128:(j + 1) * 128],
                                                                      scalar1=OFFO[:, b:b + 1], scalar2=-1.0, op0=ALU.add,
                                                                      op1=ALU.mult),
                     reads=[psn, "CUMoffo"], **kw)
            p.op("dve", lambda e: e.tensor_copy(out=DH[0], in_=DT), reads=["T32_0"], writes=["CUMdh"])
            p.dma("sp", QAv[:, 0, ch * 512:(ch + 1) * 512], DH[0], "st_QA", reads=["CUMdh"], adds=["QA"])
        p.alias(CUMN, B_RES)

    def mla_kv():
        norm_to_A(PC_KV)
        p.alias(B_RES, ["CKV", "AT", "STG0", "STG1"] + ["CKV.%d" % t for t in range(4)])
        CKV = B[:, 0:2 * NT]
        AT = [B32[:, 4096 + i * 512: 4096 + (i + 1) * 512] for i in range(2)]
        wn, wt = load_w(wkva_d, 8, 288, 0)
        for tg in range(4):
            load_cs(tg)
            for cc in range(2):
                psn, ps = next_ps()
                mm(ps[:, :], psn, [(wt[:, kc * 288 + cc * 128: kc * 288 + (cc + 1) * 128], xn(kc, tg * 512, 512)) for kc in range(8)],
                   [wn, "A.%d" % tg])
                kw = {"writes": ["AT"]} if cc == 0 else {"adds": ["AT"]}
                p.op("dve", lambda e, cc=cc, ps=ps: e.tensor_copy(out=AT[cc], in_=ps[:, :]), reads=[psn], **kw)
            rs, rsn = rms_rstd(AT, ["AT"], 256.0, 1.0, tg)
            for cc in range(2):
                kw = {"writes": ["CKV.%d" % tg]} if cc == 0 else {"adds": ["CKV.%d" % tg]}
                p.op("dve", lambda e, cc=cc, tg=tg: e.scalar_tensor_tensor(
                    out=CKV[:, cc * NT + tg * 512: cc * NT + (tg + 1) * 512], in0=AT[cc],
                    scalar=PAR[:, PC_CKV + cc:PC_CKV + cc + 1], in1=rs[:, :], op0=ALU.mult, op1=ALU.mult),
                    reads=["AT", rsn, "PAR"], **kw)
            p1n, p1 = next_ps()
            mm(p1[0:16, :], p1n, [(wt[:, kc * 288 + 256: kc * 288 + 272], xn(kc, tg * 512, 512)) for kc in range(8)], [wn, "A.%d" % tg])
            p2n, p2 = next_ps()
            mm(p2[0:16, :], p2n, [(wt[:, kc * 288 + 272: kc * 288 + 288], xn(kc, tg * 512, 512)) for kc in range(8)], [wn, "A.%d" % tg])
            ro = STG[tg % 2]
            ron = "STG%d" % (tg % 2)
            rope_apply(p1, p1n, p2, p2n, 16, tg, ro, ron)
            p.dma("sp", KXb[0:16, tg * 512:(tg + 1) * 512], ro[0:16, 0:512], "st_KXb_%d" % (tg % 2), reads=[ron], adds=["KXb"])
            p.dma("sp", KXb[16:32, tg * 512:(tg + 1) * 512], ro[0:16, 512:1024], "st_KXb_%d" % (tg % 2), reads=[ron], adds=["KXb"])
        ckv_res = lambda tg: ["CKV.%d" % tg]
        proj_fm(wuk_d, 1024, 0, 8, evac_stage_store(lambda ci: (KTb[ci * 128:(ci + 1) * 128, :], "KTb")), kin=2,
                rhs_fn=lambda kc, tg: CKV[:, kc * NT + tg * 512: kc * NT + (tg + 1) * 512], rhs_res=ckv_res)
        v_proj(lambda kc, tb: CKV[:, kc * NT + tb * 128: kc * NT + (tb + 1) * 128], lambda tb: ["CKV.%d" % (tb // 4)], wuv_d, 0, 2)
        p.cc(KTb, KTg, "KTg", reads=["KTb"], writes=["KTg"])
        p.cc(Vb, Vg, "Vg", reads=["Vb"], writes=["Vg"])
        p.cc(KXb, KXg, "KXg", reads=["KXb"], writes=["KXg"])
        p.alias(["CKV", "AT", "STG0", "STG1"] + ["CKV.%d" % t for t in range(4)], B_RES)

    def mla_q(j, l):
        norm_to_A(PC_ATTN + l * 8)
        p.alias(B_RES, ["CQ", "CQN", "STG0", "STG1"])
        CQ = [B32[:, i * 512:(i + 1) * 512] for i in range(6)]
        CQN = B[:, 8192:8192 + 6 * 512]
        qs = 1.0 / math.sqrt(96.0)
        for tg in range(4):
            load_cs(tg)
            for half in range(2):
                nb = 4 if half == 0 else 2
                wn, wt = load_w(wdq_d[j], 8, nb * 128, half * 512)
                for jj in range(nb):
                    qc = half * 4 + jj
                    psn, ps = next_ps()
                    mm(ps[:, :], psn, [(wt[:, kc * nb * 128 + jj * 128: kc * nb * 128 + (jj + 1) * 128], xn(kc, tg * 512, 512))
                                       for kc in range(8)], [wn, "A.%d" % tg])
                    kw = {"writes": ["CQ"]} if qc == 0 else {"adds": ["CQ"]}
                    p.op("dve", lambda e, qc=qc, ps=ps: e.tensor_copy(out=CQ[qc], in_=ps[:, :]), reads=[psn], **kw)
            rs, rsn = rms_rstd(CQ, ["CQ"], 768.0, qs, tg)
            for qc in range(6):
                kw = {"writes": ["CQN"]} if qc == 0 else {"adds": ["CQN"]}
                p.op("dve", lambda e, qc=qc: e.scalar_tensor_tensor(
                    out=CQN[:, qc * 512:(qc + 1) * 512], in0=CQ[qc], scalar=PAR[:, PC_CQ + j * 6 + qc:PC_CQ + j * 6 + qc + 1],
                    in1=rs[:, :], op0=ALU.mult, op1=ALU.mult), reads=["CQ", rsn, "PAR"], **kw)
            for half in range(2):
                wn, wt = load_w(wuq_d[j], 6, 512, half * 512)
                for jj in range(4):
                    hp = half * 4 + jj
                    psn, ps = next_ps()
                    mm(ps[:, :], psn, [(wt[:, kc * 512 + jj * 128: kc * 512 + (jj + 1) * 128], CQN[:, kc * 512:(kc + 1) * 512])
                                       for kc in range(6)], [wn, "CQN"])
                    s = TB16[hp % 2]
                    sn = "TB16_%d" % (hp % 2)
                    p.op("dve", lambda e, s=s, ps=ps: e.tensor_copy(out=s[:, :], in_=ps[:, :]), reads=[psn], writes=[sn])
                    p.dma("sp", QM[hp * 128:(hp + 1) * 128, tg * 512:(tg + 1) * 512], s[:, :], "st_QMb_%d" % (hp % 2), reads=[sn], adds=["QM"])
            wn, wt = load_w(wuq_d[j], 6, 512, 1024)
            for hh in range(2):
                p1n, p1 = next_ps()
                mm(p1[:, :], p1n, [(wt[:, kc * 512 + hh * 128: kc * 512 + (hh + 1) * 128], CQN[:, kc * 512:(kc + 1) * 512])
                                   for kc in range(6)], [wn, "CQN"])
                p2n, p2 = next_ps()
                mm(p2[:, :], p2n, [(wt[:, kc * 512 + 256 + hh * 128: kc * 512 + 256 + (hh + 1) * 128], CQN[:, kc * 512:(kc + 1) * 512])
                                   for kc in range(6)], [wn, "CQN"])
                ro = STG[hh]
                ron = "STG%d" % hh
                rope_apply(p1, p1n, p2, p2n, 128, tg, ro, ron)
                p.dma("sp", QR[hh * 128:(hh + 1) * 128, tg * 512:(tg + 1) * 512], ro[:, 0:512], "st_QR_%d" % hh, reads=[ron], adds=["QR"])
                p.dma("sp", QR[256 + hh * 128:256 + (hh + 1) * 128, tg * 512:(tg + 1) * 512], ro[:, 512:1024], "st_QR_%d" % hh,
                      reads=[ron], adds=["QR"])
        p.alias(["CQ", "CQN", "STG0", "STG1"], B_RES)

    stop = DEBUG_STOP
    cnt = [0]

    def go():
        cnt[0] += 1
        return cnt[0] <= stop

    for l in range(4):
        if l < 2:
            if go():
                fox_proj(l)
            if go():
                fox_cumsum()
            if go():
                attention(l, True)
            if go():
                wo_proj(fwo_d[l])
        else:
            if l == 2:
                if go():
                    mla_kv()
            if go():
                mla_q(l - 2, l)
            if go():
                attention(l, False)
            if go():
                wo_proj(mwo_d[l - 2])
        if go():
            ffn(l)

    for tg in range(4):
        chunks = [hT[:, kc * NT + tg * 512: kc * NT + (tg + 1) * 512] for kc in range(8)]
        rs, rsn = rms_rstd(chunks, ["hT.%d" % tg], float(D), 1.0, tg)
        for kc in range(8):
            o = T32[kc % 4]
            on = "T32_%d" % (kc % 4)
            p.op("dve", lambda e, kc=kc, o=o, c=chunks[kc]: e.scalar_tensor_tensor(
                out=o[:, :], in0=c, scalar=PAR[:, PC_FIN + kc:PC_FIN + kc + 1], in1=rs[:, :], op0=ALU.mult, op1=ALU.mult),
                reads=["hT.%d" % tg, rsn, "PAR"], writes=[on])
            p.dma("sp", outT_d[:, kc * NT + tg * 512: kc * NT + (tg + 1) * 512], o[:, :], "out%d" % (kc % 4), reads=[on], adds=["OUT"])
    if DEBUG_DUMP:
        for (nm, src, shp, dt) in (("d_CKd", CKd, [64, S], BF16), ("d_QA", QA, [64, NT], BF16), ("d_QM", QM, [D, NT], BF16),
                                   ("d_KTg", KTg, [NCORES * D, NT], BF16), ("d_Vb", Vb, [2048, 1024], BF16),
                                   ("d_LFg", LFg, [S, 16], F32)):
            dd = nc.dram_tensor(nm, shp, dt, kind="ExternalOutput").ap()
            p.dma("sp", dd, src, "out", reads=[nm[2:]], adds=["OUT"])
    p.wait_all("sp", ["OUT"])

    p.finalize()
    block = es.enter_context(nc.Block())

    @block.tensor
    def _(e):
        p.emit("pe", e)

    @block.scalar
    def _(e):
        p.emit("act", e)

    @block.vector
    def _(e):
        p.emit("dve", e)

    @block.gpsimd
    def _(e):
        p.emit("pool", e)

    @block.sync
    def _(e):
        p.emit("sp", e)

    es.close()
    return nc


def _tok_index(c):
    m = np.arange(8)[:, None]
    j = np.arange(256)[None, :]
    return (m * 2048 + c * 256 + j).reshape(-1)


def _blockpos(r, lb):
    return 16 * (lb // 2) + 2 * r + (lb % 2)


def kernel(x, positions, attn_norm, ffn_norm, w_gate, w_up, w_down, fox_w_in, fox_b_f, fox_w_o, kv_norm, w_kv_a,
           ckv_norm, w_uk, w_uv, mla_w_dq, cq_norm, mla_w_uq, mla_w_o, final_norm):
    f32 = np.float32
    x = np.asarray(x, f32)
    positions = np.asarray(positions)
    par = np.zeros((128, NPAR), f32)

    def colmajor(v):
        v = np.asarray(v, f32)
        return v.reshape(-1, 128).T

    for l in range(4):
        par[:, PC_ATTN + l * 8: PC_ATTN + (l + 1) * 8] = colmajor(attn_norm[l])
        par[:, PC_FFN + l * 8: PC_FFN + (l + 1) * 8] = colmajor(ffn_norm[l])
    par[:, PC_KV:PC_KV + 8] = colmajor(kv_norm)
    par[:, PC_FIN:PC_FIN + 8] = colmajor(final_norm)
    for j in range(2):
        par[:, PC_CQ + j * 6: PC_CQ + (j + 1) * 6] = colmajor(cq_norm[j])
    par[:, PC_CKV:PC_CKV + 2] = colmajor(ckv_norm)
    inv_freq = (10000.0 ** (-np.arange(0, 16, dtype=np.float32) * 2.0 / 32)).astype(f32)
    par[:, PC_INVF] = inv_freq[np.arange(128) % 16]
    bfb = np.zeros((128, 512), f32)
    for l in range(2):
        bfb[:, l * 256:(l + 1) * 256] = np.tile(np.asarray(fox_b_f[l], f32), 16)[None, :]
    ident = np.eye(128, dtype=f32)
    utr = (np.arange(128)[:, None] <= np.arange(128)[None, :]).astype(f32)
    bpos = np.array([_blockpos(b // 16, b % 16) for b in range(128)])
    mlt = (bpos[:, None] < bpos[None, :]).astype(f32)
    tri = np.where(np.arange(128)[:, None] > np.arange(128)[None, :], NEG, 0.0).astype(f32)
    wuq_p = []
    for j in range(2):
        w = np.asarray(mla_w_uq[j], f32).reshape(768, 16, 96)
        wuq_p.append(np.ascontiguousarray(np.concatenate(
            [w[:, :, 0:64].reshape(768, 1024), w[:, :, 64:80].reshape(768, 256), w[:, :, 80:96].reshape(768, 256)], axis=1)))
    shared = {
        "par": par, "bfb": bfb,
        "wkva": np.ascontiguousarray(w_kv_a, f32),
        "wuk": np.ascontiguousarray(np.asarray(w_uk, f32).reshape(256, 1024)),
        "wuv": np.ascontiguousarray(np.asarray(w_uv, f32).reshape(256, 1024)),
    }
    for l in range(2):
        shared["w_in%d" % l] = np.ascontiguousarray(fox_w_in[l], f32)
        shared["fwo%d" % l] = np.ascontiguousarray(fox_w_o[l], f32)
        shared["wdq%d" % l] = np.ascontiguousarray(mla_w_dq[l], f32)
        shared["wuq%d" % l] = wuq_p[l]
        shared["mwo%d" % l] = np.ascontiguousarray(mla_w_o[l], f32)
    for l in range(4):
        shared["wg%d" % l] = np.ascontiguousarray(w_gate[l], f32)
        shared["wu%d" % l] = np.ascontiguousarray(w_up[l], f32)
        shared["wd%d" % l] = np.ascontiguousarray(w_down[l], f32)
    in_maps = []
    idxs = []
    for c in range(NCORES):
        idx = _tok_index(c)
        idxs.append(idx)
        xc = x[0][idx]
        xT = np.ascontiguousarray(xc.T.reshape(8, 128, NT).transpose(1, 0, 2).reshape(128, 8 * NT))
        pos = np.ascontiguousarray(positions[0][idx].astype(np.int32).reshape(1, NT))
        msk = np.zeros((128, 2, 8, 2, 128), f32)
        for qp in range(2):
            for r in range(8):
                for kp in range(2):
                    pk, pq = 2 * r + kp, 2 * c + qp
                    if pk > pq:
                        msk[:, qp, r, kp, :] = NEG
                    elif pk == pq:
                        msk[:, qp, r, kp, :] = tri
        ownpos = np.array([_blockpos(c, lb) for lb in range(16)])
        mown = (bpos[:, None] < ownpos[None, :]).astype(f32)
        cst = np.ascontiguousarray(np.concatenate([ident, utr, mlt, mown], axis=1))
        m = dict(shared)
        m.update({"xT": xT, "pos": pos, "msk": np.ascontiguousarray(msk.reshape(128, 32 * 128)), "cst": cst})
        in_maps.append(m)
    nc = build_program()
    res = run_bass_kernel_spmd(nc, in_maps, core_ids=list(range(NCORES)))
    if DEBUG_DUMP:
        global DUMPS
        DUMPS = [{k: np.asarray(v) for k, v in r.items()} for r in res.results]
    out = np.zeros((1, S, D), f32)
    for c in range(NCORES):
        oT = np.asarray(res.results[c]["outT"]).reshape(128, 8, NT)
        out[0][idxs[c]] = oT.transpose(2, 1, 0).reshape(NT, D)
    return out
```

```python
import math
from contextlib import ExitStack
import numpy as np
import concourse.bass as bass
import concourse.mybir as mybir
from concourse.bass_utils import run_bass_kernel_spmd

F32 = mybir.dt.float32
BF16 = mybir.dt.bfloat16
I32 = mybir.dt.int32
AF = mybir.ActivationFunctionType
ALU = mybir.AluOpType

NCORES = 8
D = 1024
S = 16384
NT = 2048
DFF = 2816
NFC = 22
EPS = 1e-6
ROLL = 30000
NEG = -30000.0
DEBUG_STOP = 1000
DEBUG_DUMP = False

PC_ATTN = 0
PC_FFN = 32
PC_KV = 64
PC_FIN = 72
PC_CQ = 80
PC_CKV = 92
PC_INVF = 94
NPAR = 96

ENGS = ("pe", "act", "dve", "pool", "sp")


class Tok:
    __slots__ = ("kind", "eng", "sig", "sem", "val", "seq")

    def __init__(self, kind, eng=None, sem=None, val=None, seq=0):
        self.kind = kind
        self.eng = eng
        self.seq = seq
        self.sig = False
        self.sem = sem
        self.val = val


class Prog:
    def __init__(self, nc, es):
        self.nc = nc
        self.es = es
        self.q = {e: [] for e in ENGS}
        self.res = {}
        self.dsem = {}
        self.dcnt = {}
        self.esems = {e: [] for e in ENGS}

    def sem(self, name):
        return self.es.enter_context(self.nc.semaphore(name))

    def _r(self, n):
        r = self.res.get(n)
        if r is None:
            r = self.res[n] = {"w": [], "r": []}
        return r

    def _deps(self, e, reads, writes, adds):
        raw = []
        for n in reads:
            raw += self._r(n)["w"]
        oth = []
        for n in writes:
            r = self._r(n)
            oth += r["w"] + r["r"]
        for n in adds:
            oth += self._r(n)["r"]
        out = []
        seen = set()
        best = {}
        for (lst, is_raw) in ((raw, True), (oth, False)):
            for d in lst:
                if id(d) in seen:
                    continue
                seen.add(id(d))
                if d.kind == "eng":
                    if d.eng == e and (e == "pe" or not is_raw):
                        continue
                    b = best.get(d.eng)
                    if b is None or b.seq < d.seq:
                        best[d.eng] = d
                else:
                    out.append(d)
        for d in best.values():
            d.sig = True
            out.append(d)
        return out

    def _upd(self, tok, reads, writes, adds):
        for n in reads:
            self._r(n)["r"].append(tok)
        for n in writes:
            r = self._r(n)
            r["w"] = [tok]
            r["r"] = []
        for n in adds:
            self._r(n)["w"].append(tok)

    def op(self, e, fn, reads=(), writes=(), adds=()):
        deps = self._deps(e, reads, writes, adds)
        tok = Tok("eng", eng=e, seq=len(self.q[e]))
        self.q[e].append((fn, deps, tok, 1))
        self._upd(tok, reads, writes, adds)
        return tok

    def dma(self, e, out, in_, key, reads=(), writes=(), adds=()):
        deps = self._deps(e, reads, writes, adds)
        if key not in self.dsem:
            self.dsem[key] = self.sem("d_" + key)
            self.dcnt[key] = 0
        self.dcnt[key] += 16
        tok = Tok("dma", sem=self.dsem[key], val=self.dcnt[key])
        fn = lambda eng, o=out, i=in_: eng.dma_start(out=o, in_=i)
        self.q[e].append((fn, deps, tok, 16))
        self._upd(tok, reads, writes, adds)
        return tok

    def cc(self, ins, outs, key, reads=(), writes=()):
        writes = list(writes) + ["__CC__"]
        deps = self._deps("pool", reads, writes, ())
        if key not in self.dsem:
            self.dsem[key] = self.sem("c_" + key)
            self.dcnt[key] = 0
        self.dcnt[key] += 1
        tok = Tok("dma", sem=self.dsem[key], val=self.dcnt[key])
        fn = lambda eng, i=ins, o=outs: eng.collective_compute(
            "AllGather", ALU.bypass, replica_groups=[list(range(NCORES))], ins=[i], outs=[o])
        self.q["pool"].append((fn, deps, tok, 1))
        self._upd(tok, reads, writes, ())
        return tok

    def alias(self, src, dst):
        toks = []
        for n in src:
            r = self._r(n)
            toks += r["w"] + r["r"]
        for n in dst:
            self._r(n)["r"] += toks

    def wait_all(self, e, names):
        deps = self._deps(e, names, (), ())
        self.q[e].append((None, deps, None, 0))

    def finalize(self):
        for e in ENGS:
            n = 0
            for (fn, deps, tok, inc) in self.q[e]:
                if tok is not None and tok.kind == "eng" and tok.sig:
                    k = n // ROLL
                    while len(self.esems[e]) <= k:
                        self.esems[e].append(self.sem("e_%s%d" % (e, len(self.esems[e]))))
                    tok.sem = self.esems[e][k]
                    tok.val = n % ROLL + 1
                    n += 1

    def emit(self, e, eng):
        waited = {}
        for (fn, deps, tok, inc) in self.q[e]:
            need = {}
            for d in deps:
                k = id(d.sem)
                if waited.get(k, 0) >= d.val:
                    continue
                if k not in need or need[k][1] < d.val:
                    need[k] = (d.sem, d.val)
            for k, (s, v) in need.items():
                eng.wait_ge(s, v)
                waited[k] = v
            if fn is None:
                continue
            ins = fn(eng)
            if tok.kind == "dma":
                ins.then_inc(tok.sem, inc)
            elif tok.sig:
                ins.then_inc(tok.sem, 1)


def build_program():
    nc = bass.Bass("TRN2", target_bir_lowering=False)
    es = ExitStack()
    p = Prog(nc, es)

    def din(name, shape, dt=F32):
        return nc.dram_tensor(name, list(shape), dt, kind="ExternalInput").ap()

    def dscr(name, shape, dt):
        return nc.dram_tensor(name, list(shape), dt).ap()

    def sb(name, shape, dt):
        return es.enter_context(nc.sbuf_tensor(name, list(shape), dt))

    xT_d = din("xT", [128, 8 * NT])
    pos_d = din("pos", [1, NT], I32)
    par_d = din("par", [128, NPAR])
    bfb_d = din("bfb", [128, 2 * 256])
    msk_d = din("msk", [128, 32 * 128])
    cst_d = din("cst", [128, 3 * 128 + 16])
    w_in_d = [din("w_in%d" % l, [D, 3088]) for l in range(2)]
    fwo_d = [din("fwo%d" % l, [D, D]) for l in range(2)]
    wkva_d = din("wkva", [D, 288])
    wuk_d = din("wuk", [256, D])
    wuv_d = din("wuv", [256, D])
    wdq_d = [din("wdq%d" % j, [D, 768]) for j in range(2)]
    wuq_d = [din("wuq%d" % j, [768, 1536]) for j in range(2)]
    mwo_d = [din("mwo%d" % j, [D, D]) for j in range(2)]
    wg_d = [din("wg%d" % l, [D, DFF]) for l in range(4)]
    wu_d = [din("wu%d" % l, [D, DFF]) for l in range(4)]
    wd_d = [din("wd%d" % l, [DFF, D]) for l in range(4)]
    outT_d = nc.dram_tensor("outT", [128, 8 * NT], F32, kind="ExternalOutput").ap()

    QM = dscr("QM", [D, NT], BF16)
    QA = dscr("QA", [16 * 4, NT], BF16)
    QR = dscr("QR", [2 * 256, NT], BF16)
    XR = 2080
    XB = dscr("XB", [XR, NT], BF16)
    XBg = nc.dram_tensor("XBg", [NCORES * XR, NT], BF16, addr_space="Shared").ap()
    KTb = XB[0:1024, :]
    Vb = XB[1024:2048, :].rearrange("r (two c) -> (r two) c", two=2)
    LFb = XB[2048:2080, :].bitcast(F32).rearrange("r (x h) -> (r x) h", h=16)
    KXb = XB[2048:2080, :]

    def KTg_rows(r, h):
        return XBg[r * XR + h * 64: r * XR + (h + 1) * 64, :]

    def Vg_rows(r, h):
        v = XBg[r * XR + 1024: r * XR + 2048, :].rearrange("r (two c) -> (r two) c", two=2)
        return v[h * 128:(h + 1) * 128, :]

    def LFg_rank(r):
        return XBg[r * XR + 2048: r * XR + 2080, :].bitcast(F32).rearrange("r (x h) -> (r x) h", h=16)

    def KXg_rank(r):
        return XBg[r * XR + 2048: r * XR + 2080, :]

    CKd = dscr("CKd", [16 * 4, S], BF16)
    COSd = dscr("COSd", [128, NT], F32)
    SINd = dscr("SINd", [128, NT], F32)

    hT = sb("hT", [128, 8 * NT], F32)
    A = sb("A", [128, 8 * NT], BF16)
    B = sb("B", [128, 8 * NT], BF16)
    Wt = [sb("W%d" % i, [128, 4096], BF16) for i in range(3)]
    Vt = [sb("V%d" % i, [128, 16 * 128], BF16) for i in range(2)]
    MSK = sb("MSK", [128, 32 * 128], BF16)
    CST = sb("CST", [128, 3 * 128 + 16], BF16)
    PAR = sb("PAR", [128, NPAR], F32)
    BFB = sb("BFB", [128, 512], F32)
    ONES = sb("ONES", [128, 128], BF16)
    LFown = sb("LFown", [128, 256], F32)
    T32 = [sb("T32_%d" % i, [128, 512], F32) for i in range(5)]
    TB16 = [sb("TB16_%d" % i, [128, 512], BF16) for i in range(2)]
    STG = [B[:, 12288 + i * NT: 12288 + (i + 1) * NT] for i in range(2)]
    CS = [sb("CS%d" % i, [128, 512], F32) for i in range(2)]
    PS = [es.enter_context(nc.psum_tensor("ps%d" % i, [128, 512], F32)) for i in range(8)]

    IDN = CST[:, 0:128]
    UTR = CST[:, 128:256]
    MLT = CST[:, 256:384]
    MOWN = CST[:, 384:400]

    Kt = [A[:, i * NT:(i + 1) * NT] for i in range(2)]
    Qt = [A[:, (2 + i) * NT:(3 + i) * NT] for i in range(2)]
    Pt = [A[:, 4 * NT + i * 512: 4 * NT + (i + 1) * 512] for i in range(3)]
    ATT_RES = ["K0m", "K0x", "K1m", "K1x", "Q0", "Q1", "P0", "P1", "P2"]
    A_RES = ["A.%d" % t for t in range(4)]
    B_RES = ["B.%d" % t for t in range(4)]

    B32 = B[:, :].bitcast(F32)

    psi = [0]

    def next_ps():
        i = psi[0] % 8
        psi[0] += 1
        return "ps%d" % i, PS[i]

    wi = [0]

    def next_w():
        i = wi[0] % 3
        wi[0] += 1
        return "W%d" % i, Wt[i]

    def load_w(src_ap, rows_k, ncols, c0=0):
        name, t = next_w()
        view = t[:, 0:rows_k * ncols].rearrange("p (k c) -> p k c", k=rows_k)
        src = src_ap[:, c0:c0 + ncols].rearrange("(k p) c -> p k c", p=128)
        p.dma("pool", view, src, name, writes=[name])
        return name, t

    p.dma("sp", PAR[:, :], par_d, "i_par", writes=["PAR"])
    p.dma("sp", BFB[:, :], bfb_d, "i_bfb", writes=["BFB"])
    p.dma("pool", MSK[:, :], msk_d, "i_msk", writes=["MSK"])
    p.dma("pool", CST[:, :], cst_d, "i_cst", writes=["CST"])
    for kc in range(8):
        p.dma("sp", hT[:, kc * NT:(kc + 1) * NT], xT_d[:, kc * NT:(kc + 1) * NT], "i_h",
              adds=["hT.%d" % t for t in range(4)])
    p.op("pool", lambda e: e.memset(ONES[:, :], 1.0), writes=["ONES"])
    for i in range(2):
        p.op("pool", lambda e, i=i: e.memset(Vt[i][:, :], 1.0), writes=["V%d" % i])
    p.op("pool", lambda e: e.memset(A[0:16, 0:NT], 1.0), writes=["A.0"])
    CKv = CKd.rearrange("(h r) s -> h r s", r=4)
    QAv = QA.rearrange("(h r) s -> h r s", r=4)
    for q8 in range(8):
        p.dma("sp", CKv[:, 0, q8 * NT:(q8 + 1) * NT], A[0:16, 0:NT], "initw", reads=["A.0"], adds=["CKd"])
    for r in range(1, 4):
        p.dma("sp", QAv[:, r, :], A[0:16, 0:NT], "initw", reads=["A.0"], adds=["QA"])

    POSI = B[:, 0:2 * NT].bitcast(I32)
    ANG = B32[:, NT:2 * NT]
    ARG = B32[:, 2 * NT:3 * NT]
    TAB = B32[:, 3 * NT:4 * NT]
    p.dma("sp", POSI, pos_d.partition_broadcast(128), "i_pos", writes=["B.0"])
    p.op("dve", lambda e: e.tensor_copy(out=ANG, in_=POSI), reads=["B.0"], writes=["B.1"])
    p.op("dve", lambda e: e.tensor_scalar(out=ANG, in0=ANG, scalar1=PAR[:, PC_INVF:PC_INVF + 1], scalar2=None,
                                          op0=ALU.mult), reads=["PAR", "B.1"], writes=["B.1"])
    RR = B32[:, 0:NT]
    MAGIC = 12582912.0
    C1 = 6.28125
    C2 = 2.0 * math.pi - 6.28125
    PI_LO = 3.1415925
    for (dst, nm) in ((SINd, "SINd"), (COSd, "COSd")):
        if nm == "COSd":
            p.op("dve", lambda e: e.tensor_scalar(out=ANG, in0=ANG, scalar1=0.5 * math.pi, scalar2=None, op0=ALU.add),
                 reads=["B.1"], writes=["B.1"])
        p.op("dve", lambda e: e.tensor_scalar(out=ARG, in0=ANG, scalar1=1.0 / (2.0 * math.pi), scalar2=MAGIC,
                                              op0=ALU.mult, op1=ALU.add), reads=["B.1"], writes=["B.2"])
        p.op("dve", lambda e: e.tensor_scalar(out=ARG, in0=ARG, scalar1=-MAGIC, scalar2=None, op0=ALU.add),
             reads=["B.2"], writes=["B.2"])
        p.op("dve", lambda e: e.scalar_tensor_tensor(out=RR, in0=ARG, scalar=-C1, in1=ANG, op0=ALU.mult, op1=ALU.add),
             reads=["B.2", "B.1"], writes=["B.0"])
        p.op("dve", lambda e: e.scalar_tensor_tensor(out=RR, in0=ARG, scalar=-C2, in1=RR, op0=ALU.mult, op1=ALU.add),
             reads=["B.2", "B.0"], writes=["B.0"])
        p.op("dve", lambda e: e.tensor_scalar(out=RR, in0=RR, scalar1=-PI_LO, scalar2=PI_LO, op0=ALU.max, op1=ALU.min),
             reads=["B.0"], writes=["B.0"])
        p.op("act", lambda e: e.activation(out=TAB, in_=RR, func=AF.Sin), reads=["B.0"], writes=["B.3"])
        p.dma("sp", dst, TAB, "i_tab", reads=["B.3"], writes=[nm])

    def mm(out_ap, psname, pairs, reads, first=True, last=True):
        n = len(pairs)
        for i, (l, r) in enumerate(pairs):
            st = first and i == 0
            sp_ = last and i == n - 1
            if st:
                p.op("pe", lambda e, l=l, r=r, st=st, sp_=sp_: e.matmul(out_ap, lhsT=l, rhs=r, start=st, stop=sp_),
                     reads=reads, writes=[psname])
            else:
                p.op("pe", lambda e, l=l, r=r, st=st, sp_=sp_: e.matmul(out_ap, lhsT=l, rhs=r, start=st, stop=sp_),
                     reads=reads, adds=[psname])

    def rms_rstd(chunks, chunk_res, N, qs, tg_tag):
        psn, ps = next_ps()
        n = len(chunks)
        for i, c in enumerate(chunks):
            sq = TB16[i % 2]
            sqn = "TB16_%d" % (i % 2)
            p.op("act", lambda e, c=c, sq=sq: e.activation(out=sq[:, :], in_=c, func=AF.Square),
                 reads=chunk_res, writes=[sqn])
            st = (i == 0)
            sp_ = (i == n - 1)
            if st:
                p.op("pe", lambda e, sq=sq, st=st, sp_=sp_: e.matmul(ps[:, :], lhsT=ONES[:, :], rhs=sq[:, :], start=st, stop=sp_),
                     reads=[sqn, "ONES"], writes=[psn])
            else:
                p.op("pe", lambda e, sq=sq, st=st, sp_=sp_: e.matmul(ps[:, :], lhsT=ONES[:, :], rhs=sq[:, :], start=st, stop=sp_),
                     reads=[sqn, "ONES"], adds=[psn])
        rs = T32[4]
        p.op("act", lambda e: e.activation(out=rs[:, :], in_=ps[:, :], func=AF.Sqrt, scale=1.0 / (N * qs * qs),
                                           bias=EPS / (qs * qs)), reads=[psn], writes=["T32_4"])
        p.op("dve", lambda e: e.reciprocal(out=rs[:, :], in_=rs[:, :]), reads=["T32_4"], writes=["T32_4"])
        return rs, "T32_4"

    def norm_to_A(gcol):
        for tg in range(4):
            chunks = [hT[:, kc * NT + tg * 512: kc * NT + (tg + 1) * 512] for kc in range(8)]
            rs, rsn = rms_rstd(chunks, ["hT.%d" % tg], float(D), 1.0, tg)
            for kc in range(8):
                p.op("dve", lambda e, kc=kc, tg=tg, c=chunks[kc]: e.scalar_tensor_tensor(
                    out=A[:, kc * NT + tg * 512: kc * NT + (tg + 1) * 512], in0=c,
                    scalar=PAR[:, gcol + kc:gcol + kc + 1], in1=rs[:, :], op0=ALU.mult, op1=ALU.mult),
                    reads=["hT.%d" % tg, rsn, "PAR"], **({"writes": ["A.%d" % tg]} if kc == 0 else {"adds": ["A.%d" % tg]}))

    def xn(kc, t0, n):
        return A[:, kc * NT + t0: kc * NT + t0 + n]

    def proj_fm(w_src, ncol_total, c0, nchunks, dst_fn, scale=None, kin=8, rhs_fn=None, rhs_res=None):
        done = 0
        while done < nchunks:
            nb = min(4, nchunks - done)
            wn, wt = load_w(w_src, kin, nb * 128, c0 + done * 128)
            for j in range(nb):
                for tg in range(4):
                    psn, ps = next_ps()
                    pairs = []
                    for kc in range(kin):
                        l = wt[:, kc * nb * 128 + j * 128: kc * nb * 128 + (j + 1) * 128]
                        r = rhs_fn(kc, tg) if rhs_fn else xn(kc, tg * 512, 512)
                        pairs.append((l, r))
                    mm(ps[:, :], psn, pairs, [wn] + (rhs_res(tg) if rhs_res else ["A.%d" % tg]))
                    dst_fn(done + j, tg, ps, psn)
            done += nb

    def evac_stage_store(dst_dram_rows, scale=None):
        def fn(ci, tg, ps, psn, dst=dst_dram_rows):
            s = STG[ci % 2]
            sn = "STG%d" % (ci % 2)
            kw = {"writes": [sn]} if tg == 0 else {"adds": [sn]}
            if scale is None:
                p.op("dve", lambda e: e.tensor_copy(out=s[:, tg * 512:(tg + 1) * 512], in_=ps[:, :]), reads=[psn], **kw)
            else:
                p.op("dve", lambda e: e.tensor_scalar(out=s[:, tg * 512:(tg + 1) * 512], in0=ps[:, :], scalar1=scale,
                                                      scalar2=None, op0=ALU.mult), reads=[psn], **kw)
            if tg == 3:
                d_ap, d_res = dst(ci)
                p.dma("sp", d_ap, s[:, :], "st_%s_%d" % (d_res, ci % 2), reads=[sn], adds=[d_res])
        return fn

    def v_proj(lhs_fn, lhs_res, w_src, c0, kin):
        Vb4 = Vb.rearrange("(h p) (t d) -> p h t d", p=128, d=64)
        for half in range(2):
            wn, wt = load_w(w_src, kin, 512, c0 + half * 512)
            for tb in range(16):
                psn, ps = next_ps()
                pairs = [(lhs_fn(kc, tb), wt[:, kc * 512:(kc + 1) * 512]) for kc in range(kin)]
                mm(ps[:, :], psn, pairs, [wn] + lhs_res(tb))
                s = TB16[tb % 2]
                sn = "TB16_%d" % (tb % 2)
                p.op("dve", lambda e, s=s, ps=ps: e.tensor_copy(out=s[:, :], in_=ps[:, :]), reads=[psn], writes=[sn])
                p.dma("sp", Vb4[:, half * 8:(half + 1) * 8, tb, :], s[:, :].rearrange("p (h d) -> p h d", d=64),
                      "st_Vb_%d" % (tb % 2), reads=[sn], adds=["Vb"])

    def split3(src, srcres, hi, mid, lo, tmp, tmpres, outres, first):
        kw = (lambda: {"writes": [outres]}) if first else (lambda: {"adds": [outres]})
        p.op("dve", lambda e: e.tensor_copy(out=hi, in_=src), reads=srcres, **kw())
        p.op("dve", lambda e: e.tensor_tensor(out=tmp, in0=src, in1=hi, op=ALU.subtract), reads=srcres + [outres], writes=[tmpres])
        p.op("dve", lambda e: e.tensor_copy(out=mid, in_=tmp), reads=[tmpres], adds=[outres])
        p.op("dve", lambda e: e.tensor_tensor(out=tmp, in0=tmp, in1=mid, op=ALU.subtract), reads=[tmpres, outres], writes=[tmpres])
        p.op("dve", lambda e: e.tensor_copy(out=lo, in_=tmp), reads=[tmpres], adds=[outres])

    def rope_apply(x1ps, x1n, x2ps, x2n, np_, tg, o_tile, o_res):
        cosn, sinn = "CS0", "CS1"
        t = [T32[i][0:np_, :] for i in range(4)]
        tn = ["T32_%d" % i for i in range(4)]
        p.op("dve", lambda e: e.tensor_tensor(out=t[0], in0=x1ps[0:np_, :], in1=CS[0][0:np_, :], op=ALU.mult), reads=[x1n, cosn], writes=[tn[0]])
        p.op("dve", lambda e: e.tensor_tensor(out=t[1], in0=x2ps[0:np_, :], in1=CS[1][0:np_, :], op=ALU.mult), reads=[x2n, sinn], writes=[tn[1]])
        p.op("dve", lambda e: e.tensor_tensor(out=t[2], in0=x1ps[0:np_, :], in1=CS[1][0:np_, :], op=ALU.mult), reads=[x1n, sinn], writes=[tn[2]])
        p.op("dve", lambda e: e.tensor_tensor(out=t[3], in0=x2ps[0:np_, :], in1=CS[0][0:np_, :], op=ALU.mult), reads=[x2n, cosn], writes=[tn[3]])
        p.op("dve", lambda e: e.tensor_tensor(out=o_tile[0:np_, 0:512], in0=t[0], in1=t[1], op=ALU.subtract), reads=[tn[0], tn[1]], writes=[o_res])
        p.op("dve", lambda e: e.tensor_tensor(out=o_tile[0:np_, 512:1024], in0=t[2], in1=t[3], op=ALU.add), reads=[tn[2], tn[3]], adds=[o_res])

    def load_cs(tg):
        p.dma("sp", CS[0][:, :], COSd[:, tg * 512:(tg + 1) * 512], "CS0", reads=["COSd"], writes=["CS0"])
        p.dma("sp", CS[1][:, :], SINd[:, tg * 512:(tg + 1) * 512], "CS1", reads=["SINd"], writes=["CS1"])

    def attention(l, fox):
        kx = 4 if fox else 32
        nrow = 64 + kx
        p.alias(A_RES, ATT_RES)
        ACC = [(3 + g, "ps%d" % (3 + g)) for g in range(4)]
        tiles = []
        kvi = 0
        for h in range(16):
            for r in range(8):
                sl = kvi % 2
                kvi += 1
                for g in range(4):
                    for lb in range(4 * g + 4):
                        m = lb // 2
                        kp = lb % 2
                        if m < 2 * g:
                            c0, c1, msk = 0, 512, []
                        elif m == 2 * g:
                            c0, c1, msk = 0, 512, [(0, 0), (1, 128)]
                        else:
                            c0, c1, msk = 256, 512, [(0, 256), (1, 384)]
                        msk = [(qp, cc) for (qp, cc) in msk if not (r == 0 and kp == 0 and qp == 1)]
                        tiles.append(dict(h=h, r=r, sl=sl, g=g, lb=lb, kp=kp, c0=c0, c1=c1, msk=msk,
                                          first=(r == 0 and lb == 0), last=(r == 7 and lb == 4 * g + 3),
                                          newq=(r == 0 and g == 0 and lb == 0), newkv=(g == 0 and lb == 0),
                                          endhead=(r == 7 and g == 3 and lb == 15)))
        for i, t in enumerate(tiles):
            t["si"] = i % 3

        def emit_loads(t):
            h, r, sl = t["h"], t["r"], t["sl"]
            if t["newq"]:
                qn = "Q%d" % (h % 2)
                qt = Qt[h % 2]
                p.dma("sp", qt[0:64, :], QM[h * 64:(h + 1) * 64, :], qn, reads=["QM"], writes=[qn])
                if fox:
                    p.dma("sp", qt[64:68, :], QA[h * 4:(h + 1) * 4, :], qn, reads=["QA"], adds=[qn])
                else:
                    p.dma("sp", qt[64:80, :], QR[h * 16:(h + 1) * 16, :], qn, reads=["QR"], adds=[qn])
                    p.dma("sp", qt[80:96, :], QR[256 + h * 16:256 + (h + 1) * 16, :], qn, reads=["QR"], adds=[qn])
            if t["newkv"]:
                kt = Kt[sl]
                vt = Vt[sl]
                kmn, kxn, vn = "K%dm" % sl, "K%dx" % sl, "V%d" % sl
                p.dma("sp", kt[0:64, :], KTg_rows(r, h), kmn, reads=["XBg"], writes=[kmn])
                if fox:
                    p.dma("sp", kt[64:68, :], CKd[h * 4:(h + 1) * 4, r * NT:(r + 1) * NT], kxn, reads=["CKd"], writes=[kxn])
                else:
                    p.dma("sp", kt[64:96, :], KXg_rank(r), kxn, reads=["XBg"], writes=[kxn])
                p.dma("sp", vt[:, :].rearrange("p (l c) -> p l c", l=16)[:, :, 0:64],
                      Vg_rows(r, h).rearrange("p (l c) -> p l c", l=16),
                      vn, reads=["XBg"], writes=[vn])

        def emit_qk(t):
            emit_loads(t)
            h, r, sl, g, lb, kp, c0, c1, msk, si = (t[k] for k in ("h", "r", "sl", "g", "lb", "kp", "c0", "c1", "msk", "si"))
            kt = Kt[sl]
            qt = Qt[h % 2]
            kmn, kxn, qn = "K%dm" % sl, "K%dx" % sl, "Q%d" % (h % 2)
            sps = PS[si]
            spn = "ps%d" % si
            lq = kt[0:nrow, lb * 128:(lb + 1) * 128]
            rq = qt[0:nrow, g * 512 + c0: g * 512 + c1]
            nm = len(msk)
            p.op("pe", lambda e, sps=sps, lq=lq, rq=rq, c0=c0, c1=c1, nm=nm: e.matmul(
                sps[:, c0:c1], lhsT=lq, rhs=rq, start=True, stop=(nm == 0), skip_group_check=True),
                reads=[kmn, kxn, qn], writes=[spn])
            for mi, (qp, cc) in enumerate(msk):
                mo = ((qp * 8 + r) * 2 + kp) * 128
                p.op("pe", lambda e, sps=sps, cc=cc, mo=mo, mi=mi, nm=nm: e.matmul(
                    sps[:, cc:cc + 128], lhsT=IDN, rhs=MSK[:, mo:mo + 128], start=False, stop=(mi == nm - 1),
                    skip_group_check=True), reads=["CST", "MSK"], adds=[spn])

        def emit_exp(t):
            si, c0, c1 = t["si"], t["c0"], t["c1"]
            sps, pt = PS[si], Pt[si]
            p.op("act", lambda e, pt=pt, sps=sps, c0=c0, c1=c1: e.activation(
                out=pt[:, c0:c1], in_=sps[:, c0:c1], func=AF.Exp), reads=["ps%d" % si], writes=["P%d" % si])

        def emit_pv(t):
            h, sl, g, lb, c0, c1, si = (t[k] for k in ("h", "sl", "g", "lb", "c0", "c1", "si"))
            vt, pt = Vt[sl], Pt[si]
            accps, accn = PS[ACC[g][0]], ACC[g][1]
            first, last = t["first"], t["last"]
            kw = {"writes": [accn]} if first else {"adds": [accn]}
            p.op("pe", lambda e, accps=accps, vt=vt, lb=lb, pt=pt, c0=c0, c1=c1, first=first, last=last: e.matmul(
                accps[:, c0:c1], lhsT=vt[:, lb * 128:(lb + 1) * 128], rhs=pt[:, c0:c1], start=first, stop=last,
                skip_group_check=True), reads=["V%d" % sl, "P%d" % si], **kw)

        def emit_norm(h):
            for g in range(4):
                accps = PS[ACC[g][0]]
                accn = ACC[g][1]
                rc = T32[g % 2]
                rcn = "T32_%d" % (g % 2)
                p.op("dve", lambda e, rc=rc, accps=accps: e.reciprocal(out=rc[0:64, :], in_=accps[64:128, :]),
                     reads=[accn], writes=[rcn])
                po = (h % 2) * 64
                o_ap = B[po:po + 64, (h // 2) * NT + g * 512:(h // 2) * NT + (g + 1) * 512]
                p.op("dve", lambda e, rc=rc, accps=accps, o_ap=o_ap: e.tensor_tensor(out=o_ap, in0=accps[0:64, :], in1=rc[0:64, :],
                                                                                     op=ALU.mult),
                     reads=[accn, rcn], adds=["B.%d" % g])

        n = len(tiles)
        LA = 2
        for j in range(min(LA, n)):
            emit_qk(tiles[j])
        for i, t in enumerate(tiles):
            if i + LA < n:
                emit_qk(tiles[i + LA])
            emit_exp(t)
            emit_pv(t)
            if t["endhead"]:
                emit_norm(t["h"])
        p.alias(ATT_RES, A_RES)

    def wo_proj(w_src):
        for half in range(2):
            wn, wt = load_w(w_src, 8, 512, half * 512)
            for j in range(4):
                dmc = half * 4 + j
                for tg in range(4):
                    psn, ps = next_ps()
                    pairs = [(wt[:, kc * 512 + j * 128: kc * 512 + (j + 1) * 128],
                              B[:, kc * NT + tg * 512: kc * NT + (tg + 1) * 512]) for kc in range(8)]
                    mm(ps[:, :], psn, pairs, [wn, "B.%d" % tg])
                    hs = hT[:, dmc * NT + tg * 512: dmc * NT + (tg + 1) * 512]
                    p.op("dve", lambda e, hs=hs, ps=ps: e.tensor_tensor(out=hs, in0=hs, in1=ps[:, :], op=ALU.add),
                         reads=[psn], adds=["hT.%d" % tg])

    def ffn(l):
        gcol = PC_FFN + l * 8
        p.alias(A_RES, ["FX.0", "FX.1", "ACTT"])
        p.alias(B_RES, ["ACTT"])

        def aslot(fc, sub):
            if fc < 16:
                return B[:, fc * 1024 + sub * 512: fc * 1024 + (sub + 1) * 512]
            return A[:, 8192 + (fc - 16) * 1024 + sub * 512: 8192 + (fc - 16) * 1024 + (sub + 1) * 512]

        def fx(kc, sub):
            return A[:, kc * 1024 + sub * 512: kc * 1024 + (sub + 1) * 512]

        for pr in range(2):
            for sub in range(2):
                tg = pr * 2 + sub
                chunks = [hT[:, kc * NT + tg * 512: kc * NT + (tg + 1) * 512] for kc in range(8)]
                rs, rsn = rms_rstd(chunks, ["hT.%d" % tg], float(D), 1.0, tg)
                for kc in range(8):
                    p.op("dve", lambda e, kc=kc, sub=sub, c=chunks[kc], rs=rs: e.scalar_tensor_tensor(
                        out=fx(kc, sub), in0=c, scalar=PAR[:, gcol + kc:gcol + kc + 1], in1=rs[:, :], op0=ALU.mult, op1=ALU.mult),
                        reads=["hT.%d" % tg, rsn, "PAR"], **({"writes": ["FX.%d" % sub]} if kc == 0 else {"adds": ["FX.%d" % sub]}))
            for k in range(6):
                nb = 4 if k < 5 else 2
                gn, gt = load_w(wg_d[l], 8, nb * 128, k * 512)
                un, ut = load_w(wu_d[l], 8, nb * 128, k * 512)
                for j in range(nb):
                    fc = k * 4 + j
                    for sub in range(2):
                        pgn, pg = next_ps()
                        mm(pg[:, :], pgn, [(gt[:, kc * nb * 128 + j * 128: kc * nb * 128 + (j + 1) * 128], fx(kc, sub))
                                           for kc in range(8)], [gn, "FX.%d" % sub])
                        pun, pu = next_ps()
                        mm(pu[:, :], pun, [(ut[:, kc * nb * 128 + j * 128: kc * nb * 128 + (j + 1) * 128], fx(kc, sub))
                                           for kc in range(8)], [un, "FX.%d" % sub])
                        sg = T32[(fc * 2 + sub) % 2]
                        sgn = "T32_%d" % ((fc * 2 + sub) % 2)
                        p.op("act", lambda e, sg=sg, pg=pg: e.activation(out=sg[:, :], in_=pg[:, :], func=AF.Silu),
                             reads=[pgn], writes=[sgn])
                        kw = {"writes": ["ACTT"]} if (fc == 0 and sub == 0) else {"adds": ["ACTT"]}
                        p.op("dve", lambda e, sg=sg, pu=pu, fc=fc, sub=sub: e.tensor_tensor(out=aslot(fc, sub), in0=sg[:, :],
                                                                                          in1=pu[:, :], op=ALU.mult),
                             reads=[sgn, pun], **kw)
            for dmc in range(8):
                name, t = next_w()
                view = t[:, 0:NFC * 128].rearrange("p (k c) -> p k c", k=NFC)
                src = wd_d[l][:, dmc * 128:(dmc + 1) * 128].rearrange("(k p) c -> p k c", p=128)
                p.dma("pool", view, src, name, writes=[name])
                for sub in range(2):
                    tg = pr * 2 + sub
                    psn, ps = next_ps()
                    mm(ps[:, :], psn, [(t[:, fc * 128:(fc + 1) * 128], aslot(fc, sub)) for fc in range(NFC)], [name, "ACTT"])
                    hs = hT[:, dmc * NT + tg * 512: dmc * NT + (tg + 1) * 512]
                    p.op("dve", lambda e, hs=hs, ps=ps: e.tensor_tensor(out=hs, in0=hs, in1=ps[:, :], op=ALU.add),
                         reads=[psn], adds=["hT.%d" % tg])
        p.alias(["FX.0", "FX.1", "ACTT"], A_RES)
        p.alias(["ACTT"], B_RES)

    def fox_proj(l):
        norm_to_A(PC_ATTN + l * 8)
        p.alias(B_RES, ["STG0", "STG1"])
        w = w_in_d[l]
        proj_fm(w, 3088, 1024, 8, evac_stage_store(lambda ci: (KTb[ci * 128:(ci + 1) * 128, :], "KTb")))
        v_proj(lambda kc, tb: xn(kc, tb * 128, 128), lambda tb: ["A.%d" % (tb // 4)], w, 2048, 8)
        wn, wt = load_w(w, 8, 16, 3072)
        psn, ps = next_ps()
        for tb in range(16):
            for kc in range(8):
                kw = {"writes": [psn]} if (tb == 0 and kc == 0) else {"adds": [psn]}
                p.op("pe", lambda e, tb=tb, kc=kc: e.matmul(ps[:, tb * 16:(tb + 1) * 16], lhsT=xn(kc, tb * 128, 128),
                                                            rhs=wt[:, kc * 16:(kc + 1) * 16], start=(kc == 0), stop=(kc == 7),
                                                            skip_group_check=True),
                     reads=[wn, "A.%d" % (tb // 4)], **kw)
        z = T32[3]
        p.op("dve", lambda e: e.tensor_tensor(out=z[:, 0:256], in0=ps[:, 0:256], in1=BFB[:, l * 256:(l + 1) * 256], op=ALU.add),
             reads=[psn, "BFB"], writes=["T32_3"])
        p.op("act", lambda e: e.activation(out=z[:, 0:256], in_=z[:, 0:256], func=AF.Exp, scale=-1.0),
             reads=["T32_3"], writes=["T32_3"])
        p.op("act", lambda e: e.activation(out=LFown[:, :], in_=z[:, 0:256], func=AF.Ln, bias=1.0, scale=1.0),
             reads=["T32_3"], writes=["LFown"])
        p.dma("sp", LFb.rearrange("(t p) h -> p t h", p=128), LFown[:, :].rearrange("p (t h) -> p t h", h=16),
              "st_LFb", reads=["LFown"], writes=["LFb"])
        p.cc(XB, XBg, "XBg", reads=["KTb", "Vb", "LFb"], writes=["XBg"])
        proj_fm(w, 3088, 0, 8, evac_stage_store(lambda ci: (QM[ci * 128:(ci + 1) * 128, :], "QM"), scale=0.125))
        p.alias(["STG0", "STG1"], B_RES)

    def fox_cumsum():
        CUMN = ["CUM", "CUMh", "CUMt", "CUMo", "CUMtot", "CUMth", "CUMoff", "CUMoffo", "CUMdh"]
        p.alias(B_RES, CUMN)
        LFall = B32[:, 0:2048]
        HML = [B[:, 4096 + i * 2048: 4096 + (i + 1) * 2048] for i in range(3)]
        TMP = B32[:, 5120:6144]
        DHall = B[0:16, 12288:12288 + 1536]
        DH = [B[0:16, 12288 + i * 512: 12288 + (i + 1) * 512] for i in range(3)]
        OHML = [B[:, 14336 + i * 256: 14336 + (i + 1) * 256] for i in range(3)]
        TOTS = B32[:, 7552:7568]
        THML = [B[:, 15136 + i * 16: 15136 + (i + 1) * 16] for i in range(3)]
        OFFS = B32[0:16, 7600:7728]
        OFFO = B32[0:16, 7728:7744]
        DT = T32[0][0:16, :]
        DTMP = T32[1][0:16, :]
        for r in range(8):
            kw = {"writes": ["CUM"]} if r == 0 else {"adds": ["CUM"]}
            p.dma("sp", LFall[:, r * 256:(r + 1) * 256].rearrange("p (b h) -> p b h", h=16),
                  LFg_rank(r).rearrange("(b p) h -> p b h", p=128), "CUMld", reads=["XBg"], **kw)
        for hf in range(2):
            sl = slice(hf * 1024, (hf + 1) * 1024)
            split3(LFall[:, sl], ["CUM"], HML[0][:, sl], HML[1][:, sl], HML[2][:, sl], TMP, "CUMt", "CUMh", hf == 0)
        split3(LFown[:, :], ["LFown"], OHML[0], OHML[1], OHML[2], TMP[:, 0:256], "CUMt", "CUMo", True)
        psn, ps = next_ps()
        first = True
        for h in range(16):
            for i in range(3):
                kw = {"writes": [psn]} if first else {"adds": [psn]}
                first = False
                l_ap = HML[i].rearrange("p (b h) -> p b h", h=16)[:, :, h]
                p.op("pe", lambda e, l_ap=l_ap, h=h, i=i, ps=ps: e.matmul(ps[:, h:h + 1], lhsT=l_ap, rhs=ONES[:, 0:1], start=(i == 0),
                                                                  stop=(i == 2), skip_group_check=True),
                     reads=["CUMh", "ONES"], **kw)
        p.op("dve", lambda e, ps=ps: e.tensor_copy(out=TOTS, in_=ps[:, 0:16]), reads=[psn], writes=["CUMtot"])
        split3(TOTS, ["CUMtot"], THML[0], THML[1], THML[2], TMP[:, 0:16], "CUMt", "CUMth", True)
        psn2, ps2 = next_ps()
        mm(ps2[0:16, 0:128], psn2, [(THML[i], MLT) for i in range(3)], ["CUMth", "CST"])
        p.op("dve", lambda e: e.tensor_copy(out=OFFS, in_=ps2[0:16, 0:128]), reads=[psn2], writes=["CUMoff"])
        psn3, ps3 = next_ps()
        mm(ps3[0:16, 0:16], psn3, [(THML[i], MOWN) for i in range(3)], ["CUMth", "CST"])
        p.op("dve", lambda e: e.tensor_copy(out=OFFO, in_=ps3[0:16, 0:16]), reads=[psn3], writes=["CUMoffo"])
        for ch in range(32):
            psn, ps = next_ps()
            for j in range(4):
                b = ch * 4 + j
                for i in range(3):
                    kw = {"writes": [psn]} if (j == 0 and i == 0) else {"adds": [psn]}
                    p.op("pe", lambda e, ps=ps, j=j, b=b, i=i: e.matmul(ps[0:16, j * 128:(j + 1) * 128],
                                                                        lhsT=HML[i][:, b * 16:(b + 1) * 16], rhs=UTR,
                                                                        start=(i == 0), stop=(i == 2), skip_group_check=True),
                         reads=["CUMh", "CST"], **kw)
            for j in range(4):
                b = ch * 4 + j
                kw = {"writes": ["T32_0"]} if j == 0 else {"adds": ["T32_0"]}
                p.op("dve", lambda e, ps=ps, j=j, b=b: e.tensor_scalar(out=DT[:, j * 128:(j + 1) * 128],
                                                                      in0=ps[0:16, j * 128:(j + 1) * 128],
                                                                      scalar1=OFFS[:, b:b + 1], scalar2=None, op0=ALU.add),
                     reads=[psn, "CUMoff"], **kw)
            split3(DT, ["T32_0"], DH[0], DH[1], DH[2], DTMP, "T32_1", "CUMdh", True)
            p.dma("sp", CKv[:, 1:4, ch * 512:(ch + 1) * 512], DHall.rearrange("p (r t) -> p r t", r=3), "st_CKd",
                  reads=["CUMdh"], adds=["CKd"])
        for ch in range(4):
            psn, ps = next_ps()
            for j in range(4):
                b = ch * 4 + j
                for i in range(3):
                    kw = {"writes": [psn]} if (j == 0 and i == 0) else {"adds": [psn]}
                    p.op("pe", lambda e, ps=ps, j=j, b=b, i=i: e.matmul(ps[0:16, j * 128:(j + 1) * 128],
                                                                        lhsT=OHML[i][:, b * 16:(b + 1) * 16], rhs=UTR,
                                                                        start=(i == 0), stop=(i == 2), skip_group_check=True),
                         reads=["CUMo", "CST"], **kw)
            for j in range(4):
                b = ch * 4 + j
                kw = {"writes": ["T32_0"]} if j == 0 else {"adds": ["T32_0"]}
                p.op("dve", lambda e, ps=ps, j=j, b=b: e.tensor_scalar(out=DT[:, j * 128:(j + 1) * 128],
                                                                      in0=ps[0:16, j * 128:(j + 1) * 128],
                                                                      scalar1=OFFO[:, b:b + 1], scalar2=-1.0, op0=ALU.add,
                                                                      op1=ALU.mult),
                     reads=[psn, "CUMoffo"], **kw)
            p.op("dve", lambda e: e.tensor_copy(out=DH[0], in_=DT), reads=["T32_0"], writes=["CUMdh"])
            p.dma("sp", QAv[:, 0, ch * 512:(ch + 1) * 512], DH[0], "st_QA", reads=["CUMdh"], adds=["QA"])
        p.alias(CUMN, B_RES)

    def mla_kv():
        norm_to_A(PC_KV)
        p.alias(B_RES, ["CKV", "AT", "STG0", "STG1"] + ["CKV.%d" % t for t in range(4)])
        CKV = B[:, 0:2 * NT]
        AT = [B32[:, 4096 + i * 512: 4096 + (i + 1) * 512] for i in range(2)]
        wn, wt = load_w(wkva_d, 8, 288, 0)
        for tg in range(4):
            load_cs(tg)
            for cc in range(2):
                psn, ps = next_ps()
                mm(ps[:, :], psn, [(wt[:, kc * 288 + cc * 128: kc * 288 + (cc + 1) * 128], xn(kc, tg * 512, 512)) for kc in range(8)],
                   [wn, "A.%d" % tg])
                kw = {"writes": ["AT"]} if cc == 0 else {"adds": ["AT"]}
                p.op("dve", lambda e, cc=cc, ps=ps: e.tensor_copy(out=AT[cc], in_=ps[:, :]), reads=[psn], **kw)
            rs, rsn = rms_rstd(AT, ["AT"], 256.0, 1.0, tg)
            for cc in range(2):
                kw = {"writes": ["CKV.%d" % tg]} if cc == 0 else {"adds": ["CKV.%d" % tg]}
                p.op("dve", lambda e, cc=cc, tg=tg: e.scalar_tensor_tensor(
                    out=CKV[:, cc * NT + tg * 512: cc * NT + (tg + 1) * 512], in0=AT[cc],
                    scalar=PAR[:, PC_CKV + cc:PC_CKV + cc + 1], in1=rs[:, :], op0=ALU.mult, op1=ALU.mult),
                    reads=["AT", rsn, "PAR"], **kw)
            p1n, p1 = next_ps()
            mm(p1[0:16, :], p1n, [(wt[:, kc * 288 + 256: kc * 288 + 272], xn(kc, tg * 512, 512)) for kc in range(8)], [wn, "A.%d" % tg])
            p2n, p2 = next_ps()
            mm(p2[0:16, :], p2n, [(wt[:, kc * 288 + 272: kc * 288 + 288], xn(kc, tg * 512, 512)) for kc in range(8)], [wn, "A.%d" % tg])
            ro = STG[tg % 2]
            ron = "STG%d" % (tg % 2)
            rope_apply(p1, p1n, p2, p2n, 16, tg, ro, ron)
            p.dma("sp", KXb[0:16, tg * 512:(tg + 1) * 512], ro[0:16, 0:512], "st_KXb_%d" % (tg % 2), reads=[ron], adds=["KXb"])
            p.dma("sp", KXb[16:32, tg * 512:(tg + 1) * 512], ro[0:16, 512:1024], "st_KXb_%d" % (tg % 2), reads=[ron], adds=["KXb"])
        ckv_res = lambda tg: ["CKV.%d" % tg]
        proj_fm(wuk_d, 1024, 0, 8, evac_stage_store(lambda ci: (KTb[ci * 128:(ci + 1) * 128, :], "KTb")), kin=2,
                rhs_fn=lambda kc, tg: CKV[:, kc * NT + tg * 512: kc * NT + (tg + 1) * 512], rhs_res=ckv_res)
        v_proj(lambda kc, tb: CKV[:, kc * NT + tb * 128: kc * NT + (tb + 1) * 128], lambda tb: ["CKV.%d" % (tb // 4)], wuv_d, 0, 2)
        p.cc(XB, XBg, "XBg", reads=["KTb", "Vb", "KXb"], writes=["XBg"])
        p.alias(["CKV", "AT", "STG0", "STG1"] + ["CKV.%d" % t for t in range(4)], B_RES)

    def mla_q(j, l):
        norm_to_A(PC_ATTN + l * 8)
        p.alias(B_RES, ["CQ", "CQN", "STG0", "STG1"])
        CQ = [B32[:, i * 512:(i + 1) * 512] for i in range(6)]
        CQN = B[:, 8192:8192 + 6 * 512]
        qs = 1.0 / math.sqrt(96.0)
        for tg in range(4):
            load_cs(tg)
            for half in range(2):
                nb = 4 if half == 0 else 2
                wn, wt = load_w(wdq_d[j], 8, nb * 128, half * 512)
                for jj in range(nb):
                    qc = half * 4 + jj
                    psn, ps = next_ps()
                    mm(ps[:, :], psn, [(wt[:, kc * nb * 128 + jj * 128: kc * nb * 128 + (jj + 1) * 128], xn(kc, tg * 512, 512))
                                       for kc in range(8)], [wn, "A.%d" % tg])
                    kw = {"writes": ["CQ"]} if qc == 0 else {"adds": ["CQ"]}
                    p.op("dve", lambda e, qc=qc, ps=ps: e.tensor_copy(out=CQ[qc], in_=ps[:, :]), reads=[psn], **kw)
            rs, rsn = rms_rstd(CQ, ["CQ"], 768.0, qs, tg)
            for qc in range(6):
                kw = {"writes": ["CQN"]} if qc == 0 else {"adds": ["CQN"]}
                p.op("dve", lambda e, qc=qc: e.scalar_tensor_tensor(
                    out=CQN[:, qc * 512:(qc + 1) * 512], in0=CQ[qc], scalar=PAR[:, PC_CQ + j * 6 + qc:PC_CQ + j * 6 + qc + 1],
                    in1=rs[:, :], op0=ALU.mult, op1=ALU.mult), reads=["CQ", rsn, "PAR"], **kw)
            for half in range(2):
                wn, wt = load_w(wuq_d[j], 6, 512, half * 512)
                for jj in range(4):
                    hp = half * 4 + jj
                    psn, ps = next_ps()
                    mm(ps[:, :], psn, [(wt[:, kc * 512 + jj * 128: kc * 512 + (jj + 1) * 128], CQN[:, kc * 512:(kc + 1) * 512])
                                       for kc in range(6)], [wn, "CQN"])
                    s = TB16[hp % 2]
                    sn = "TB16_%d" % (hp % 2)
                    p.op("dve", lambda e, s=s, ps=ps: e.tensor_copy(out=s[:, :], in_=ps[:, :]), reads=[psn], writes=[sn])
                    p.dma("sp", QM[hp * 128:(hp + 1) * 128, tg * 512:(tg + 1) * 512], s[:, :], "st_QMb_%d" % (hp % 2), reads=[sn], adds=["QM"])
            wn, wt = load_w(wuq_d[j], 6, 512, 1024)
            for hh in range(2):
                p1n, p1 = next_ps()
                mm(p1[:, :], p1n, [(wt[:, kc * 512 + hh * 128: kc * 512 + (hh + 1) * 128], CQN[:, kc * 512:(kc + 1) * 512])
                                   for kc in range(6)], [wn, "CQN"])
                p2n, p2 = next_ps()
                mm(p2[:, :], p2n, [(wt[:, kc * 512 + 256 + hh * 128: kc * 512 + 256 + (hh + 1) * 128], CQN[:, kc * 512:(kc + 1) * 512])
                                   for kc in range(6)], [wn, "CQN"])
                ro = STG[hh]
                ron = "STG%d" % hh
                rope_apply(p1, p1n, p2, p2n, 128, tg, ro, ron)
                p.dma("sp", QR[hh * 128:(hh + 1) * 128, tg * 512:(tg + 1) * 512], ro[:, 0:512], "st_QR_%d" % hh, reads=[ron], adds=["QR"])
                p.dma("sp", QR[256 + hh * 128:256 + (hh + 1) * 128, tg * 512:(tg + 1) * 512], ro[:, 512:1024], "st_QR_%d" % hh,
                      reads=[ron], adds=["QR"])
        p.alias(["CQ", "CQN", "STG0", "STG1"], B_RES)

    stop = DEBUG_STOP
    cnt = [0]

    def go():
        cnt[0] += 1
        return cnt[0] <= stop

    for l in range(4):
        if l < 2:
            if go():
                fox_proj(l)
            if go():
                fox_cumsum()
            if go():
                attention(l, True)
            if go():
                wo_proj(fwo_d[l])
        else:
            if l == 2:
                if go():
                    mla_kv()
            if go():
                mla_q(l - 2, l)
            if go():
                attention(l, False)
            if go():
                wo_proj(mwo_d[l - 2])
        if go():
            ffn(l)

    for tg in range(4):
        chunks = [hT[:, kc * NT + tg * 512: kc * NT + (tg + 1) * 512] for kc in range(8)]
        rs, rsn = rms_rstd(chunks, ["hT.%d" % tg], float(D), 1.0, tg)
        for kc in range(8):
            o = T32[kc % 4]
            on = "T32_%d" % (kc % 4)
            p.op("dve", lambda e, kc=kc, o=o, c=chunks[kc]: e.scalar_tensor_tensor(
                out=o[:, :], in0=c, scalar=PAR[:, PC_FIN + kc:PC_FIN + kc + 1], in1=rs[:, :], op0=ALU.mult, op1=ALU.mult),
                reads=["hT.%d" % tg, rsn, "PAR"], writes=[on])
            p.dma("sp", outT_d[:, kc * NT + tg * 512: kc * NT + (tg + 1) * 512], o[:, :], "out%d" % (kc % 4), reads=[on], adds=["OUT"])
    p.wait_all("sp", ["OUT"])

    p.finalize()
    block = es.enter_context(nc.Block())

    @block.tensor
    def _(e):
        p.emit("pe", e)

    @block.scalar
    def _(e):
        p.emit("act", e)

    @block.vector
    def _(e):
        p.emit("dve", e)

    @block.gpsimd
    def _(e):
        p.emit("pool", e)

    @block.sync
    def _(e):
        p.emit("sp", e)

    es.close()
    return nc


def _tok_index(c):
    m = np.arange(8)[:, None]
    j = np.arange(256)[None, :]
    return (m * 2048 + c * 256 + j).reshape(-1)


def _blockpos(r, lb):
    return 16 * (lb // 2) + 2 * r + (lb % 2)


def kernel(x, positions, attn_norm, ffn_norm, w_gate, w_up, w_down, fox_w_in, fox_b_f, fox_w_o, kv_norm, w_kv_a,
           ckv_norm, w_uk, w_uv, mla_w_dq, cq_norm, mla_w_uq, mla_w_o, final_norm):
    f32 = np.float32
    x = np.asarray(x, f32)
    positions = np.asarray(positions)
    par = np.zeros((128, NPAR), f32)

    def colmajor(v):
        v = np.asarray(v, f32)
        return v.reshape(-1, 128).T

    for l in range(4):
        par[:, PC_ATTN + l * 8: PC_ATTN + (l + 1) * 8] = colmajor(attn_norm[l])
        par[:, PC_FFN + l * 8: PC_FFN + (l + 1) * 8] = colmajor(ffn_norm[l])
    par[:, PC_KV:PC_KV + 8] = colmajor(kv_norm)
    par[:, PC_FIN:PC_FIN + 8] = colmajor(final_norm)
    for j in range(2):
        par[:, PC_CQ + j * 6: PC_CQ + (j + 1) * 6] = colmajor(cq_norm[j])
    par[:, PC_CKV:PC_CKV + 2] = colmajor(ckv_norm)
    inv_freq = (10000.0 ** (-np.arange(0, 16, dtype=np.float32) * 2.0 / 32)).astype(f32)
    par[:, PC_INVF] = inv_freq[np.arange(128) % 16]
    bfb = np.zeros((128, 512), f32)
    for l in range(2):
        bfb[:, l * 256:(l + 1) * 256] = np.tile(np.asarray(fox_b_f[l], f32), 16)[None, :]
    ident = np.eye(128, dtype=f32)
    utr = (np.arange(128)[:, None] <= np.arange(128)[None, :]).astype(f32)
    bpos = np.array([_blockpos(b // 16, b % 16) for b in range(128)])
    mlt = (bpos[:, None] < bpos[None, :]).astype(f32)
    tri = np.where(np.arange(128)[:, None] > np.arange(128)[None, :], NEG, 0.0).astype(f32)
    wuq_p = []
    for j in range(2):
        w = np.asarray(mla_w_uq[j], f32).reshape(768, 16, 96)
        wuq_p.append(np.ascontiguousarray(np.concatenate(
            [w[:, :, 0:64].reshape(768, 1024), w[:, :, 64:80].reshape(768, 256), w[:, :, 80:96].reshape(768, 256)], axis=1)))
    shared = {
        "par": par, "bfb": bfb,
        "wkva": np.ascontiguousarray(w_kv_a, f32),
        "wuk": np.ascontiguousarray(np.asarray(w_uk, f32).reshape(256, 1024)),
        "wuv": np.ascontiguousarray(np.asarray(w_uv, f32).reshape(256, 1024)),
    }
    for l in range(2):
        shared["w_in%d" % l] = np.ascontiguousarray(fox_w_in[l], f32)
        shared["fwo%d" % l] = np.ascontiguousarray(fox_w_o[l], f32)
        shared["wdq%d" % l] = np.ascontiguousarray(mla_w_dq[l], f32)
        shared["wuq%d" % l] = wuq_p[l]
        shared["mwo%d" % l] = np.ascontiguousarray(mla_w_o[l], f32)
    for l in range(4):
        shared["wg%d" % l] = np.ascontiguousarray(w_gate[l], f32)
        shared["wu%d" % l] = np.ascontiguousarray(w_up[l], f32)
        shared["wd%d" % l] = np.ascontiguousarray(w_down[l], f32)
    in_maps = []
    idxs = []
    for c in range(NCORES):
        idx = _tok_index(c)
        idxs.append(idx)
        xc = x[0][idx]
        xT = np.ascontiguousarray(xc.T.reshape(8, 128, NT).transpose(1, 0, 2).reshape(128, 8 * NT))
        pos = np.ascontiguousarray(positions[0][idx].astype(np.int32).reshape(1, NT))
        msk = np.zeros((128, 2, 8, 2, 128), f32)
        for qp in range(2):
            for r in range(8):
                for kp in range(2):
                    pk, pq = 2 * r + kp, 2 * c + qp
                    if pk > pq:
                        msk[:, qp, r, kp, :] = NEG
                    elif pk == pq:
                        msk[:, qp, r, kp, :] = tri
        ownpos = np.array([_blockpos(c, lb) for lb in range(16)])
        mown = (bpos[:, None] < ownpos[None, :]).astype(f32)
        cst = np.ascontiguousarray(np.concatenate([ident, utr, mlt, mown], axis=1))
        m = dict(shared)
        m.update({"xT": xT, "pos": pos, "msk": np.ascontiguousarray(msk.reshape(128, 32 * 128)), "cst": cst})
        in_maps.append(m)
    nc = build_program()
    res = run_bass_kernel_spmd(nc, in_maps, core_ids=list(range(NCORES)))
    if DEBUG_DUMP:
        global DUMPS
        DUMPS = [{k: np.asarray(v) for k, v in r.items()} for r in res.results]
    out = np.zeros((1, S, D), f32)
    for c in range(NCORES):
        oT = np.asarray(res.results[c]["outT"]).reshape(128, 8, NT)
        out[0][idxs[c]] = oT.transpose(2, 1, 0).reshape(NT, D)
    return out
```

```python
import math
from contextlib import ExitStack
import numpy as np
import concourse.bass as bass
import concourse.mybir as mybir
from concourse.bass_utils import run_bass_kernel_spmd

F32 = mybir.dt.float32
BF16 = mybir.dt.bfloat16
I32 = mybir.dt.int32
AF = mybir.ActivationFunctionType
ALU = mybir.AluOpType

NCORES = 8
D = 1024
S = 16384
NT = 2048
DFF = 2816
NFC = 22
EPS = 1e-6
ROLL = 30000
NEG = -30000.0
DEBUG_STOP = 1000
DEBUG_DUMP = False

PC_ATTN = 0
PC_FFN = 32
PC_KV = 64
PC_FIN = 72
PC_CQ = 80
PC_CKV = 92
PC_INVF = 94
NPAR = 96

ENGS = ("pe", "act", "dve", "pool", "sp")


class Tok:
    __slots__ = ("kind", "eng", "sig", "sem", "val", "seq")

    def __init__(self, kind, eng=None, sem=None, val=None, seq=0):
        self.kind = kind
        self.eng = eng
        self.seq = seq
        self.sig = False
        self.sem = sem
        self.val = val


class Prog:
    def __init__(self, nc, es):
        self.nc = nc
        self.es = es
        self.q = {e: [] for e in ENGS}
        self.res = {}
        self.dsem = {}
        self.dcnt = {}
        self.esems = {e: [] for e in ENGS}

    def sem(self, name):
        return self.es.enter_context(self.nc.semaphore(name))

    def _r(self, n):
        r = self.res.get(n)
        if r is None:
            r = self.res[n] = {"w": [], "r": []}
        return r

    def _deps(self, e, reads, writes, adds):
        raw = []
        for n in reads:
            raw += self._r(n)["w"]
        oth = []
        for n in writes:
            r = self._r(n)
            oth += r["w"] + r["r"]
        for n in adds:
            oth += self._r(n)["r"]
        out = []
        seen = set()
        best = {}
        for (lst, is_raw) in ((raw, True), (oth, False)):
            for d in lst:
                if id(d) in seen:
                    continue
                seen.add(id(d))
                if d.kind == "eng":
                    if d.eng == e and (e == "pe" or not is_raw):
                        continue
                    b = best.get(d.eng)
                    if b is None or b.seq < d.seq:
                        best[d.eng] = d
                else:
                    out.append(d)
        for d in best.values():
            d.sig = True
            out.append(d)
        return out

    def _upd(self, tok, reads, writes, adds):
        for n in reads:
            self._r(n)["r"].append(tok)
        for n in writes:
            r = self._r(n)
            r["w"] = [tok]
            r["r"] = []
        for n in adds:
            self._r(n)["w"].append(tok)

    def op(self, e, fn, reads=(), writes=(), adds=()):
        deps = self._deps(e, reads, writes, adds)
        tok = Tok("eng", eng=e, seq=len(self.q[e]))
        self.q[e].append((fn, deps, tok, 1))
        self._upd(tok, reads, writes, adds)
        return tok

    def dma(self, e, out, in_, key, reads=(), writes=(), adds=()):
        deps = self._deps(e, reads, writes, adds)
        if key not in self.dsem:
            self.dsem[key] = self.sem("d_" + key)
            self.dcnt[key] = 0
        self.dcnt[key] += 16
        tok = Tok("dma", sem=self.dsem[key], val=self.dcnt[key])
        fn = lambda eng, o=out, i=in_: eng.dma_start(out=o, in_=i)
        self.q[e].append((fn, deps, tok, 16))
        self._upd(tok, reads, writes, adds)
        return tok

    def cc(self, ins, outs, key, reads=(), writes=()):
        writes = list(writes) + ["__CC__"]
        deps = self._deps("pool", reads, writes, ())
        if key not in self.dsem:
            self.dsem[key] = self.sem("c_" + key)
            self.dcnt[key] = 0
        self.dcnt[key] += 1
        tok = Tok("dma", sem=self.dsem[key], val=self.dcnt[key])
        fn = lambda eng, i=ins, o=outs: eng.collective_compute(
            "AllGather", ALU.bypass, replica_groups=[list(range(NCORES))], ins=[i], outs=[o])
        self.q["pool"].append((fn, deps, tok, 1))
        self._upd(tok, reads, writes, ())
        return tok

    def alias(self, src, dst):
        toks = []
        for n in src:
            r = self._r(n)
            toks += r["w"] + r["r"]
        for n in dst:
            self._r(n)["r"] += toks

    def wait_all(self, e, names):
        deps = self._deps(e, names, (), ())
        self.q[e].append((None, deps, None, 0))

    def finalize(self):
        for e in ENGS:
            n = 0
            for (fn, deps, tok, inc) in self.q[e]:
                if tok is not None and tok.kind == "eng" and tok.sig:
                    k = n // ROLL
                    while len(self.esems[e]) <= k:
                        self.esems[e].append(self.sem("e_%s%d" % (e, len(self.esems[e]))))
                    tok.sem = self.esems[e][k]
                    tok.val = n % ROLL + 1
                    n += 1

    def emit(self, e, eng):
        waited = {}
        for (fn, deps, tok, inc) in self.q[e]:
            need = {}
            for d in deps:
                k = id(d.sem)
                if waited.get(k, 0) >= d.val:
                    continue
                if k not in need or need[k][1] < d.val:
                    need[k] = (d.sem, d.val)
            for k, (s, v) in need.items():
                eng.wait_ge(s, v)
                waited[k] = v
            if fn is None:
                continue
            ins = fn(eng)
            if tok.kind == "dma":
                ins.then_inc(tok.sem, inc)
            elif tok.sig:
                ins.then_inc(tok.sem, 1)


def build_program():
    nc = bass.Bass("TRN2", target_bir_lowering=False)
    es = ExitStack()
    p = Prog(nc, es)

    def din(name, shape, dt=F32):
        return nc.dram_tensor(name, list(shape), dt, kind="ExternalInput").ap()

    def dscr(name, shape, dt):
        return nc.dram_tensor(name, list(shape), dt).ap()

    def sb(name, shape, dt):
        return es.enter_context(nc.sbuf_tensor(name, list(shape), dt))

    xT_d = din("xT", [128, 8 * NT])
    pos_d = din("pos", [1, NT], I32)
    par_d = din("par", [128, NPAR])
    bfb_d = din("bfb", [128, 2 * 256])
    msk_d = din("msk", [128, 16 * 128])
    cst_d = din("cst", [128, 3 * 128 + 16])
    w_in_d = [din("w_in%d" % l, [D, 3088]) for l in range(2)]
    fwo_d = [din("fwo%d" % l, [D, D]) for l in range(2)]
    wkva_d = din("wkva", [D, 288])
    wuk_d = din("wuk", [256, D])
    wuv_d = din("wuv", [256, D])
    wdq_d = [din("wdq%d" % j, [D, 768]) for j in range(2)]
    wuq_d = [din("wuq%d" % j, [768, 1536]) for j in range(2)]
    mwo_d = [din("mwo%d" % j, [D, D]) for j in range(2)]
    wg_d = [din("wg%d" % l, [D, DFF]) for l in range(4)]
    wu_d = [din("wu%d" % l, [D, DFF]) for l in range(4)]
    wd_d = [din("wd%d" % l, [DFF, D]) for l in range(4)]
    outT_d = nc.dram_tensor("outT", [128, 8 * NT], F32, kind="ExternalOutput").ap()

    QM = dscr("QM", [D, NT], BF16)
    QA = dscr("QA", [16 * 4, NT], BF16)
    QR = dscr("QR", [2 * 256, NT], BF16)
    XR = 2080
    XB = dscr("XB", [XR, NT], BF16)
    XBg = nc.dram_tensor("XBg", [NCORES * XR, NT], BF16, addr_space="Shared").ap()
    KTb = XB[0:1024, :]
    Vb = XB[1024:2048, :].rearrange("r (two c) -> (r two) c", two=2)
    LFb = dscr("LFb2", [NT, 16], F32)
    LFg = nc.dram_tensor("LFg2", [S, 16], F32, addr_space="Shared").ap()
    KXb = XB[2048:2080, :]

    def KTg_rows(r, h):
        return XBg[r * XR + h * 64: r * XR + (h + 1) * 64, :]

    def Vg_rows(r, h):
        v = XBg[r * XR + 1024: r * XR + 2048, :].rearrange("r (two c) -> (r two) c", two=2)
        return v[h * 128:(h + 1) * 128, :]

    def LFg_rank(r):
        return LFg[r * NT:(r + 1) * NT, :]

    def KXg_rank(r):
        return XBg[r * XR + 2048: r * XR + 2080, :]

    CKd = dscr("CKd", [16 * 4, S], BF16)
    COSd = dscr("COSd", [128, NT], F32)
    SINd = dscr("SINd", [128, NT], F32)

    hT = sb("hT", [128, 8 * NT], F32)
    A = sb("A", [128, 8 * NT], BF16)
    B = sb("B", [128, 8 * NT], BF16)
    Wt = [sb("W%d" % i, [128, 4096], BF16) for i in range(3)]
    Vt = [sb("V%d" % i, [128, 16 * 128], BF16) for i in range(2)]
    MSK = sb("MSK", [128, 16 * 128], BF16)
    CST = sb("CST", [128, 3 * 128 + 16], BF16)
    PAR = sb("PAR", [128, NPAR], F32)
    BFB = sb("BFB", [128, 512], F32)
    ONES = sb("ONES", [128, 128], BF16)
    LFown = sb("LFown", [128, 256], F32)
    T32 = [sb("T32_%d" % i, [128, 512], F32) for i in range(5)]
    TB16 = [sb("TB16_%d" % i, [128, 512], BF16) for i in range(2)]
    STG = [B[:, 12288 + i * NT: 12288 + (i + 1) * NT] for i in range(2)]
    CS = [sb("CS%d" % i, [128, 512], F32) for i in range(2)]
    PS = [es.enter_context(nc.psum_tensor("ps%d" % i, [128, 512], F32)) for i in range(8)]

    IDN = CST[:, 0:128]
    UTR = CST[:, 128:256]
    MLT = CST[:, 256:384]
    MOWN = CST[:, 384:400]

    Kt = [A[:, i * NT:(i + 1) * NT] for i in range(2)]
    Qt = [A[:, (2 + i) * NT:(3 + i) * NT] for i in range(2)]
    Pt = [A[:, 4 * NT + i * 512: 4 * NT + (i + 1) * 512] for i in range(4)]
    ATT_RES = ["K0m", "K0x", "K1m", "K1x", "Q0", "Q1", "P0", "P1", "P2", "P3"]
    SBANK = [0, 1, 2, 7]
    A_RES = ["A.%d" % t for t in range(4)]
    B_RES = ["B.%d" % t for t in range(4)]

    B32 = B[:, :].bitcast(F32)

    psi = [0]

    def next_ps():
        i = psi[0] % 8
        psi[0] += 1
        return "ps%d" % i, PS[i]

    wi = [0]

    def next_w():
        i = wi[0] % 3
        wi[0] += 1
        return "W%d" % i, Wt[i]

    def load_w(src_ap, rows_k, ncols, c0=0):
        name, t = next_w()
        view = t[:, 0:rows_k * ncols].rearrange("p (k c) -> p k c", k=rows_k)
        src = src_ap[:, c0:c0 + ncols].rearrange("(k p) c -> p k c", p=128)
        p.dma("pool", view, src, name, writes=[name])
        return name, t

    p.dma("sp", PAR[:, :], par_d, "i_par", writes=["PAR"])
    p.dma("sp", BFB[:, :], bfb_d, "i_bfb", writes=["BFB"])
    p.dma("pool", MSK[:, :], msk_d, "i_msk", writes=["MSK"])
    p.dma("pool", CST[:, :], cst_d, "i_cst", writes=["CST"])
    for kc in range(8):
        p.dma("sp", hT[:, kc * NT:(kc + 1) * NT], xT_d[:, kc * NT:(kc + 1) * NT], "i_h",
              adds=["hT.%d" % t for t in range(4)])
    p.op("pool", lambda e: e.memset(ONES[:, :], 1.0), writes=["ONES"])
    for i in range(2):
        p.op("pool", lambda e, i=i: e.memset(Vt[i][:, :], 1.0), writes=["V%d" % i])
    p.op("pool", lambda e: e.memset(A[0:16, 0:NT], 1.0), writes=["A.0", "A.1", "A.2", "A.3"])
    CKv = CKd.rearrange("(h r) s -> h r s", r=4)
    QAv = QA.rearrange("(h r) s -> h r s", r=4)
    for q8 in range(8):
        p.dma("sp", CKv[:, 0, q8 * NT:(q8 + 1) * NT], A[0:16, 0:NT], "initw", reads=["A.0", "A.1", "A.2", "A.3"], adds=["CKd"])
    for r in range(1, 4):
        p.dma("sp", QAv[:, r, :], A[0:16, 0:NT], "initw", reads=["A.0", "A.1", "A.2", "A.3"], adds=["QA"])
    p.op("pool", lambda e: e.memset(A[32:64, 0:NT], 0.0), adds=["A.0", "A.1", "A.2", "A.3"])
    p.dma("sp", KXb, A[32:64, 0:NT], "i_kx", reads=["A.0", "A.1", "A.2", "A.3"], adds=["KXb"])

    POSI = B[:, 0:2 * NT].bitcast(I32)
    ANG = B32[:, NT:2 * NT]
    ARG = B32[:, 2 * NT:3 * NT]
    TAB = B32[:, 3 * NT:4 * NT]
    p.dma("sp", POSI, pos_d.partition_broadcast(128), "i_pos", writes=["B.0"])
    p.op("dve", lambda e: e.tensor_copy(out=ANG, in_=POSI), reads=["B.0"], writes=["B.1"])
    p.op("dve", lambda e: e.tensor_scalar(out=ANG, in0=ANG, scalar1=PAR[:, PC_INVF:PC_INVF + 1], scalar2=None,
                                          op0=ALU.mult), reads=["PAR", "B.1"], writes=["B.1"])
    RR = B32[:, 0:NT]
    MAGIC = 12582912.0
    C1 = 6.28125
    C2 = 2.0 * math.pi - 6.28125
    PI_LO = 3.1415925
    for (dst, nm) in ((SINd, "SINd"), (COSd, "COSd")):
        if nm == "COSd":
            p.op("dve", lambda e: e.tensor_scalar(out=ANG, in0=ANG, scalar1=0.5 * math.pi, scalar2=None, op0=ALU.add),
                 reads=["B.1"], writes=["B.1"])
        p.op("dve", lambda e: e.tensor_scalar(out=ARG, in0=ANG, scalar1=1.0 / (2.0 * math.pi), scalar2=MAGIC,
                                              op0=ALU.mult, op1=ALU.add), reads=["B.1"], writes=["B.2"])
        p.op("dve", lambda e: e.tensor_scalar(out=ARG, in0=ARG, scalar1=-MAGIC, scalar2=None, op0=ALU.add),
             reads=["B.2"], writes=["B.2"])
        p.op("dve", lambda e: e.scalar_tensor_tensor(out=RR, in0=ARG, scalar=-C1, in1=ANG, op0=ALU.mult, op1=ALU.add),
             reads=["B.2", "B.1"], writes=["B.0"])
        p.op("dve", lambda e: e.scalar_tensor_tensor(out=RR, in0=ARG, scalar=-C2, in1=RR, op0=ALU.mult, op1=ALU.add),
             reads=["B.2", "B.0"], writes=["B.0"])
        p.op("dve", lambda e: e.tensor_scalar(out=RR, in0=RR, scalar1=-PI_LO, scalar2=PI_LO, op0=ALU.max, op1=ALU.min),
             reads=["B.0"], writes=["B.0"])
        p.op("act", lambda e: e.activation(out=TAB, in_=RR, func=AF.Sin), reads=["B.0"], writes=["B.3"])
        p.dma("sp", dst, TAB, "i_tab", reads=["B.3"], writes=[nm])

    def mm(out_ap, psname, pairs, reads, first=True, last=True):
        n = len(pairs)
        for i, (l, r) in enumerate(pairs):
            st = first and i == 0
            sp_ = last and i == n - 1
            if st:
                p.op("pe", lambda e, l=l, r=r, st=st, sp_=sp_: e.matmul(out_ap, lhsT=l, rhs=r, start=st, stop=sp_),
                     reads=reads, writes=[psname])
            else:
                p.op("pe", lambda e, l=l, r=r, st=st, sp_=sp_: e.matmul(out_ap, lhsT=l, rhs=r, start=st, stop=sp_),
                     reads=reads, adds=[psname])

    def rms_rstd(chunks, chunk_res, N, qs, tg_tag):
        psn, ps = next_ps()
        n = len(chunks)
        for i, c in enumerate(chunks):
            sq = TB16[i % 2]
            sqn = "TB16_%d" % (i % 2)
            p.op("act", lambda e, c=c, sq=sq: e.activation(out=sq[:, :], in_=c, func=AF.Square),
                 reads=chunk_res, writes=[sqn])
            st = (i == 0)
            sp_ = (i == n - 1)
            if st:
                p.op("pe", lambda e, sq=sq, st=st, sp_=sp_: e.matmul(ps[:, :], lhsT=ONES[:, :], rhs=sq[:, :], start=st, stop=sp_),
                     reads=[sqn, "ONES"], writes=[psn])
            else:
                p.op("pe", lambda e, sq=sq, st=st, sp_=sp_: e.matmul(ps[:, :], lhsT=ONES[:, :], rhs=sq[:, :], start=st, stop=sp_),
                     reads=[sqn, "ONES"], adds=[psn])
        rs = T32[4]
        p.op("act", lambda e: e.activation(out=rs[:, :], in_=ps[:, :], func=AF.Sqrt, scale=1.0 / (N * qs * qs),
                                           bias=EPS / (qs * qs)), reads=[psn], writes=["T32_4"])
        p.op("dve", lambda e: e.reciprocal(out=rs[:, :], in_=rs[:, :]), reads=["T32_4"], writes=["T32_4"])
        return rs, "T32_4"

    def norm_to_A(gcol):
        for tg in range(4):
            chunks = [hT[:, kc * NT + tg * 512: kc * NT + (tg + 1) * 512] for kc in range(8)]
            rs, rsn = rms_rstd(chunks, ["hT.%d" % tg], float(D), 1.0, tg)
            for kc in range(8):
                p.op("dve", lambda e, kc=kc, tg=tg, c=chunks[kc]: e.scalar_tensor_tensor(
                    out=A[:, kc * NT + tg * 512: kc * NT + (tg + 1) * 512], in0=c,
                    scalar=PAR[:, gcol + kc:gcol + kc + 1], in1=rs[:, :], op0=ALU.mult, op1=ALU.mult),
                    reads=["hT.%d" % tg, rsn, "PAR"], **({"writes": ["A.%d" % tg]} if kc == 0 else {"adds": ["A.%d" % tg]}))

    def xn(kc, t0, n):
        return A[:, kc * NT + t0: kc * NT + t0 + n]

    def proj_fm(w_src, ncol_total, c0, nchunks, dst_fn, scale=None, kin=8, rhs_fn=None, rhs_res=None):
        done = 0
        while done < nchunks:
            nb = min(4, nchunks - done)
            wn, wt = load_w(w_src, kin, nb * 128, c0 + done * 128)
            for j in range(nb):
                for tg in range(4):
                    psn, ps = next_ps()
                    pairs = []
                    for kc in range(kin):
                        l = wt[:, kc * nb * 128 + j * 128: kc * nb * 128 + (j + 1) * 128]
                        r = rhs_fn(kc, tg) if rhs_fn else xn(kc, tg * 512, 512)
                        pairs.append((l, r))
                    mm(ps[:, :], psn, pairs, [wn] + (rhs_res(tg) if rhs_res else ["A.%d" % tg]))
                    dst_fn(done + j, tg, ps, psn)
            done += nb

    def evac_stage_store(dst_dram_rows, scale=None):
        def fn(ci, tg, ps, psn, dst=dst_dram_rows):
            s = STG[ci % 2]
            sn = "STG%d" % (ci % 2)
            kw = {"writes": [sn]} if tg == 0 else {"adds": [sn]}
            if scale is None:
                p.op("dve", lambda e: e.tensor_copy(out=s[:, tg * 512:(tg + 1) * 512], in_=ps[:, :]), reads=[psn], **kw)
            else:
                p.op("dve", lambda e: e.tensor_scalar(out=s[:, tg * 512:(tg + 1) * 512], in0=ps[:, :], scalar1=scale,
                                                      scalar2=None, op0=ALU.mult), reads=[psn], **kw)
            if tg == 3:
                d_ap, d_res = dst(ci)
                p.dma("sp", d_ap, s[:, :], "st_%s_%d" % (d_res, ci % 2), reads=[sn], adds=[d_res])
        return fn

    def v_proj(lhs_fn, lhs_res, w_src, c0, kin):
        Vb4 = Vb.rearrange("(h p) (t d) -> p h t d", p=128, d=64)
        for half in range(2):
            wn, wt = load_w(w_src, kin, 512, c0 + half * 512)
            for tb in range(16):
                psn, ps = next_ps()
                pairs = [(lhs_fn(kc, tb), wt[:, kc * 512:(kc + 1) * 512]) for kc in range(kin)]
                mm(ps[:, :], psn, pairs, [wn] + lhs_res(tb))
                s = TB16[tb % 2]
                sn = "TB16_%d" % (tb % 2)
                p.op("dve", lambda e, s=s, ps=ps: e.tensor_copy(out=s[:, :], in_=ps[:, :]), reads=[psn], writes=[sn])
                p.dma("sp", Vb4[:, half * 8:(half + 1) * 8, tb, :], s[:, :].rearrange("p (h d) -> p h d", d=64),
                      "st_Vb_%d" % (tb % 2), reads=[sn], adds=["Vb"])

    def split3(src, srcres, hi, mid, lo, tmp, tmpres, outres, first):
        kw = (lambda: {"writes": [outres]}) if first else (lambda: {"adds": [outres]})
        p.op("dve", lambda e: e.tensor_copy(out=hi, in_=src), reads=srcres, **kw())
        p.op("dve", lambda e: e.tensor_tensor(out=tmp, in0=src, in1=hi, op=ALU.subtract), reads=srcres + [outres], writes=[tmpres])
        p.op("dve", lambda e: e.tensor_copy(out=mid, in_=tmp), reads=[tmpres], adds=[outres])
        p.op("dve", lambda e: e.tensor_tensor(out=tmp, in0=tmp, in1=mid, op=ALU.subtract), reads=[tmpres, outres], writes=[tmpres])
        p.op("dve", lambda e: e.tensor_copy(out=lo, in_=tmp), reads=[tmpres], adds=[outres])

    def rope_apply(x1ps, x1n, x2ps, x2n, np_, tg, o_tile, o_res):
        cosn, sinn = "CS0", "CS1"
        t = [T32[i][0:np_, :] for i in range(4)]
        tn = ["T32_%d" % i for i in range(4)]
        p.op("dve", lambda e: e.tensor_tensor(out=t[0], in0=x1ps[0:np_, :], in1=CS[0][0:np_, :], op=ALU.mult), reads=[x1n, cosn], writes=[tn[0]])
        p.op("dve", lambda e: e.tensor_tensor(out=t[1], in0=x2ps[0:np_, :], in1=CS[1][0:np_, :], op=ALU.mult), reads=[x2n, sinn], writes=[tn[1]])
        p.op("dve", lambda e: e.tensor_tensor(out=t[2], in0=x1ps[0:np_, :], in1=CS[1][0:np_, :], op=ALU.mult), reads=[x1n, sinn], writes=[tn[2]])
        p.op("dve", lambda e: e.tensor_tensor(out=t[3], in0=x2ps[0:np_, :], in1=CS[0][0:np_, :], op=ALU.mult), reads=[x2n, cosn], writes=[tn[3]])
        p.op("dve", lambda e: e.tensor_tensor(out=o_tile[0:np_, 0:512], in0=t[0], in1=t[1], op=ALU.subtract), reads=[tn[0], tn[1]], writes=[o_res])
        p.op("dve", lambda e: e.tensor_tensor(out=o_tile[0:np_, 512:1024], in0=t[2], in1=t[3], op=ALU.add), reads=[tn[2], tn[3]], adds=[o_res])

    def load_cs(tg):
        p.dma("sp", CS[0][:, :], COSd[:, tg * 512:(tg + 1) * 512], "CS0", reads=["COSd"], writes=["CS0"])
        p.dma("sp", CS[1][:, :], SINd[:, tg * 512:(tg + 1) * 512], "CS1", reads=["SINd"], writes=["CS1"])

    def attention(l, fox):
        kx = 4 if fox else 32
        nrow = 64 + kx
        p.alias(A_RES, ATT_RES)
        ACC = [(3 + g, "ps%d" % (3 + g)) for g in range(4)]
        tiles = []
        kvi = 0
        for h in range(16):
            for r in range(8):
                sl = kvi % 2
                kvi += 1
                for g in range(4):
                    for lb in range(4 * g + 4):
                        m = lb // 2
                        kp = lb % 2
                        if m < 2 * g:
                            c0, c1, msk = 0, 512, []
                        elif m == 2 * g:
                            if kp == 0:
                                c0, c1, msk = 0, 512, [(0, 0)]
                            else:
                                c0, c1, msk = 128, 512, [(1, 128)]
                        else:
                            if kp == 0:
                                c0, c1, msk = 256, 512, [(0, 256)]
                            else:
                                c0, c1, msk = 384, 512, [(1, 384)]
                        tiles.append(dict(h=h, r=r, sl=sl, g=g, lb=lb, kp=kp, c0=c0, c1=c1, msk=msk,
                                          first=(r == 0 and lb == 0), last=(r == 7 and lb == 4 * g + 3),
                                          newq=(r == 0 and g == 0 and lb == 0), newkv=(g == 0 and lb == 0),
                                          endhead=(r == 7 and g == 3 and lb == 15)))
        for i, t in enumerate(tiles):
            t["si"] = i % 4

        def emit_loads(t):
            h, r, sl = t["h"], t["r"], t["sl"]
            if t["newq"]:
                qn = "Q%d" % (h % 2)
                qt = Qt[h % 2]
                p.dma("sp", qt[0:64, :], QM[h * 64:(h + 1) * 64, :], qn, reads=["QM"], writes=[qn])
                if fox:
                    p.dma("sp", qt[64:68, :], QA[h * 4:(h + 1) * 4, :], qn, reads=["QA"], adds=[qn])
                else:
                    p.dma("sp", qt[64:80, :], QR[h * 16:(h + 1) * 16, :], qn, reads=["QR"], adds=[qn])
                    p.dma("sp", qt[80:96, :], QR[256 + h * 16:256 + (h + 1) * 16, :], qn, reads=["QR"], adds=[qn])
            if t["newkv"]:
                kt = Kt[sl]
                vt = Vt[sl]
                kmn, kxn, vn = "K%dm" % sl, "K%dx" % sl, "V%d" % sl
                p.dma("sp", kt[0:64, :], KTg_rows(r, h), kmn, reads=["XBg"], writes=[kmn])
                if fox:
                    p.dma("sp", kt[64:68, :], CKd[h * 4:(h + 1) * 4, r * NT:(r + 1) * NT], kxn, reads=["CKd"], writes=[kxn])
                else:
                    p.dma("sp", kt[64:96, :], KXg_rank(r), kxn, reads=["XBg"], writes=[kxn])
                p.dma("sp", vt[:, :].rearrange("p (l c) -> p l c", l=16)[:, :, 0:64],
                      Vg_rows(r, h).rearrange("p (l c) -> p l c", l=16),
                      vn, reads=["XBg"], writes=[vn])

        def emit_qk(t):
            emit_loads(t)
            h, r, sl, g, lb, kp, c0, c1, msk, si = (t[k] for k in ("h", "r", "sl", "g", "lb", "kp", "c0", "c1", "msk", "si"))
            kt = Kt[sl]
            qt = Qt[h % 2]
            kmn, kxn, qn = "K%dm" % sl, "K%dx" % sl, "Q%d" % (h % 2)
            sps = PS[SBANK[si]]
            spn = "ps%d" % SBANK[si]
            lq = kt[0:nrow, lb * 128:(lb + 1) * 128]
            rq = qt[0:nrow, g * 512 + c0: g * 512 + c1]
            nm = len(msk)
            p.op("pe", lambda e, sps=sps, lq=lq, rq=rq, c0=c0, c1=c1, nm=nm: e.matmul(
                sps[:, c0:c1], lhsT=lq, rhs=rq, start=True, stop=(nm == 0), skip_group_check=True),
                reads=[kmn, kxn, qn], writes=[spn])
            for mi, (qp, cc) in enumerate(msk):
                mo = (qp * 8 + r) * 128
                p.op("pe", lambda e, sps=sps, cc=cc, mo=mo, mi=mi, nm=nm: e.matmul(
                    sps[:, cc:cc + 128], lhsT=IDN, rhs=MSK[:, mo:mo + 128], start=False, stop=(mi == nm - 1),
                    skip_group_check=True), reads=["CST", "MSK"], adds=[spn])

        def emit_exp(t):
            si, c0, c1 = t["si"], t["c0"], t["c1"]
            sps, pt = PS[SBANK[si]], Pt[si]
            p.op("act", lambda e, pt=pt, sps=sps, c0=c0, c1=c1: e.activation(
                out=pt[:, c0:c1], in_=sps[:, c0:c1], func=AF.Exp), reads=["ps%d" % SBANK[si]], writes=["P%d" % si])

        def emit_pv(t):
            h, sl, g, lb, c0, c1, si = (t[k] for k in ("h", "sl", "g", "lb", "c0", "c1", "si"))
            vt, pt = Vt[sl], Pt[si]
            accps, accn = PS[ACC[g][0]], ACC[g][1]
            first, last = t["first"], t["last"]
            kw = {"writes": [accn]} if first else {"adds": [accn]}
            p.op("pe", lambda e, accps=accps, vt=vt, lb=lb, pt=pt, c0=c0, c1=c1, first=first, last=last: e.matmul(
                accps[:, c0:c1], lhsT=vt[:, lb * 128:(lb + 1) * 128], rhs=pt[:, c0:c1], start=first, stop=last,
                skip_group_check=True), reads=["V%d" % sl, "P%d" % si], **kw)

        def emit_norm(h, g):
            if True:
                accps = PS[ACC[g][0]]
                accn = ACC[g][1]
                rc = T32[g % 2]
                rcn = "T32_%d" % (g % 2)
                p.op("dve", lambda e, rc=rc, accps=accps: e.reciprocal(out=rc[0:64, :], in_=accps[64:128, :]),
                     reads=[accn], writes=[rcn])
                po = (h % 2) * 64
                o_ap = B[po:po + 64, (h // 2) * NT + g * 512:(h // 2) * NT + (g + 1) * 512]
                p.op("dve", lambda e, rc=rc, accps=accps, o_ap=o_ap: e.tensor_tensor(out=o_ap, in0=accps[0:64, :], in1=rc[0:64, :],
                                                                                     op=ALU.mult),
                     reads=[accn, rcn], adds=["B.%d" % g])

        n = len(tiles)
        LA = 3
        for j in range(min(LA, n)):
            emit_qk(tiles[j])
        for i, t in enumerate(tiles):
            if i + LA < n:
                emit_qk(tiles[i + LA])
            emit_exp(t)
            emit_pv(t)
            if t["last"]:
                emit_norm(t["h"], t["g"])
        p.alias(ATT_RES, A_RES)

    def wo_proj(w_src):
        for half in range(2):
            wn, wt = load_w(w_src, 8, 512, half * 512)
            for j in range(4):
                dmc = half * 4 + j
                for tg in range(4):
                    psn, ps = next_ps()
                    pairs = [(wt[:, kc * 512 + j * 128: kc * 512 + (j + 1) * 128],
                              B[:, kc * NT + tg * 512: kc * NT + (tg + 1) * 512]) for kc in range(8)]
                    mm(ps[:, :], psn, pairs, [wn, "B.%d" % tg])
                    hs = hT[:, dmc * NT + tg * 512: dmc * NT + (tg + 1) * 512]
                    p.op("dve", lambda e, hs=hs, ps=ps: e.tensor_tensor(out=hs, in0=hs, in1=ps[:, :], op=ALU.add),
                         reads=[psn], adds=["hT.%d" % tg])

    def ffn(l):
        gcol = PC_FFN + l * 8
        p.alias(A_RES, ["FX.0", "FX.1", "ACTT"])
        p.alias(B_RES, ["ACTT"])

        def aslot(fc, sub):
            if fc < 16:
                return B[:, fc * 1024 + sub * 512: fc * 1024 + (sub + 1) * 512]
            return A[:, 8192 + (fc - 16) * 1024 + sub * 512: 8192 + (fc - 16) * 1024 + (sub + 1) * 512]

        def fx(kc, sub):
            return A[:, kc * 1024 + sub * 512: kc * 1024 + (sub + 1) * 512]

        for pr in range(2):
            for sub in range(2):
                tg = pr * 2 + sub
                chunks = [hT[:, kc * NT + tg * 512: kc * NT + (tg + 1) * 512] for kc in range(8)]
                rs, rsn = rms_rstd(chunks, ["hT.%d" % tg], float(D), 1.0, tg)
                for kc in range(8):
                    p.op("dve", lambda e, kc=kc, sub=sub, c=chunks[kc], rs=rs: e.scalar_tensor_tensor(
                        out=fx(kc, sub), in0=c, scalar=PAR[:, gcol + kc:gcol + kc + 1], in1=rs[:, :], op0=ALU.mult, op1=ALU.mult),
                        reads=["hT.%d" % tg, rsn, "PAR"], **({"writes": ["FX.%d" % sub]} if kc == 0 else {"adds": ["FX.%d" % sub]}))
            for k in range(6):
                nb = 4 if k < 5 else 2
                gn, gt = load_w(wg_d[l], 8, nb * 128, k * 512)
                un, ut = load_w(wu_d[l], 8, nb * 128, k * 512)
                for j in range(nb):
                    fc = k * 4 + j
                    for sub in range(2):
                        pgn, pg = next_ps()
                        mm(pg[:, :], pgn, [(gt[:, kc * nb * 128 + j * 128: kc * nb * 128 + (j + 1) * 128], fx(kc, sub))
                                           for kc in range(8)], [gn, "FX.%d" % sub])
                        pun, pu = next_ps()
                        mm(pu[:, :], pun, [(ut[:, kc * nb * 128 + j * 128: kc * nb * 128 + (j + 1) * 128], fx(kc, sub))
                                           for kc in range(8)], [un, "FX.%d" % sub])
                        sg = T32[(fc * 2 + sub) % 2]
                        sgn = "T32_%d" % ((fc * 2 + sub) % 2)
                        p.op("act", lambda e, sg=sg, pg=pg: e.activation(out=sg[:, :], in_=pg[:, :], func=AF.Silu),
                             reads=[pgn], writes=[sgn])
                        kw = {"writes": ["ACTT"]} if (fc == 0 and sub == 0) else {"adds": ["ACTT"]}
                        p.op("dve", lambda e, sg=sg, pu=pu, fc=fc, sub=sub: e.tensor_tensor(out=aslot(fc, sub), in0=sg[:, :],
                                                                                          in1=pu[:, :], op=ALU.mult),
                             reads=[sgn, pun], **kw)
            for dmc in range(8):
                name, t = next_w()
                view = t[:, 0:NFC * 128].rearrange("p (k c) -> p k c", k=NFC)
                src = wd_d[l][:, dmc * 128:(dmc + 1) * 128].rearrange("(k p) c -> p k c", p=128)
                p.dma("pool", view, src, name, writes=[name])
                for sub in range(2):
                    tg = pr * 2 + sub
                    psn, ps = next_ps()
                    mm(ps[:, :], psn, [(t[:, fc * 128:(fc + 1) * 128], aslot(fc, sub)) for fc in range(NFC)], [name, "ACTT"])
                    hs = hT[:, dmc * NT + tg * 512: dmc * NT + (tg + 1) * 512]
                    p.op("dve", lambda e, hs=hs, ps=ps: e.tensor_tensor(out=hs, in0=hs, in1=ps[:, :], op=ALU.add),
                         reads=[psn], adds=["hT.%d" % tg])
        p.alias(["FX.0", "FX.1", "ACTT"], A_RES)
        p.alias(["ACTT"], B_RES)

    def fox_proj(l):
        norm_to_A(PC_ATTN + l * 8)
        p.alias(B_RES, ["STG0", "STG1"])
        w = w_in_d[l]
        wn, wt = load_w(w, 8, 16, 3072)
        psn, ps = next_ps()
        for tb in range(16):
            for kc in range(8):
                kw = {"writes": [psn]} if (tb == 0 and kc == 0) else {"adds": [psn]}
                p.op("pe", lambda e, tb=tb, kc=kc: e.matmul(ps[:, tb * 16:(tb + 1) * 16], lhsT=xn(kc, tb * 128, 128),
                                                            rhs=wt[:, kc * 16:(kc + 1) * 16], start=(kc == 0), stop=(kc == 7),
                                                            skip_group_check=True),
                     reads=[wn, "A.%d" % (tb // 4)], **kw)
        z = T32[3]
        p.op("dve", lambda e: e.tensor_tensor(out=z[:, 0:256], in0=ps[:, 0:256], in1=BFB[:, l * 256:(l + 1) * 256], op=ALU.add),
             reads=[psn, "BFB"], writes=["T32_3"])
        p.op("act", lambda e: e.activation(out=z[:, 0:256], in_=z[:, 0:256], func=AF.Exp, scale=-1.0),
             reads=["T32_3"], writes=["T32_3"])
        p.op("act", lambda e: e.activation(out=LFown[:, :], in_=z[:, 0:256], func=AF.Ln, bias=1.0, scale=1.0),
             reads=["T32_3"], writes=["LFown"])
        p.dma("sp", LFb.rearrange("(t p) h -> p t h", p=128), LFown[:, :].rearrange("p (t h) -> p t h", h=16),
              "st_LFb", reads=["LFown"], writes=["LFb"])
        p.cc(LFb, LFg, "LFg", reads=["LFb"], writes=["LFg"])
        proj_fm(w, 3088, 1024, 8, evac_stage_store(lambda ci: (KTb[ci * 128:(ci + 1) * 128, :], "KTb")))
        v_proj(lambda kc, tb: xn(kc, tb * 128, 128), lambda tb: ["A.%d" % (tb // 4)], w, 2048, 8)
        p.cc(XB, XBg, "XBg", reads=["KTb", "Vb", "KXb"], writes=["XBg"])
        proj_fm(w, 3088, 0, 8, evac_stage_store(lambda ci: (QM[ci * 128:(ci + 1) * 128, :], "QM"), scale=0.125))
        p.alias(["STG0", "STG1"], B_RES)

    def fox_cumsum():
        CUMN = ["CUM", "CUMh", "CUMt", "CUMo", "CUMtot", "CUMth", "CUMoff", "CUMoffo", "CUMdh"]
        p.alias(B_RES, CUMN)
        LFall = B32[:, 0:2048]
        HML = [B[:, 4096 + i * 2048: 4096 + (i + 1) * 2048] for i in range(3)]
        TMP = B32[:, 5120:6144]
        DHall = B[0:16, 12288:12288 + 1536]
        DH = [B[0:16, 12288 + i * 512: 12288 + (i + 1) * 512] for i in range(3)]
        OHML = [B[:, 14336 + i * 256: 14336 + (i + 1) * 256] for i in range(3)]
        TOTS = B32[:, 7552:7568]
        THML = [B[:, 15136 + i * 16: 15136 + (i + 1) * 16] for i in range(3)]
        OFFS = B32[0:16, 7600:7728]
        OFFO = B32[0:16, 7728:7744]
        DT = T32[0][0:16, :]
        DTMP = T32[1][0:16, :]
        for r in range(8):
            kw = {"writes": ["CUM"]} if r == 0 else {"adds": ["CUM"]}
            p.dma("sp", LFall[:, r * 256:(r + 1) * 256].rearrange("p (b h) -> p b h", h=16),
                  LFg_rank(r).rearrange("(b p) h -> p b h", p=128), "CUMld", reads=["LFg"], **kw)
        for hf in range(2):
            sl = slice(hf * 1024, (hf + 1) * 1024)
            split3(LFall[:, sl], ["CUM"], HML[0][:, sl], HML[1][:, sl], HML[2][:, sl], TMP, "CUMt", "CUMh", hf == 0)
        split3(LFown[:, :], ["LFown"], OHML[0], OHML[1], OHML[2], TMP[:, 0:256], "CUMt", "CUMo", True)
        psn, ps = next_ps()
        first = True
        for h in range(16):
            for i in range(3):
                kw = {"writes": [psn]} if first else {"adds": [psn]}
                first = False
                l_ap = HML[i].rearrange("p (b h) -> p b h", h=16)[:, :, h]
                p.op("pe", lambda e, l_ap=l_ap, h=h, i=i, ps=ps: e.matmul(ps[:, h:h + 1], lhsT=l_ap, rhs=ONES[:, 0:1], start=(i == 0),
                                                                  stop=(i == 2), skip_group_check=True),
                     reads=["CUMh", "ONES"], **kw)
        p.op("dve", lambda e, ps=ps: e.tensor_copy(out=TOTS, in_=ps[:, 0:16]), reads=[psn], writes=["CUMtot"])
        split3(TOTS, ["CUMtot"], THML[0], THML[1], THML[2], TMP[:, 0:16], "CUMt", "CUMth", True)
        psn2, ps2 = next_ps()
        mm(ps2[0:16, 0:128], psn2, [(THML[i], MLT) for i in range(3)], ["CUMth", "CST"])
        p.op("dve", lambda e: e.tensor_copy(out=OFFS, in_=ps2[0:16, 0:128]), reads=[psn2], writes=["CUMoff"])
        psn3, ps3 = next_ps()
        mm(ps3[0:16, 0:16], psn3, [(THML[i], MOWN) for i in range(3)], ["CUMth", "CST"])
        p.op("dve", lambda e: e.tensor_copy(out=OFFO, in_=ps3[0:16, 0:16]), reads=[psn3], writes=["CUMoffo"])
        for ch in range(32):
            psn, ps = next_ps()
            for j in range(4):
                b = ch * 4 + j
                for i in range(3):
                    kw = {"writes": [psn]} if (j == 0 and i == 0) else {"adds": [psn]}
                    p.op("pe", lambda e, ps=ps, j=j, b=b, i=i: e.matmul(ps[0:16, j * 128:(j + 1) * 128],
                                                                        lhsT=HML[i][:, b * 16:(b + 1) * 16], rhs=UTR,
                                                                        start=(i == 0), stop=(i == 2), skip_group_check=True),
                         reads=["CUMh", "CST"], **kw)
            for j in range(4):
                b = ch * 4 + j
                kw = {"writes": ["T32_0"]} if j == 0 else {"adds": ["T32_0"]}
                p.op("dve", lambda e, ps=ps, j=j, b=b: e.tensor_scalar(out=DT[:, j * 128:(j + 1) * 128],
                                                                      in0=ps[0:16, j * 128:(j + 1) * 128],
                                                                      scalar1=OFFS[:, b:b + 1], scalar2=None, op0=ALU.add),
                     reads=[psn, "CUMoff"], **kw)
            split3(DT, ["T32_0"], DH[0], DH[1], DH[2], DTMP, "T32_1", "CUMdh", True)
            p.dma("sp", CKv[:, 1:4, ch * 512:(ch + 1) * 512], DHall.rearrange("p (r t) -> p r t", r=3), "st_CKd",
                  reads=["CUMdh"], adds=["CKd"])
        for ch in range(4):
            psn, ps = next_ps()
            for j in range(4):
                b = ch * 4 + j
                for i in range(3):
                    kw = {"writes": [psn]} if (j == 0 and i == 0) else {"adds": [psn]}
                    p.op("pe", lambda e, ps=ps, j=j, b=b, i=i: e.matmul(ps[0:16, j * 128:(j + 1) * 128],
                                                                        lhsT=OHML[i][:, b * 16:(b + 1) * 16], rhs=UTR,
                                                                        start=(i == 0), stop=(i == 2), skip_group_check=True),
                         reads=["CUMo", "CST"], **kw)
            for j in range(4):
                b = ch * 4 + j
                kw = {"writes": ["T32_0"]} if j == 0 else {"adds": ["T32_0"]}
                p.op("dve", lambda e, ps=ps, j=j, b=b: e.tensor_scalar(out=DT[:, j * 128:(j + 1) * 128],
                                                                      in0=ps[0:16, j * 128:(j + 1) * 128],
                                                                      scalar1=OFFO[:, b:b + 1], scalar2=-1.0, op0=ALU.add,
                                                                      op1=ALU.mult),
                     reads=[psn, "CUMoffo"], **kw)
            p.op("dve", lambda e: e.tensor_copy(out=DH[0], in_=DT), reads=["T32_0"], writes=["CUMdh"])
            p.dma("sp", QAv[:, 0, ch * 512:(ch + 1) * 512], DH[0], "st_QA", reads=["CUMdh"], adds=["QA"])
        p.alias(CUMN, B_RES)

    def mla_kv():
        norm_to_A(PC_KV)
        p.alias(B_RES, ["CKV", "AT", "STG0", "STG1"] + ["CKV.%d" % t for t in range(4)])
        CKV = B[:, 0:2 * NT]
        AT = [B32[:, 4096 + i * 512: 4096 + (i + 1) * 512] for i in range(2)]
        wn, wt = load_w(wkva_d, 8, 288, 0)
        for tg in range(4):
            load_cs(tg)
            for cc in range(2):
                psn, ps = next_ps()
                mm(ps[:, :], psn, [(wt[:, kc * 288 + cc * 128: kc * 288 + (cc + 1) * 128], xn(kc, tg * 512, 512)) for kc in range(8)],
                   [wn, "A.%d" % tg])
                kw = {"writes": ["AT"]} if cc == 0 else {"adds": ["AT"]}
                p.op("dve", lambda e, cc=cc, ps=ps: e.tensor_copy(out=AT[cc], in_=ps[:, :]), reads=[psn], **kw)
            rs, rsn = rms_rstd(AT, ["AT"], 256.0, 1.0, tg)
            for cc in range(2):
                kw = {"writes": ["CKV.%d" % tg]} if cc == 0 else {"adds": ["CKV.%d" % tg]}
                p.op("dve", lambda e, cc=cc, tg=tg: e.scalar_tensor_tensor(
                    out=CKV[:, cc * NT + tg * 512: cc * NT + (tg + 1) * 512], in0=AT[cc],
                    scalar=PAR[:, PC_CKV + cc:PC_CKV + cc + 1], in1=rs[:, :], op0=ALU.mult, op1=ALU.mult),
                    reads=["AT", rsn, "PAR"], **kw)
            p1n, p1 = next_ps()
            mm(p1[0:16, :], p1n, [(wt[:, kc * 288 + 256: kc * 288 + 272], xn(kc, tg * 512, 512)) for kc in range(8)], [wn, "A.%d" % tg])
            p2n, p2 = next_ps()
            mm(p2[0:16, :], p2n, [(wt[:, kc * 288 + 272: kc * 288 + 288], xn(kc, tg * 512, 512)) for kc in range(8)], [wn, "A.%d" % tg])
            ro = STG[tg % 2]
            ron = "STG%d" % (tg % 2)
            rope_apply(p1, p1n, p2, p2n, 16, tg, ro, ron)
            p.dma("sp", KXb[0:16, tg * 512:(tg + 1) * 512], ro[0:16, 0:512], "st_KXb_%d" % (tg % 2), reads=[ron], adds=["KXb"])
            p.dma("sp", KXb[16:32, tg * 512:(tg + 1) * 512], ro[0:16, 512:1024], "st_KXb_%d" % (tg % 2), reads=[ron], adds=["KXb"])
        ckv_res = lambda tg: ["CKV.%d" % tg]
        proj_fm(wuk_d, 1024, 0, 8, evac_stage_store(lambda ci: (KTb[ci * 128:(ci + 1) * 128, :], "KTb")), kin=2,
                rhs_fn=lambda kc, tg: CKV[:, kc * NT + tg * 512: kc * NT + (tg + 1) * 512], rhs_res=ckv_res)
        v_proj(lambda kc, tb: CKV[:, kc * NT + tb * 128: kc * NT + (tb + 1) * 128], lambda tb: ["CKV.%d" % (tb // 4)], wuv_d, 0, 2)
        p.cc(XB, XBg, "XBg", reads=["KTb", "Vb", "KXb"], writes=["XBg"])
        p.alias(["CKV", "AT", "STG0", "STG1"] + ["CKV.%d" % t for t in range(4)], B_RES)

    def mla_q(j, l):
        norm_to_A(PC_ATTN + l * 8)
        p.alias(B_RES, ["CQ", "CQN", "STG0", "STG1"])
        CQ = [B32[:, i * 512:(i + 1) * 512] for i in range(6)]
        CQN = B[:, 8192:8192 + 6 * 512]
        qs = 1.0 / math.sqrt(96.0)
        for tg in range(4):
            load_cs(tg)
            for half in range(2):
                nb = 4 if half == 0 else 2
                wn, wt = load_w(wdq_d[j], 8, nb * 128, half * 512)
                for jj in range(nb):
                    qc = half * 4 + jj
                    psn, ps = next_ps()
                    mm(ps[:, :], psn, [(wt[:, kc * nb * 128 + jj * 128: kc * nb * 128 + (jj + 1) * 128], xn(kc, tg * 512, 512))
                                       for kc in range(8)], [wn, "A.%d" % tg])
                    kw = {"writes": ["CQ"]} if qc == 0 else {"adds": ["CQ"]}
                    p.op("dve", lambda e, qc=qc, ps=ps: e.tensor_copy(out=CQ[qc], in_=ps[:, :]), reads=[psn], **kw)
            rs, rsn = rms_rstd(CQ, ["CQ"], 768.0, qs, tg)
            for qc in range(6):
                kw = {"writes": ["CQN"]} if qc == 0 else {"adds": ["CQN"]}
                p.op("dve", lambda e, qc=qc: e.scalar_tensor_tensor(
                    out=CQN[:, qc * 512:(qc + 1) * 512], in0=CQ[qc], scalar=PAR[:, PC_CQ + j * 6 + qc:PC_CQ + j * 6 + qc + 1],
                    in1=rs[:, :], op0=ALU.mult, op1=ALU.mult), reads=["CQ", rsn, "PAR"], **kw)
            for half in range(2):
                wn, wt = load_w(wuq_d[j], 6, 512, half * 512)
                for jj in range(4):
                    hp = half * 4 + jj
                    psn, ps = next_ps()
                    mm(ps[:, :], psn, [(wt[:, kc * 512 + jj * 128: kc * 512 + (jj + 1) * 128], CQN[:, kc * 512:(kc + 1) * 512])
                                       for kc in range(6)], [wn, "CQN"])
                    s = TB16[hp % 2]
                    sn = "TB16_%d" % (hp % 2)
                    p.op("dve", lambda e, s=s, ps=ps: e.tensor_copy(out=s[:, :], in_=ps[:, :]), reads=[psn], writes=[sn])
                    p.dma("sp", QM[hp * 128:(hp + 1) * 128, tg * 512:(tg + 1) * 512], s[:, :], "st_QMb_%d" % (hp % 2), reads=[sn], adds=["QM"])
            wn, wt = load_w(wuq_d[j], 6, 512, 1024)
            for hh in range(2):
                p1n, p1 = next_ps()
                mm(p1[:, :], p1n, [(wt[:, kc * 512 + hh * 128: kc * 512 + (hh + 1) * 128], CQN[:, kc * 512:(kc + 1) * 512])
                                   for kc in range(6)], [wn, "CQN"])
                p2n, p2 = next_ps()
                mm(p2[:, :], p2n, [(wt[:, kc * 512 + 256 + hh * 128: kc * 512 + 256 + (hh + 1) * 128], CQN[:, kc * 512:(kc + 1) * 512])
                                   for kc in range(6)], [wn, "CQN"])
                ro = STG[hh]
                ron = "STG%d" % hh
                rope_apply(p1, p1n, p2, p2n, 128, tg, ro, ron)
                p.dma("sp", QR[hh * 128:(hh + 1) * 128, tg * 512:(tg + 1) * 512], ro[:, 0:512], "st_QR_%d" % hh, reads=[ron], adds=["QR"])
                p.dma("sp", QR[256 + hh * 128:256 + (hh + 1) * 128, tg * 512:(tg + 1) * 512], ro[:, 512:1024], "st_QR_%d" % hh,
                      reads=[ron], adds=["QR"])
        p.alias(["CQ", "CQN", "STG0", "STG1"], B_RES)

    stop = DEBUG_STOP
    cnt = [0]

    def go():
        cnt[0] += 1
        return cnt[0] <= stop

    for l in range(4):
        if l < 2:
            if go():
                fox_proj(l)
            if go():
                fox_cumsum()
            if go():
                attention(l, True)
            if go():
                wo_proj(fwo_d[l])
        else:
            if l == 2:
                if go():
                    mla_kv()
            if go():
                mla_q(l - 2, l)
            if go():
                attention(l, False)
            if go():
                wo_proj(mwo_d[l - 2])
        if go():
            ffn(l)

    for tg in range(4):
        chunks = [hT[:, kc * NT + tg * 512: kc * NT + (tg + 1) * 512] for kc in range(8)]
        rs, rsn = rms_rstd(chunks, ["hT.%d" % tg], float(D), 1.0, tg)
        for kc in range(8):
            o = T32[kc % 4]
            on = "T32_%d" % (kc % 4)
            p.op("dve", lambda e, kc=kc, o=o, c=chunks[kc]: e.scalar_tensor_tensor(
                out=o[:, :], in0=c, scalar=PAR[:, PC_FIN + kc:PC_FIN + kc + 1], in1=rs[:, :], op0=ALU.mult, op1=ALU.mult),
                reads=["hT.%d" % tg, rsn, "PAR"], writes=[on])
            p.dma("sp", outT_d[:, kc * NT + tg * 512: kc * NT + (tg + 1) * 512], o[:, :], "out%d" % (kc % 4), reads=[on], adds=["OUT"])
    p.wait_all("sp", ["OUT"])

    p.finalize()
    block = es.enter_context(nc.Block())

    @block.tensor
    def _(e):
        p.emit("pe", e)

    @block.scalar
    def _(e):
        p.emit("act", e)

    @block.vector
    def _(e):
        p.emit("dve", e)

    @block.gpsimd
    def _(e):
        p.emit("pool", e)

    @block.sync
    def _(e):
        p.emit("sp", e)

    es.close()
    return nc


def _tok_index(c):
    out = []
    for m in range(8):
        for pos in (c, 15 - c):
            out.append((16 * m + pos) * 128 + np.arange(128))
    return np.concatenate(out)


def _blockpos(r, lb):
    return 16 * (lb // 2) + (r if lb % 2 == 0 else 15 - r)


def kernel(x, positions, attn_norm, ffn_norm, w_gate, w_up, w_down, fox_w_in, fox_b_f, fox_w_o, kv_norm, w_kv_a,
           ckv_norm, w_uk, w_uv, mla_w_dq, cq_norm, mla_w_uq, mla_w_o, final_norm):
    f32 = np.float32
    x = np.asarray(x, f32)
    positions = np.asarray(positions)
    par = np.zeros((128, NPAR), f32)

    def colmajor(v):
        v = np.asarray(v, f32)
        return v.reshape(-1, 128).T

    for l in range(4):
        par[:, PC_ATTN + l * 8: PC_ATTN + (l + 1) * 8] = colmajor(attn_norm[l])
        par[:, PC_FFN + l * 8: PC_FFN + (l + 1) * 8] = colmajor(ffn_norm[l])
    par[:, PC_KV:PC_KV + 8] = colmajor(kv_norm)
    par[:, PC_FIN:PC_FIN + 8] = colmajor(final_norm)
    for j in range(2):
        par[:, PC_CQ + j * 6: PC_CQ + (j + 1) * 6] = colmajor(cq_norm[j])
    par[:, PC_CKV:PC_CKV + 2] = colmajor(ckv_norm)
    inv_freq = (10000.0 ** (-np.arange(0, 16, dtype=np.float32) * 2.0 / 32)).astype(f32)
    par[:, PC_INVF] = inv_freq[np.arange(128) % 16]
    bfb = np.zeros((128, 512), f32)
    for l in range(2):
        bfb[:, l * 256:(l + 1) * 256] = np.tile(np.asarray(fox_b_f[l], f32), 16)[None, :]
    ident = np.eye(128, dtype=f32)
    utr = (np.arange(128)[:, None] <= np.arange(128)[None, :]).astype(f32)
    bpos = np.array([_blockpos(b // 16, b % 16) for b in range(128)])
    mlt = (bpos[:, None] < bpos[None, :]).astype(f32)
    tri = np.where(np.arange(128)[:, None] > np.arange(128)[None, :], NEG, 0.0).astype(f32)
    wuq_p = []
    for j in range(2):
        w = np.asarray(mla_w_uq[j], f32).reshape(768, 16, 96)
        wuq_p.append(np.ascontiguousarray(np.concatenate(
            [w[:, :, 0:64].reshape(768, 1024), w[:, :, 64:80].reshape(768, 256), w[:, :, 80:96].reshape(768, 256)], axis=1)))
    shared = {
        "par": par, "bfb": bfb,
        "wkva": np.ascontiguousarray(w_kv_a, f32),
        "wuk": np.ascontiguousarray(np.asarray(w_uk, f32).reshape(256, 1024)),
        "wuv": np.ascontiguousarray(np.asarray(w_uv, f32).reshape(256, 1024)),
    }
    for l in range(2):
        shared["w_in%d" % l] = np.ascontiguousarray(fox_w_in[l], f32)
        shared["fwo%d" % l] = np.ascontiguousarray(fox_w_o[l], f32)
        shared["wdq%d" % l] = np.ascontiguousarray(mla_w_dq[l], f32)
        shared["wuq%d" % l] = wuq_p[l]
        shared["mwo%d" % l] = np.ascontiguousarray(mla_w_o[l], f32)
    for l in range(4):
        shared["wg%d" % l] = np.ascontiguousarray(w_gate[l], f32)
        shared["wu%d" % l] = np.ascontiguousarray(w_up[l], f32)
        shared["wd%d" % l] = np.ascontiguousarray(w_down[l], f32)
    in_maps = []
    idxs = []
    for c in range(NCORES):
        idx = _tok_index(c)
        idxs.append(idx)
        xc = x[0][idx]
        xT = np.ascontiguousarray(xc.T.reshape(8, 128, NT).transpose(1, 0, 2).reshape(128, 8 * NT))
        pos = np.ascontiguousarray(positions[0][idx].astype(np.int32).reshape(1, NT))
        msk = np.zeros((128, 2, 8, 128), f32)
        for r in range(8):
            if r > c:
                msk[:, 0, r, :] = NEG
            elif r == c:
                msk[:, 0, r, :] = tri
            if r < c:
                msk[:, 1, r, :] = NEG
            elif r == c:
                msk[:, 1, r, :] = tri
        ownpos = np.array([_blockpos(c, lb) for lb in range(16)])
        mown = (bpos[:, None] < ownpos[None, :]).astype(f32)
        cst = np.ascontiguousarray(np.concatenate([ident, utr, mlt, mown], axis=1))
        m = dict(shared)
        m.update({"xT": xT, "pos": pos, "msk": np.ascontiguousarray(msk.reshape(128, 16 * 128)), "cst": cst})
        in_maps.append(m)
    nc = build_program()
    res = run_bass_kernel_spmd(nc, in_maps, core_ids=list(range(NCORES)))
    if DEBUG_DUMP:
        global DUMPS
        DUMPS = [{k: np.asarray(v) for k, v in r.items()} for r in res.results]
    out = np.zeros((1, S, D), f32)
    for c in range(NCORES):
        oT = np.asarray(res.results[c]["outT"]).reshape(128, 8, NT)
        out[0][idxs[c]] = oT.transpose(2, 1, 0).reshape(NT, D)
    return out
```
